# Optimizing a Trainium2 kernel written in Bass

```python
import math
import jax, jax.numpy as jnp
from jax import lax
import numpy as np

D_MODEL = 1024
BATCH = 16
SEQ = 4096
DEPTH = 4
DEC_BATCH = 8
DEC_SEQ = 2048
PAST_LEN = 128

N_MIXERS = 2
N_A_LAYERS = (DEPTH + N_MIXERS - 1) // N_MIXERS
N_B_LAYERS = DEPTH // N_MIXERS

A_HEADS = 8
A_QK_DIM = 64
A_V_DIM = 2 * A_QK_DIM
A_WIDTH = A_HEADS * A_V_DIM
A_QK_WIDTH = A_HEADS * 2 * A_QK_DIM
A_IN = 2 * A_QK_WIDTH + A_WIDTH + A_WIDTH
A_ROT_DIM = A_QK_DIM // 4
ROPE_THETA = 500000.0

B_HEADS = 8
B_KV_HEADS = 2
B_GROUP = B_HEADS // B_KV_HEADS
B_HEAD_DIM = 128
B_WIDTH = B_HEADS * B_HEAD_DIM
B_KV_WIDTH = B_KV_HEADS * B_HEAD_DIM
B_IN = B_WIDTH + 2 * B_KV_WIDTH + B_WIDTH
B_AXIS_DIM = B_HEAD_DIM // 2
AXIAL_THETA = 10000.0

GRID_W = 64
Q_BLOCK = 128
NORM_EPS = 1e-6

kernel_name = "hybrid_diffattn_axialgqa_encoder"


def _rms_norm(x, w):
    x32 = x.astype(jnp.float32)
    y = x32 * lax.rsqrt(jnp.mean(x32 * x32, axis=-1, keepdims=True) + NORM_EPS)
    return (y * w.astype(jnp.float32)).astype(x.dtype)


def _rope_angles(pos, dim, theta):
    inv_freq = theta ** (-(jnp.arange(0, dim, 2, dtype=jnp.float32) / dim))
    ang = pos[:, None] * inv_freq[None, :]
    return jnp.cos(ang), jnp.sin(ang)


def _rotate_half(x, cos, sin):
    x32 = x.astype(jnp.float32)
    half = x32.shape[-1] // 2
    x1, x2 = x32[..., :half], x32[..., half:]
    out = jnp.concatenate([x1 * cos - x2 * sin, x2 * cos + x1 * sin], axis=-1)
    return out.astype(x.dtype)


def _sweep_query_blocks(q, block_fn):
    B, S = q.shape[0], q.shape[1]
    nb = S // Q_BLOCK
    qb = jnp.moveaxis(q.reshape((B, nb, Q_BLOCK) + q.shape[2:]), 1, 0)
    out = lax.map(block_fn, qb)
    out = jnp.moveaxis(out, 0, 1)
    return out.reshape((B, S) + out.shape[3:])


def _diff_attention_mixer(h, w_in, w_out, lam, subln, layer_idx):
    B, S, _ = h.shape
    proj = h @ w_in
    q, k, v, gate = jnp.split(proj, [A_QK_WIDTH, 2 * A_QK_WIDTH, 2 * A_QK_WIDTH + A_WIDTH], axis=-1)
    q = q.reshape(B, S, A_HEADS, 2, A_QK_DIM)
    k = k.reshape(B, S, A_HEADS, 2, A_QK_DIM)
    v = v.reshape(B, S, A_HEADS, A_V_DIM)
    cos, sin = _rope_angles(jnp.arange(S, dtype=jnp.float32), A_ROT_DIM, ROPE_THETA)
    cos, sin = cos[None, :, None, None, :], sin[None, :, None, None, :]
    q = jnp.concatenate([_rotate_half(q[..., :A_ROT_DIM], cos, sin), q[..., A_ROT_DIM:]], axis=-1)
    k = jnp.concatenate([_rotate_half(k[..., :A_ROT_DIM], cos, sin), k[..., A_ROT_DIM:]], axis=-1)
    lam_init = 0.8 - 0.6 * math.exp(-0.3 * layer_idx)
    lam32 = lam.astype(jnp.float32)
    lam_full = (jnp.exp(jnp.sum(lam32[0] * lam32[1])) - jnp.exp(jnp.sum(lam32[2] * lam32[3]))
                + lam_init)
    scale = A_QK_DIM ** -0.5

    def block(qb):
        s = jnp.einsum('bqhmd,bkhmd->bhmqk', qb, k, preferred_element_type=jnp.float32) * scale
        p = jax.nn.softmax(s, axis=-1)
        p_diff = p[:, :, 0] - lam_full * p[:, :, 1]
        o = jnp.einsum('bhqk,bkhe->bqhe', p_diff.astype(v.dtype), v,
                       preferred_element_type=jnp.float32)
        return o.astype(h.dtype)

    o = _sweep_query_blocks(q, block)
    o = _rms_norm(o, subln) * (1.0 - lam_init)
    o = o.reshape(B, S, A_WIDTH) * jax.nn.silu(gate)
    return o @ w_out


def _axial_gqa_mixer(h, w_in, w_out, q_norm, k_norm):
    B, S, _ = h.shape
    proj = h @ w_in
    q, k, v, gate = jnp.split(proj, [B_WIDTH, B_WIDTH + B_KV_WIDTH, B_WIDTH + 2 * B_KV_WIDTH], axis=-1)
    q = _rms_norm(q.reshape(B, S, B_HEADS, B_HEAD_DIM), q_norm)
    k = _rms_norm(k.reshape(B, S, B_KV_HEADS, B_HEAD_DIM), k_norm)
    v = v.reshape(B, S, B_KV_HEADS, B_HEAD_DIM)
    rows = S // GRID_W
    row_ids = jnp.repeat(jnp.arange(rows, dtype=jnp.float32), GRID_W)
    col_ids = jnp.tile(jnp.arange(GRID_W, dtype=jnp.float32), rows)
    cr, sr = _rope_angles(row_ids, B_AXIS_DIM, AXIAL_THETA)
    cc, sc = _rope_angles(col_ids, B_AXIS_DIM, AXIAL_THETA)
    cr, sr, cc, sc = (a[None, :, None, :] for a in (cr, sr, cc, sc))

    def axial(t):
        return jnp.concatenate([_rotate_half(t[..., :B_AXIS_DIM], cr, sr),
                                _rotate_half(t[..., B_AXIS_DIM:], cc, sc)], axis=-1)

    q = axial(q).reshape(B, S, B_KV_HEADS, B_GROUP, B_HEAD_DIM)
    k = axial(k)
    scale = B_HEAD_DIM ** -0.5

    def block(qb):
        s = jnp.einsum('bqkgd,bskd->bkgqs', qb, k, preferred_element_type=jnp.float32) * scale
        p = jax.nn.softmax(s, axis=-1)
        o = jnp.einsum('bkgqs,bskd->bqkgd', p.astype(v.dtype), v,
                       preferred_element_type=jnp.float32)
        return o.astype(h.dtype)

    o = _sweep_query_blocks(q, block)
    o = o.reshape(B, S, B_WIDTH) * jax.nn.silu(gate)
    return o @ w_out


def _trunk(x, norm_pre, norm_post, a_w_in, a_w_out, a_lam, a_subln,
           b_w_in, b_w_out, b_q_norm, b_k_norm):
    for i in range(DEPTH):
        h = _rms_norm(x, norm_pre[i])
        j = i // N_MIXERS
        if i % N_MIXERS == 0:
            m = _diff_attention_mixer(h, a_w_in[j], a_w_out[j], a_lam[j], a_subln[j], i)
        else:
            m = _axial_gqa_mixer(h, b_w_in[j], b_w_out[j], b_q_norm[j], b_k_norm[j])
        x = x + _rms_norm(m, norm_post[i])
    return x


def setup_inputs(seed: int = 0) -> dict:
    key = jax.random.key(seed)
    ks = jax.random.split(key, 12)
    f32 = jnp.float32
    return {
        "x_prompt": jax.random.normal(ks[0], (BATCH, SEQ, D_MODEL), f32),
        "x_sample": jax.random.normal(ks[1], (DEC_BATCH, DEC_SEQ, D_MODEL), f32),
        "norm_pre": 1.0 + 0.02 * jax.random.normal(ks[2], (DEPTH, D_MODEL), f32),
        "norm_post": 1.0 + 0.02 * jax.random.normal(ks[3], (DEPTH, D_MODEL), f32),
        "a_w_in": jax.random.normal(ks[4], (N_A_LAYERS, D_MODEL, A_IN), f32) * D_MODEL ** -0.5,
        "a_w_out": jax.random.normal(ks[5], (N_A_LAYERS, A_WIDTH, D_MODEL), f32) * A_WIDTH ** -0.5,
        "a_lam": 0.1 * jax.random.normal(ks[6], (N_A_LAYERS, 4, A_QK_DIM), f32),
        "a_subln": 1.0 + 0.02 * jax.random.normal(ks[7], (N_A_LAYERS, A_V_DIM), f32),
        "b_w_in": jax.random.normal(ks[8], (N_B_LAYERS, D_MODEL, B_IN), f32) * D_MODEL ** -0.5,
        "b_w_out": jax.random.normal(ks[9], (N_B_LAYERS, B_WIDTH, D_MODEL), f32) * B_WIDTH ** -0.5,
        "b_q_norm": 1.0 + 0.02 * jax.random.normal(ks[10], (N_B_LAYERS, B_HEAD_DIM), f32),
        "b_k_norm": 1.0 + 0.02 * jax.random.normal(ks[11], (N_B_LAYERS, B_HEAD_DIM), f32),
    }


def reference(x_prompt, x_sample, norm_pre, norm_post, a_w_in, a_w_out, a_lam, a_subln,
              b_w_in, b_w_out, b_q_norm, b_k_norm):
    y_prompt = _trunk(x_prompt, norm_pre, norm_post, a_w_in, a_w_out, a_lam, a_subln,
                      b_w_in, b_w_out, b_q_norm, b_k_norm)
    y_sample = _trunk(x_sample, norm_pre, norm_post, a_w_in, a_w_out, a_lam, a_subln,
                      b_w_in, b_w_out, b_q_norm, b_k_norm)
    return (y_prompt, y_sample)
```

```python
import math
from contextlib import ExitStack

import numpy as np
import ml_dtypes

import concourse.bass as bass
import concourse.mybir as mybir
from concourse.bass_utils import run_bass_kernel_spmd

F32 = mybir.dt.float32
BF16 = mybir.dt.bfloat16
AF = mybir.ActivationFunctionType
ALU = mybir.AluOpType
AX = mybir.AxisListType

D = 1024
DEPTH = 4
S_P, S_S = 4096, 2048
NCORES = 8
EPS = 1e-6
A_IN, B_IN = 4096, 2560
NTMAX = S_P // 128


class Tracker:
    COMPUTE = ("pe", "act", "dve", "pool")

    def __init__(self, nc, es, n_dma_sems=12):
        self.nc = nc
        self.eng = {"pe": nc.tensor, "act": nc.scalar, "dve": nc.vector, "pool": nc.gpsimd, "sp": nc.sync}
        self.sem = {}
        self.cnt = {}
        for e in self.COMPUTE:
            self.sem[e] = es.enter_context(nc.semaphore("s_" + e))
            self.cnt[e] = 0
        self.rings = {}
        for q in ("sp", "pool"):
            names = []
            for i in range(n_dma_sems):
                k = "d_%s%d" % (q, i)
                self.sem[k] = es.enter_context(nc.semaphore(k))
                self.cnt[k] = 0
                names.append(k)
            self.rings[q] = [names, 0]
        self.seen = {e: {} for e in self.eng}
        self.last_w = {}
        self.readers = {}

    def _wait(self, e, ticket):
        if ticket is None:
            return
        k, v = ticket
        if e == "pe" and k == "pe":
            return
        if self.seen[e].get(k, 0) >= v:
            return
        self.eng[e].wait_ge(self.sem[k], v)
        self.seen[e][k] = v

    def _deps(self, e, reads, writes):
        for r in reads:
            self._wait(e, self.last_w.get(r))
        for w in writes:
            self._wait(e, self.last_w.get(w))
            for t in self.readers.get(w, ()):
                self._wait(e, t)

    def _record(self, ticket, reads, writes):
        for w in writes:
            self.last_w[w] = ticket
            self.readers[w] = []
        for r in reads:
            self.readers.setdefault(r, []).append(ticket)

    def op(self, e, fn, reads=(), writes=(), tag=None):
        self._deps(e, reads, writes)
        ins = fn(self.eng[e])
        self.cnt[e] += 1
        ins.then_inc(self.sem[e], 1)
        t = (e, self.cnt[e])
        self._record(t, reads, writes)
        return t

    def dma(self, q, out, in_, reads=(), writes=(), tag=None):
        names, idx = self.rings[q]
        k = names[idx % len(names)]
        self.rings[q][1] = idx + 1
        if self.cnt[k] > 0:
            self._wait(q, (k, self.cnt[k]))
        self._deps(q, reads, writes)
        self.eng[q].dma_start(out=out, in_=in_).then_inc(self.sem[k], 16)
        self.cnt[k] += 16
        t = (k, self.cnt[k])
        self._record(t, reads, writes)
        return t

    def barrier(self):
        for e in self.eng:
            for k, v in self.cnt.items():
                if v > 0:
                    self._wait(e, (k, v))
        self.last_w.clear()
        self.readers.clear()

    def final_wait(self, e="sp"):
        for k, v in self.cnt.items():
            if v > 0:
                self._wait(e, (k, v))


def build_nc(layers=(0, 1, 2, 3), seq_ids=(0, 1, 2), phases="WPTO"):
    nc = bass.Bass("TRN2", target_bir_lowering=False)
    dt_in = lambda n, s, d=F32: nc.dram_tensor(n, s, d, kind="ExternalInput").ap()
    xp = dt_in("xp", [2, S_P, D])
    xs = dt_in("xs", [1, S_S, D])
    norm_pre = dt_in("norm_pre", [DEPTH, D])
    norm_post = dt_in("norm_post", [DEPTH, D])
    a_w_in = dt_in("a_w_in", [2, D, A_IN])
    a_w_out = dt_in("a_w_out", [2, D, D])
    a_lam = dt_in("a_lam", [2, 4, 64])
    a_subln = dt_in("a_subln", [2, 128])
    b_w_in = dt_in("b_w_in", [2, D, B_IN])
    b_w_out = dt_in("b_w_out", [2, D, D])
    b_q_norm = dt_in("b_q_norm", [2, 128])
    b_k_norm = dt_in("b_k_norm", [2, 128])
    tab_a = dt_in("tab_a", [S_P, 128])
    tab_b = dt_in("tab_b", [S_P, 512])
    ident_d = dt_in("ident", [128, 128], BF16)
    yp = nc.dram_tensor("yp", [2, S_P, D], F32, kind="ExternalOutput").ap()
    ys = nc.dram_tensor("ys", [1, S_S, D], F32, kind="ExternalOutput").ap()
    QT_all = nc.dram_tensor("QT_d", [3, 1024, S_P], BF16, kind="Internal").ap()
    KT_all = nc.dram_tensor("KT_d", [3, 1024, S_P], BF16, kind="Internal").ap()
    VX_all = nc.dram_tensor("VX_d", [3, 8, 128, NTMAX, 129], BF16, kind="Internal").ap()
    G_all = nc.dram_tensor("G_d", [3, 8, 128, NTMAX, 128], F32, kind="Internal").ap()

    seqs = [(xp[0], yp[0], S_P), (xp[1], yp[1], S_P), (xs[0], ys[0], S_S)]

    with ExitStack() as es:
        uniq = [0]

        def sb(n, s, d, st=es):
            uniq[0] += 1
            return st.enter_context(nc.sbuf_tensor("%s_%d" % (n, uniq[0]), s, d))
        T = Tracker(nc, es)

        BIG = sb("BIG", [128, 32768], BF16)
        WOUT = sb("WOUT", [128, 8, 1024], BF16)
        IDENT = sb("IDENT", [128, 128], BF16)
        NPRE = sb("NPRE", [128, 1024], F32)
        NPOST = sb("NPOST", [128, 1024], F32)
        SMW = sb("SMW", [128, 3, 128], F32)
        LAMB = sb("LAMB", [128, 4, 64], F32)
        LAMP = sb("LAMP", [128, 2, 64], F32)
        LAMS = sb("LAMS", [128, 8], F32)
        NEGH = sb("NEGH", [128, 4], F32)
        JK = sb("JK", [128, 1024], BF16)
        JKF = sb("JKF", [128, 128], F32)
        TPS = es.enter_context(nc.psum_tensor("TPS", [128, 1024], BF16))
        PJ = es.enter_context(nc.psum_tensor("PJ", [128, 3, 512], F32))
        ACC = es.enter_context(nc.psum_tensor("ACC", [128, 4, 512], F32))

        WINv = BIG[:].rearrange("p (k n) -> p k n", k=8)
        OGv = BIG[:].rearrange("p (t w) -> p t w", w=1024)

        T.dma("sp", IDENT[:], ident_d, writes=[("IDENT",)])
        T.op("dve", lambda e: e.memset(NEGH[:], -0.5), writes=[("NEGH",)])

        cnt = {"pj": 0, "pt": 0, "acc": 0}
        scope = {"es": None, "cache": {}}

        def lb(name, shape, dtype):
            if name not in scope["cache"]:
                scope["cache"][name] = sb(name, shape, dtype, scope["es"])
            return scope["cache"][name]

        def first(flag):
            if flag in scope["cache"]:
                return False
            scope["cache"][flag] = True
            return True

        def pj_slot():
            s = cnt["pj"] % 3
            cnt["pj"] += 1
            return s

        def load_weight(dst3, src, ncols, chunk, STG, key):
            i = 0
            for kc in range(8):
                for c0 in range(0, ncols, chunk):
                    w = min(chunk, ncols - c0)
                    sl = i % 2
                    T.dma("sp", STG[:, sl, 0:w], src[kc * 128:(kc + 1) * 128, c0:c0 + w], writes=[("STG", sl)])
                    eng = "dve" if i % 2 == 0 else "pool"
                    T.op(eng, lambda e, kc=kc, c0=c0, w=w, sl=sl: e.tensor_copy(out=dst3[:, kc, c0:c0 + w], in_=STG[:, sl, 0:w]),
                         reads=[("STG", sl)], writes=[(key, kc, c0)])
                    i += 1

        def layer_setup(layer):
            typ = "A" if layer % 2 == 0 else "B"
            j = layer // 2
            with ExitStack() as ls:
                STG = sb("STGo", [128, 2, 1024], F32, ls)
                load_weight(WOUT, (a_w_out if typ == "A" else b_w_out)[j], 1024, 1024, STG, "WOUT")
                T.dma("sp", NPRE[:], norm_pre[layer:layer + 1, :].partition_broadcast(128), writes=[("NPRE",)])
                T.dma("sp", NPOST[:], norm_post[layer:layer + 1, :].partition_broadcast(128), writes=[("NPOST",)])
                if typ == "A":
                    lam_init = 0.8 - 0.6 * math.exp(-0.3 * layer)
                    T.dma("sp", SMW[:, 2, :], a_subln[j:j + 1, :].partition_broadcast(128), writes=[("SMW", 2)])
                    T.op("dve", lambda e: e.tensor_scalar(out=SMW[:, 0, :], in0=SMW[:, 2, :], scalar1=float(1.0 - lam_init),
                                                          scalar2=None, op0=ALU.mult), reads=[("SMW", 2)], writes=[("SMW", 0)])
                    T.dma("sp", LAMB[:].rearrange("p a b -> p (a b)"),
                          a_lam[j:j + 1].rearrange("o a b -> o (a b)").partition_broadcast(128), writes=[("LAMB",)])
                    lv = LAMB[:].rearrange("p (a b) c -> p a b c", b=2)
                    T.op("dve", lambda e: e.tensor_tensor(out=LAMP[:], in0=lv[:, :, 0, :], in1=lv[:, :, 1, :], op=ALU.mult),
                         reads=[("LAMB",)], writes=[("LAMP",)])
                    T.op("dve", lambda e: e.tensor_reduce(out=LAMS[:, 0:2], in_=LAMP[:], axis=AX.X, op=ALU.add),
                         reads=[("LAMP",)], writes=[("LAMS", 0)])
                    T.op("act", lambda e: e.activation(out=LAMS[:, 2:4], in_=LAMS[:, 0:2], func=AF.Exp),
                         reads=[("LAMS", 0)], writes=[("LAMS", 2)])
                    T.op("dve", lambda e: e.scalar_tensor_tensor(out=LAMS[:, 4:5], in0=LAMS[:, 3:4], scalar=float(-lam_init),
                                                                 in1=LAMS[:, 2:3], op0=ALU.add, op1=ALU.subtract),
                         reads=[("LAMS", 2)], writes=[("LAMS", 4)])
                else:
                    T.dma("sp", SMW[:, 0, :], b_q_norm[j:j + 1, :].partition_broadcast(128), writes=[("SMW", 0)])
                    T.dma("sp", SMW[:, 1, :], b_k_norm[j:j + 1, :].partition_broadcast(128), writes=[("SMW", 1)])
                T.barrier()

        def load_win(layer):
            typ = "A" if layer % 2 == 0 else "B"
            j = layer // 2
            with ExitStack() as ls:
                STG = sb("STGi", [128, 2, 2048], F32, ls)
                if typ == "A":
                    load_weight(WINv, a_w_in[j], A_IN, 2048, STG, "WIN")
                else:
                    load_weight(WINv, b_w_in[j], B_IN, 1280, STG, "WIN")
                T.barrier()

        def phase_P(layer, si):
            typ = "A" if layer % 2 == 0 else "B"
            xin, yout, S = seqs[si]
            QT_d, KT_d, VX_d, G_d = QT_all[si], KT_all[si], VX_all[si], G_all[si]
            xsrc = xin if layer == 0 else yout
            NT = S // 128
            NG = 8 if typ == "A" else 5
            tab_d = tab_a if typ == "A" else tab_b
            tabw = 128 if typ == "A" else 512
            nkc = 8 if typ == "A" else 2
            nvh = 8 if typ == "A" else 2
            if True:
                XT = lb("XT", [128, 2, 1024], F32)
                TAB = lb("TAB", [128, 3, 512], F32)
                ST = lb("ST", [128, 2, 16], F32)
                H = lb("H", [128, 2, 1024], BF16)
                HT = lb("HT", [128, 2, 8, 128], BF16)
                QKF = lb("QKF", [128, 8, 512], BF16)
                SQ = lb("SQ", [128, 3, 512], F32)
                QN = lb("QN", [128, 3, 512], F32)
                RT = lb("RT", [128, 3, 4, 256], F32)
                ST2 = lb("ST2", [128, 3, 12], F32)
                VXS = lb("VXS", [128, 2, 8, 2, 129], BF16)
                GS = lb("GS", [128, 2, 1024], F32)
                QTS = lb("QTS", [128, 2, 8, 256], BF16)
                KTS = lb("KTS", [128, 2, 8, 256], BF16)

                if first("vxs_ones"):
                    for sl in range(2):
                        T.op("pool", lambda e, sl=sl: e.memset(VXS[:, sl, :, :, 128:129], 1.0), writes=[("VXS", sl, "ones")])

                qc = {"qkf": 0, "rt": 0}

                def loads(t):
                    xsl = t % 2
                    T.dma("sp", XT[:, xsl, :], xsrc[t * 128:(t + 1) * 128, :], writes=[("XT", xsl)])
                    T.dma("sp", TAB[:, t % 3, 0:tabw], tab_d[t * 128:(t + 1) * 128, :], writes=[("TAB", t % 3)])

                def head(t):
                    xsl = t % 2
                    T.op("act", lambda e: e.activation(out=JK[:], in_=XT[:, xsl, :], func=AF.Square, accum_out=ST[:, xsl, 0:1]),
                         reads=[("XT", xsl)], writes=[("ST", xsl, 0), ("JK",)])
                    T.op("dve", lambda e: e.tensor_scalar(out=ST[:, xsl, 1:2], in0=ST[:, xsl, 0:1], scalar1=1.0 / D, scalar2=EPS,
                                                          op0=ALU.mult, op1=ALU.add), reads=[("ST", xsl, 0)], writes=[("ST", xsl, 1)])
                    T.op("pool", lambda e: e.tensor_tensor(out=ST[:, xsl, 2:3], in0=ST[:, xsl, 1:2], in1=NEGH[:, 0:1], op=ALU.pow),
                         reads=[("ST", xsl, 1), ("NEGH",)], writes=[("ST", xsl, 2)])
                    T.op("dve", lambda e: e.scalar_tensor_tensor(out=H[:, xsl, :], in0=XT[:, xsl, :], scalar=ST[:, xsl, 2:3],
                                                                 in1=NPRE[:], op0=ALU.mult, op1=ALU.mult),
                         reads=[("XT", xsl), ("ST", xsl, 2), ("NPRE",)], writes=[("H", xsl)])

                    def th(e):
                        for kc in range(8):
                            ins = e.transpose(out=TPS[:, kc * 128:(kc + 1) * 128], in_=H[:, xsl, kc * 128:(kc + 1) * 128], identity=IDENT[:])
                        return ins
                    T.op("pe", th, reads=[("H", xsl), ("IDENT",)], writes=[("TPS",)])
                    T.op("act", lambda e: e.activation(out=HT[:, xsl, :, :], in_=TPS[:].rearrange("p (k t) -> p k t", k=8), func=AF.Copy),
                         reads=[("TPS",)], writes=[("HT", xsl)])

                def body(t):
                    xsl = t % 2
                    tsl = t % 3
                    tt = t % 2
                    sts = (t // 2) % 2
                    pend_tq = []

                    def mm_group(ng):
                        ps = pj_slot()
                        def f(e):
                            for kc in range(8):
                                ins = e.matmul(PJ[:, ps, :], lhsT=HT[:, xsl, kc, :], rhs=WINv[:, kc, ng * 512:(ng + 1) * 512],
                                               start=(kc == 0), stop=(kc == 7))
                            return ins
                        T.op("pe", f, reads=[("HT", xsl)], writes=[("PJ", ps)])
                        return ps

                    def emit_tq(qs, nch, dst, c0):
                        def f(e):
                            for c in range(nch):
                                ins = e.transpose(out=TPS[:, c * 128:(c + 1) * 128], in_=QKF[:, qs, c * 128:(c + 1) * 128], identity=IDENT[:])
                            return ins
                        T.op("pe", f, reads=[("QKF", qs, "a"), ("QKF", qs, "b"), ("QKF", qs, "c"), ("IDENT",)], writes=[("TPS",)])
                        dkey = "QTS" if dst is QTS else "KTS"
                        T.op("dve", lambda e: e.tensor_copy(out=dst[:, sts, c0:c0 + nch, tt * 128:(tt + 1) * 128],
                                                            in_=TPS[:, 0:nch * 128].rearrange("p (c t) -> p c t", c=nch)),
                             reads=[("TPS",)], writes=[(dkey, sts, c0, tt)])

                    def post_A_qk(ng, ps):
                        qs = qc["qkf"] % 8
                        qc["qkf"] += 1
                        r = qc["rt"] % 3
                        qc["rt"] += 1
                        pjv = QN[:, r, :].rearrange("p (h d) -> p h d", d=64)
                        qfv = QKF[:, qs, :].rearrange("p (h d) -> p h d", d=64)
                        T.op("act", lambda e: e.activation(out=QN[:, r, :], in_=PJ[:, ps, :], func=AF.Copy),
                             reads=[("PJ", ps)], writes=[("QN", r)])
                        T.op("pool", lambda e: e.tensor_copy(out=qfv[:, :, 16:64], in_=pjv[:, :, 16:64]),
                             reads=[("QN", r)], writes=[("QKF", qs, "a")])
                        cos = TAB[:, tsl, 0:64].rearrange("p (h j) -> p h j", j=8)
                        sin = TAB[:, tsl, 64:128].rearrange("p (h j) -> p h j", j=8)
                        x1, x2 = pjv[:, :, 0:8], pjv[:, :, 8:16]
                        tv = [RT[:, r, k, 0:64].rearrange("p (h j) -> p h j", j=8) for k in range(4)]
                        for k, (xa, tb) in enumerate(((x1, cos), (x2, sin), (x2, cos), (x1, sin))):
                            T.op("dve", lambda e, k=k, xa=xa, tb=tb: e.tensor_tensor(out=tv[k], in0=xa, in1=tb, op=ALU.mult),
                                 reads=[("QN", r), ("TAB", tsl)], writes=[("RT", r, k)])
                        T.op("dve", lambda e: e.tensor_tensor(out=qfv[:, :, 0:8], in0=tv[0], in1=tv[1], op=ALU.subtract),
                             reads=[("RT", r, 0), ("RT", r, 1)], writes=[("QKF", qs, "b")])
                        T.op("dve", lambda e: e.tensor_tensor(out=qfv[:, :, 8:16], in0=tv[2], in1=tv[3], op=ALU.add),
                             reads=[("RT", r, 2), ("RT", r, 3)], writes=[("QKF", qs, "c")])
                        if ng < 2:
                            pend_tq.append((qs, 4, QTS, ng * 4))
                        else:
                            pend_tq.append((qs, 4, KTS, (ng - 2) * 4))

                    def post_B_qk(ps, col0, nh, wrow, dst, c0):
                        qs = qc["qkf"] % 8
                        qc["qkf"] += 1
                        r = qc["rt"] % 3
                        qc["rt"] += 1
                        n = nh * 128
                        qnk = [("QN", r, h) for h in range(nh)]
                        T.op("act", lambda e: e.activation(out=QN[:, r, 0:n], in_=PJ[:, ps, col0:col0 + n], func=AF.Copy),
                             reads=[("PJ", ps)], writes=[("QN", r, h) for h in range(4)])
                        T.op("act", lambda e: e.activation(out=SQ[:, r, 0:n], in_=QN[:, r, 0:n], func=AF.Square),
                             reads=qnk, writes=[("SQ", r)])
                        T.op("dve", lambda e: e.tensor_reduce(out=ST2[:, r, 0:nh], in_=SQ[:, r, 0:n].rearrange("p (h d) -> p h d", d=128),
                                                              axis=AX.X, op=ALU.add), reads=[("SQ", r)], writes=[("ST2", r, 0)])
                        T.op("dve", lambda e: e.tensor_scalar(out=ST2[:, r, 4:4 + nh], in0=ST2[:, r, 0:nh], scalar1=1.0 / 128, scalar2=EPS,
                                                              op0=ALU.mult, op1=ALU.add), reads=[("ST2", r, 0)], writes=[("ST2", r, 4)])
                        T.op("pool", lambda e: e.tensor_tensor(out=ST2[:, r, 8:8 + nh], in0=ST2[:, r, 4:4 + nh],
                                                               in1=NEGH[:, 0:nh], op=ALU.pow),
                             reads=[("ST2", r, 4), ("NEGH",)], writes=[("ST2", r, 8)])
                        for h in range(nh):
                            T.op("dve", lambda e, h=h: e.scalar_tensor_tensor(
                                out=QN[:, r, h * 128:(h + 1) * 128], in0=QN[:, r, h * 128:(h + 1) * 128],
                                scalar=ST2[:, r, 8 + h:9 + h], in1=SMW[:, wrow, :], op0=ALU.mult, op1=ALU.mult),
                                reads=[("QN", r, h), ("ST2", r, 8), ("SMW", wrow)], writes=[("QN", r, h)])
                        qnv = QN[:, r, 0:n].rearrange("p (h a f j) -> p h a f j", a=2, f=2, j=32)
                        qfv = QKF[:, qs, 0:n].rearrange("p (h a f j) -> p h a f j", a=2, f=2, j=32)
                        cos = TAB[:, tsl, 0:256].rearrange("p (h a j) -> p h a j", a=2, j=32)[:, 0:nh]
                        sin = TAB[:, tsl, 256:512].rearrange("p (h a j) -> p h a j", a=2, j=32)[:, 0:nh]
                        x1, x2 = qnv[:, :, :, 0, :], qnv[:, :, :, 1, :]
                        tv = [RT[:, r, k, 0:nh * 64].rearrange("p (h a j) -> p h a j", a=2, j=32) for k in range(4)]
                        for k, (xa, tb) in enumerate(((x1, cos), (x2, sin), (x2, cos), (x1, sin))):
                            T.op("dve", lambda e, k=k, xa=xa, tb=tb: e.tensor_tensor(out=tv[k], in0=xa, in1=tb, op=ALU.mult),
                                 reads=qnk + [("TAB", tsl)], writes=[("RT", r, k)])
                        T.op("dve", lambda e: e.tensor_tensor(out=qfv[:, :, :, 0, :], in0=tv[0], in1=tv[1], op=ALU.subtract),
                             reads=[("RT", r, 0), ("RT", r, 1)], writes=[("QKF", qs, "b")])
                        T.op("dve", lambda e: e.tensor_tensor(out=qfv[:, :, :, 1, :], in0=tv[2], in1=tv[3], op=ALU.add),
                             reads=[("RT", r, 2), ("RT", r, 3)], writes=[("QKF", qs, "c")])
                        pend_tq.append((qs, nh, dst, c0))

                    def post_v(ps, col0, nh, h0):
                        T.op("dve", lambda e: e.tensor_copy(out=VXS[:, sts, h0:h0 + nh, tt, 0:128],
                                                            in_=PJ[:, ps, col0:col0 + nh * 128].rearrange("p (h e) -> p h e", e=128)),
                             reads=[("PJ", ps)], writes=[("VXS", sts, tt, h0)])

                    def post_gate(ps, gc):
                        T.op("act", lambda e: e.activation(out=GS[:, xsl, gc * 512:(gc + 1) * 512], in_=PJ[:, ps, :], func=AF.Silu),
                             reads=[("PJ", ps)], writes=[("GS", xsl, gc)])

                    for ng in range(NG):
                        ps = mm_group(ng)
                        if typ == "A":
                            if ng < 4:
                                post_A_qk(ng, ps)
                            elif ng < 6:
                                post_v(ps, 0, 4, (ng - 4) * 4)
                            else:
                                post_gate(ps, ng - 6)
                        else:
                            if ng < 2:
                                post_B_qk(ps, 0, 4, 0, QTS, ng * 4)
                            elif ng == 2:
                                post_B_qk(ps, 0, 2, 1, KTS, 0)
                                post_v(ps, 256, 2, 0)
                            else:
                                post_gate(ps, ng - 3)
                    def emit_all_tq():
                        while pend_tq:
                            emit_tq(*pend_tq.pop(0))

                    def stores():
                        T.dma("pool", G_d[:, :, t, :].rearrange("h p e -> p h e"), GS[:, xsl, :].rearrange("p (h e) -> p h e", e=128),
                              reads=[("GS", xsl, 0), ("GS", xsl, 1)], tag="g")
                        if tt == 1:
                            t0 = (t - 1) * 128
                            T.dma("pool", QT_d.rearrange("(c p) s -> p c s", p=128)[:, :, t0:t0 + 256], QTS[:, sts, :, :],
                                  reads=[("QTS", sts, c0, k) for c0 in (0, 4) for k in (0, 1)], tag="q")
                            kreads = [("KTS", sts, c0, k) for c0 in ((0, 4) if typ == "A" else (0,)) for k in (0, 1)]
                            T.dma("pool", KT_d.rearrange("(c p) s -> p c s", p=128)[:, 0:nkc, t0:t0 + 256], KTS[:, sts, 0:nkc, :], reads=kreads, tag="k")
                            vreads = [("VXS", sts, "ones")] + [("VXS", sts, k, h0) for k in (0, 1) for h0 in ((0, 4) if typ == "A" else (0,))]
                            T.dma("pool", VX_d[0:nvh, :, t - 1:t + 1, :].rearrange("h p n e -> p h (n e)"),
                                  VXS[:, sts, 0:nvh, :, :].rearrange("p h n e -> p h (n e)"), reads=vreads, tag="v")

                    return lambda: (emit_all_tq(), stores())

                loads(0)
                if NT > 1:
                    loads(1)
                head(0)
                prev_tail = None
                for t in range(NT):
                    if t + 2 < NT:
                        loads(t + 2)
                    if t + 1 < NT:
                        head(t + 1)
                    tl = body(t)
                    if prev_tail is not None:
                        prev_tail()
                    prev_tail = tl
                prev_tail()

        def phase_T(layer, si):
            typ = "A" if layer % 2 == 0 else "B"
            _, _, S = seqs[si]
            QT_d, KT_d, VX_d, G_d = QT_all[si], KT_all[si], VX_all[si], G_all[si]
            NT = S // 128
            NKB = NT
            NQG = S // 512
            dk = 64 if typ == "A" else 128
            scale = float(dk) ** -0.5
            if True:
                KTB = lb("KTB", [128, 2, 2, S_P], BF16)
                VXB = lb("VXB", [128, 2, NTMAX, 129], BF16)
                QTB = lb("QTB", [128, 2, 512], BF16)
                GB = lb("GB", [128, 2, 4, 128], F32)
                PT = lb("PT", [128, 3, 512], BF16)
                O1 = lb("O1", [128, 4, 128], F32)
                DD = lb("DD", [128, 4, 128], F32)
                TMP = lb("TMP", [128, 2, 4, 128], F32)
                RZ = lb("RZ", [128, 2, 24], F32)

                ctr = {"kv": 0, "q": 0, "ev": 0}
                if typ == "A" and first("ktb_zero"):
                    for sl in range(2):
                        T.op("dve", lambda e, sl=sl: e.memset(KTB[64:128, sl, 0, :], 0.0), writes=[("KTB", sl, 0)])
                        T.op("pool", lambda e, sl=sl: e.memset(KTB[0:64, sl, 1, :], 0.0), writes=[("KTB", sl, 1)])

                def acc_ap(aset, qt):
                    b, jj = (2 * aset, qt) if qt < 3 else (2 * aset + 1, 0)
                    return ACC[:, b, jj * 129:(jj + 1) * 129]

                def load_kv(kchunk, vhead):
                    sl = ctr["kv"] % 2
                    ctr["kv"] += 1
                    if typ == "A":
                        T.dma("sp", KTB[0:64, sl, 0, 0:S], KT_d[kchunk * 128:kchunk * 128 + 64, 0:S], writes=[("KTB", sl, 0)])
                        T.dma("sp", KTB[64:128, sl, 1, 0:S], KT_d[kchunk * 128 + 64:(kchunk + 1) * 128, 0:S], writes=[("KTB", sl, 1)])
                    else:
                        T.dma("sp", KTB[:, sl, 0, 0:S], KT_d[kchunk * 128:(kchunk + 1) * 128, 0:S], writes=[("KTB", sl, 0)])
                    T.dma("sp", VXB[:, sl, 0:NT, :], VX_d[vhead, :, 0:NT, :], writes=[("VXB", sl)])
                    return sl

                def load_q(qchunk, qg, h):
                    sl = ctr["q"] % 2
                    ctr["q"] += 1
                    T.dma("sp", QTB[:, sl, :], QT_d[qchunk * 128:(qchunk + 1) * 128, qg * 512:(qg + 1) * 512], writes=[("QTB", sl)])
                    T.dma("sp", GB[:, sl, :, :], G_d[h, :, qg * 4:(qg + 1) * 4, :], writes=[("GB", sl)])
                    return sl

                def attn_unit(kvs, qs, p0):
                    aset = cnt["acc"] % 2
                    cnt["acc"] += 1
                    slots = {}

                    def qk(kb):
                        ps = pj_slot()
                        slots[kb] = ps
                        mp = p0 // 64
                        T.op("pe", lambda e: e.matmul(PJ[:, ps, :], lhsT=KTB[:, kvs, mp, kb * 128:(kb + 1) * 128],
                                                      rhs=QTB[:, qs, :], start=True, stop=True),
                             reads=[("KTB", kvs, mp), ("QTB", qs)], writes=[("PJ", ps)])
                        pts = cnt["pt"] % 3
                        cnt["pt"] += 1
                        T.op("act", lambda e: e.activation(out=PT[:, pts, :], in_=PJ[:, ps, :], func=AF.Exp, scale=scale),
                             reads=[("PJ", ps)], writes=[("PT", pts)])
                        slots[kb] = pts

                    def pv(kb):
                        pts = slots.pop(kb)
                        def f(e):
                            for qt in range(4):
                                ins = e.matmul(acc_ap(aset, qt), lhsT=PT[:, pts, qt * 128:(qt + 1) * 128], rhs=VXB[:, kvs, kb, :],
                                               start=(kb == 0 and qt in (0, 3)), stop=(kb == NKB - 1), skip_group_check=True)
                            return ins
                        T.op("pe", f, reads=[("PT", pts), ("VXB", kvs)], writes=[("ACC", aset)])

                    qk(0)
                    qk(1)
                    for kb in range(NKB):
                        if kb + 2 < NKB:
                            qk(kb + 2)
                        pv(kb)
                    return aset

                def recip_z(aset, r):
                    zv = ACC[:, 2 * aset, 0:387].rearrange("p (a b) -> p a b", b=129)[:, :, 128:129]
                    T.op("dve", lambda e: e.reciprocal(out=RZ[:, r, 0:3].rearrange("p (a b) -> p a b", b=1), in_=zv),
                         reads=[("ACC", aset)], writes=[("RZ", r, 0)])
                    T.op("dve", lambda e: e.reciprocal(out=RZ[:, r, 3:4], in_=ACC[:, 2 * aset + 1, 128:129]),
                         reads=[("ACC", aset)], writes=[("RZ", r, 3)])

                def og_out(h, qg, r, gsl):
                    ogv = OGv[:, qg * 4:(qg + 1) * 4, h * 128:(h + 1) * 128]
                    T.op("pool", lambda e: e.tensor_tensor(out=ogv, in0=TMP[:, r, :, :], in1=GB[:, gsl, :, :], op=ALU.mult),
                         reads=[("TMP", r, q) for q in range(4)] + [("GB", gsl)], writes=[("OG", h, qg)])

                def evac_B(aset, h, qg, gsl):
                    r = ctr["ev"] % 2
                    ctr["ev"] += 1
                    recip_z(aset, r)
                    for qt in range(4):
                        T.op("dve", lambda e, qt=qt: e.tensor_scalar(out=TMP[:, r, qt, :], in0=acc_ap(aset, qt)[:, 0:128],
                                                                      scalar1=RZ[:, r, qt:qt + 1], scalar2=None, op0=ALU.mult),
                             reads=[("ACC", aset), ("RZ", r, 0), ("RZ", r, 3)], writes=[("TMP", r, qt)])
                    og_out(h, qg, r, gsl)

                def evac_A0(aset):
                    r = ctr["ev"] % 2
                    ctr["ev"] += 1
                    recip_z(aset, r)
                    for qt in range(4):
                        T.op("dve", lambda e, qt=qt: e.tensor_scalar(out=O1[:, qt, :], in0=acc_ap(aset, qt)[:, 0:128],
                                                                      scalar1=RZ[:, r, qt:qt + 1], scalar2=None, op0=ALU.mult),
                             reads=[("ACC", aset), ("RZ", r, 0), ("RZ", r, 3)], writes=[("O1", qt)])

                def evac_A1(aset, h, qg, gsl):
                    r = ctr["ev"] % 2
                    ctr["ev"] += 1
                    recip_z(aset, r)
                    T.op("dve", lambda e: e.tensor_scalar(out=RZ[:, r, 4:8], in0=RZ[:, r, 0:4], scalar1=LAMS[:, 4:5], scalar2=None, op0=ALU.mult),
                         reads=[("RZ", r, 0), ("RZ", r, 3), ("LAMS", 4)], writes=[("RZ", r, 4)])
                    for qt in range(4):
                        T.op("dve", lambda e, qt=qt: e.scalar_tensor_tensor(out=DD[:, qt, :], in0=acc_ap(aset, qt)[:, 0:128],
                                                                             scalar=RZ[:, r, 4 + qt:5 + qt], in1=O1[:, qt, :],
                                                                             op0=ALU.mult, op1=ALU.add),
                             reads=[("ACC", aset), ("RZ", r, 4), ("O1", qt)], writes=[("DD", qt)])
                    for qt in range(4):
                        T.op("dve", lambda e, qt=qt: e.scalar_tensor_tensor(out=JKF[:], in0=DD[:, qt, :], scalar=1.0, in1=DD[:, qt, :],
                                                                             op0=ALU.mult, op1=ALU.mult, accum_out=RZ[:, r, 8 + qt:9 + qt]),
                             reads=[("DD", qt)], writes=[("RZ", r, 8 + qt), ("JKF",)])
                    T.op("dve", lambda e: e.tensor_scalar(out=RZ[:, r, 12:16], in0=RZ[:, r, 8:12], scalar1=1.0 / 128, scalar2=EPS,
                                                          op0=ALU.mult, op1=ALU.add),
                         reads=[("RZ", r, 8 + q) for q in range(4)], writes=[("RZ", r, 12)])
                    T.op("pool", lambda e: e.tensor_tensor(out=RZ[:, r, 16:20], in0=RZ[:, r, 12:16], in1=NEGH[:, 0:4],
                                                           op=ALU.pow), reads=[("RZ", r, 12), ("NEGH",)], writes=[("RZ", r, 16)])
                    for qt in range(4):
                        T.op("dve", lambda e, qt=qt: e.scalar_tensor_tensor(out=TMP[:, r, qt, :], in0=DD[:, qt, :],
                                                                             scalar=RZ[:, r, 16 + qt:17 + qt], in1=SMW[:, 0, :],
                                                                             op0=ALU.mult, op1=ALU.mult),
                             reads=[("DD", qt), ("RZ", r, 16), ("SMW", 0)], writes=[("TMP", r, qt)])
                    og_out(h, qg, r, gsl)

                if typ == "A":
                    heads = [(h, h, h) for h in range(8)]
                else:
                    heads = [(h, h // 4, h // 4) for h in range(8)]
                items = [(h, kch, vh, qg) for (h, kch, vh) in heads for qg in range(NQG)]
                kv_list = []
                for (h, kch, vh) in heads:
                    if not kv_list or kv_list[-1] != (kch, vh):
                        kv_list.append((kch, vh))
                kv_idx = 0
                kvs = load_kv(*kv_list[0])
                nxt_kvs = load_kv(*kv_list[1]) if len(kv_list) > 1 else None
                q_next = load_q(items[0][0], items[0][3], items[0][0])
                for i, (h, kch, vh, qg) in enumerate(items):
                    if (kch, vh) != kv_list[kv_idx]:
                        kv_idx += 1
                        kvs = nxt_kvs
                        if kv_idx + 1 < len(kv_list):
                            nxt_kvs = load_kv(*kv_list[kv_idx + 1])
                    cur = q_next
                    if i + 1 < len(items):
                        q_next = load_q(items[i + 1][0], items[i + 1][3], items[i + 1][0])
                    if typ == "A":
                        a0 = attn_unit(kvs, cur, 0)
                        evac_A0(a0)
                        a1 = attn_unit(kvs, cur, 64)
                        evac_A1(a1, h, qg, cur)
                    else:
                        a0 = attn_unit(kvs, cur, 0)
                        evac_B(a0, h, qg, cur)

        def phase_O(layer, si):
            xin, yout, S = seqs[si]
            xsrc = xin if layer == 0 else yout
            NT = S // 128
            if True:
                XT = lb("XTo", [128, 2, 1024], F32)
                OGT = lb("OGT", [128, 2, 8, 128], BF16)
                M = lb("M", [128, 2, 1024], F32)
                ST = lb("STo", [128, 2, 8], F32)
                T.dma("sp", XT[:, 0, :], xsrc[0:128, :], writes=[("XT", 0)])
                for t in range(NT):
                    sl = t % 2
                    if t + 1 < NT:
                        T.dma("sp", XT[:, (t + 1) % 2, :], xsrc[(t + 1) * 128:(t + 2) * 128, :], writes=[("XT", (t + 1) % 2)])

                    def th(e):
                        for wc in range(8):
                            ins = e.transpose(out=TPS[:, wc * 128:(wc + 1) * 128], in_=OGv[:, t, wc * 128:(wc + 1) * 128], identity=IDENT[:])
                        return ins
                    T.op("pe", th, reads=[("IDENT",)] + [("OG", hh, t // 4) for hh in range(8)], writes=[("TPS",)])
                    T.op("act", lambda e: e.activation(out=OGT[:, sl, :, :], in_=TPS[:].rearrange("p (k t) -> p k t", k=8), func=AF.Copy),
                         reads=[("TPS",)], writes=[("OGT", sl)])
                    for ng in range(2):
                        ps = pj_slot()
                        def f(e, ng=ng, ps=ps):
                            for wc in range(8):
                                ins = e.matmul(PJ[:, ps, :], lhsT=OGT[:, sl, wc, :], rhs=WOUT[:, wc, ng * 512:(ng + 1) * 512],
                                               start=(wc == 0), stop=(wc == 7))
                            return ins
                        T.op("pe", f, reads=[("OGT", sl)], writes=[("PJ", ps)])
                        T.op("act", lambda e, ng=ng, ps=ps: e.activation(out=M[:, sl, ng * 512:(ng + 1) * 512], in_=PJ[:, ps, :], func=AF.Copy),
                             reads=[("PJ", ps)], writes=[("M", sl, ng)])
                        T.op("dve", lambda e, ng=ng: e.scalar_tensor_tensor(out=JK[:, 0:512], in0=M[:, sl, ng * 512:(ng + 1) * 512], scalar=1.0,
                                                                             in1=M[:, sl, ng * 512:(ng + 1) * 512], op0=ALU.mult, op1=ALU.mult,
                                                                             accum_out=ST[:, sl, ng:ng + 1]),
                             reads=[("M", sl, ng)], writes=[("ST", sl, ng), ("JK",)])
                    T.op("dve", lambda e: e.tensor_tensor(out=ST[:, sl, 2:3], in0=ST[:, sl, 0:1], in1=ST[:, sl, 1:2], op=ALU.add),
                         reads=[("ST", sl, 0), ("ST", sl, 1)], writes=[("ST", sl, 2)])
                    T.op("dve", lambda e: e.tensor_scalar(out=ST[:, sl, 3:4], in0=ST[:, sl, 2:3], scalar1=1.0 / D, scalar2=EPS,
                                                          op0=ALU.mult, op1=ALU.add), reads=[("ST", sl, 2)], writes=[("ST", sl, 3)])
                    T.op("pool", lambda e: e.tensor_tensor(out=ST[:, sl, 4:5], in0=ST[:, sl, 3:4], in1=NEGH[:, 0:1], op=ALU.pow),
                         reads=[("ST", sl, 3), ("NEGH",)], writes=[("ST", sl, 4)])
                    T.op("dve", lambda e: e.scalar_tensor_tensor(out=M[:, sl, :], in0=M[:, sl, :], scalar=ST[:, sl, 4:5], in1=NPOST[:],
                                                                 op0=ALU.mult, op1=ALU.mult),
                         reads=[("M", sl, 0), ("M", sl, 1), ("ST", sl, 4), ("NPOST",)], writes=[("M", sl, 0), ("M", sl, 1)])
                    T.op("pool", lambda e: e.tensor_tensor(out=M[:, sl, :], in0=M[:, sl, :], in1=XT[:, sl, :], op=ALU.add),
                         reads=[("M", sl, 0), ("M", sl, 1), ("XT", sl)], writes=[("M", sl, 0), ("M", sl, 1)])
                    T.dma("pool", yout[t * 128:(t + 1) * 128, :], M[:, sl, :], reads=[("M", sl, 0), ("M", sl, 1)])

        T.barrier()
        for layer in layers:
            layer_setup(layer)
            if "W" in phases:
                load_win(layer)
            if "P" in phases:
                with ExitStack() as sc:
                    scope["es"], scope["cache"] = sc, {}
                    for si in seq_ids:
                        phase_P(layer, si)
                    T.barrier()
            with ExitStack() as sc:
                scope["es"], scope["cache"] = sc, {}
                for si in seq_ids:
                    if "T" in phases:
                        phase_T(layer, si)
                    if "O" in phases:
                        phase_O(layer, si)
                T.barrier()
        T.final_wait("sp")
        T.final_wait("pool")
    return nc


def _tables():
    pos = np.arange(S_P, dtype=np.float32)
    inv_a = (np.float32(500000.0) ** (-(np.arange(0, 16, 2, dtype=np.float32) / np.float32(16)))).astype(np.float32)
    ang = pos[:, None] * inv_a[None, :]
    ca, sa = np.cos(ang).astype(np.float32), np.sin(ang).astype(np.float32)
    tab_a = np.concatenate([np.tile(ca, (1, 8)), np.tile(sa, (1, 8))], axis=1)
    inv_b = (np.float32(10000.0) ** (-(np.arange(0, 64, 2, dtype=np.float32) / np.float32(64)))).astype(np.float32)
    rows = np.floor(pos / 64).astype(np.float32)
    cols = (pos - rows * 64).astype(np.float32)
    ar, ac = rows[:, None] * inv_b[None, :], cols[:, None] * inv_b[None, :]
    cosb = np.concatenate([np.cos(ar), np.cos(ac)], axis=1).astype(np.float32)
    sinb = np.concatenate([np.sin(ar), np.sin(ac)], axis=1).astype(np.float32)
    tab_b = np.concatenate([np.tile(cosb, (1, 4)), np.tile(sinb, (1, 4))], axis=1)
    return np.ascontiguousarray(tab_a, dtype=np.float32), np.ascontiguousarray(tab_b, dtype=np.float32)


_NC_CACHE = {}


def kernel(x_prompt, x_sample, norm_pre, norm_post, a_w_in, a_w_out, a_lam, a_subln,
           b_w_in, b_w_out, b_q_norm, b_k_norm):
    f = lambda a: np.ascontiguousarray(np.asarray(a), dtype=np.float32)
    x_prompt, x_sample = f(x_prompt), f(x_sample)
    shared = {
        "norm_pre": f(norm_pre), "norm_post": f(norm_post), "a_w_in": f(a_w_in), "a_w_out": f(a_w_out),
        "a_lam": f(a_lam), "a_subln": f(a_subln), "b_w_in": f(b_w_in), "b_w_out": f(b_w_out),
        "b_q_norm": f(b_q_norm), "b_k_norm": f(b_k_norm),
    }
    tab_a, tab_b = _tables()
    shared["tab_a"] = tab_a
    shared["tab_b"] = tab_b
    shared["ident"] = np.eye(128, dtype=np.float32).astype(ml_dtypes.bfloat16)
    if "nc" not in _NC_CACHE:
        _NC_CACHE["nc"] = build_nc()
    nc = _NC_CACHE["nc"]
    in_maps = []
    for c in range(NCORES):
        m = dict(shared)
        m["xp"] = x_prompt[2 * c:2 * c + 2]
        m["xs"] = x_sample[c:c + 1]
        in_maps.append(m)
    res = run_bass_kernel_spmd(nc, in_maps, core_ids=list(range(NCORES)))
    y_prompt = np.concatenate([np.asarray(r["yp"], dtype=np.float32) for r in res.results], axis=0)
    y_sample = np.concatenate([np.asarray(r["ys"], dtype=np.float32) for r in res.results], axis=0)
    return (y_prompt, y_sample)
```

```python
import math
from contextlib import ExitStack

import numpy as np
import ml_dtypes

import concourse.bass as bass
import concourse.mybir as mybir
from concourse.bass_utils import run_bass_kernel_spmd

F32 = mybir.dt.float32
BF16 = mybir.dt.bfloat16
AF = mybir.ActivationFunctionType
ALU = mybir.AluOpType
AX = mybir.AxisListType

D = 1024
DEPTH = 4
S_P, S_S = 4096, 2048
NCORES = 8
EPS = 1e-6
A_IN, B_IN = 4096, 2560
NTMAX = S_P // 128


class Tracker:
    COMPUTE = ("pe", "act", "dve", "pool")

    def __init__(self, nc, es, n_dma_sems=12):
        self.nc = nc
        self.eng = {"pe": nc.tensor, "act": nc.scalar, "dve": nc.vector, "pool": nc.gpsimd, "sp": nc.sync}
        self.sem = {}
        self.cnt = {}
        for e in self.COMPUTE:
            self.sem[e] = es.enter_context(nc.semaphore("s_" + e))
            self.cnt[e] = 0
        self.rings = {}
        for q in ("sp", "pool"):
            names = []
            for i in range(n_dma_sems):
                k = "d_%s%d" % (q, i)
                self.sem[k] = es.enter_context(nc.semaphore(k))
                self.cnt[k] = 0
                names.append(k)
            self.rings[q] = [names, 0]
        self.seen = {e: {} for e in self.eng}
        self.last_w = {}
        self.readers = {}

    def _wait(self, e, ticket):
        if ticket is None:
            return
        k, v = ticket
        if e == "pe" and k == "pe":
            return
        if self.seen[e].get(k, 0) >= v:
            return
        self.eng[e].wait_ge(self.sem[k], v)
        self.seen[e][k] = v

    def _deps(self, e, reads, writes):
        for r in reads:
            self._wait(e, self.last_w.get(r))
        for w in writes:
            self._wait(e, self.last_w.get(w))
            for t in self.readers.get(w, ()):
                self._wait(e, t)

    def _record(self, ticket, reads, writes):
        for w in writes:
            self.last_w[w] = ticket
            self.readers[w] = []
        for r in reads:
            self.readers.setdefault(r, []).append(ticket)

    def op(self, e, fn, reads=(), writes=(), tag=None):
        self._deps(e, reads, writes)
        ins = fn(self.eng[e])
        self.cnt[e] += 1
        ins.then_inc(self.sem[e], 1)
        t = (e, self.cnt[e])
        self._record(t, reads, writes)
        return t

    def dma(self, q, out, in_, reads=(), writes=(), tag=None):
        names, idx = self.rings[q]
        k = names[idx % len(names)]
        self.rings[q][1] = idx + 1
        if self.cnt[k] > 0:
            self._wait(q, (k, self.cnt[k]))
        self._deps(q, reads, writes)
        self.eng[q].dma_start(out=out, in_=in_).then_inc(self.sem[k], 16)
        self.cnt[k] += 16
        t = (k, self.cnt[k])
        self._record(t, reads, writes)
        return t

    def barrier(self):
        for e in self.eng:
            for k, v in self.cnt.items():
                if v > 0:
                    self._wait(e, (k, v))
        self.last_w.clear()
        self.readers.clear()

    def final_wait(self, e="sp"):
        for k, v in self.cnt.items():
            if v > 0:
                self._wait(e, (k, v))


def build_nc(layers=(0, 1, 2, 3), seq_ids=(0, 1, 2), phases="WPTO"):
    nc = bass.Bass("TRN2", target_bir_lowering=False)
    dt_in = lambda n, s, d=F32: nc.dram_tensor(n, s, d, kind="ExternalInput").ap()
    xp = dt_in("xp", [2, S_P, D])
    xs = dt_in("xs", [1, S_S, D])
    norm_pre = dt_in("norm_pre", [DEPTH, D])
    norm_post = dt_in("norm_post", [DEPTH, D])
    a_w_in = dt_in("a_w_in", [2, D, A_IN])
    a_w_out = dt_in("a_w_out", [2, D, D])
    a_lam = dt_in("a_lam", [2, 4, 64])
    a_subln = dt_in("a_subln", [2, 128])
    b_w_in = dt_in("b_w_in", [2, D, B_IN])
    b_w_out = dt_in("b_w_out", [2, D, D])
    b_q_norm = dt_in("b_q_norm", [2, 128])
    b_k_norm = dt_in("b_k_norm", [2, 128])
    tab_a = dt_in("tab_a", [S_P, 128])
    tab_b = dt_in("tab_b", [S_P, 512])
    ident_d = dt_in("ident", [128, 128], BF16)
    yp = nc.dram_tensor("yp", [2, S_P, D], F32, kind="ExternalOutput").ap()
    ys = nc.dram_tensor("ys", [1, S_S, D], F32, kind="ExternalOutput").ap()
    QT_all = nc.dram_tensor("QT_d", [3, 1024, S_P], BF16, kind="Internal").ap()
    KT_all = nc.dram_tensor("KT_d", [3, 1024, S_P], BF16, kind="Internal").ap()
    VX_all = nc.dram_tensor("VX_d", [3, 8, 128, NTMAX, 129], BF16, kind="Internal").ap()
    G_all = nc.dram_tensor("G_d", [3, 8, 128, NTMAX, 128], F32, kind="Internal").ap()

    seqs = [(xp[0], yp[0], S_P), (xp[1], yp[1], S_P), (xs[0], ys[0], S_S)]

    with ExitStack() as es:
        uniq = [0]

        def sb(n, s, d, st=es):
            uniq[0] += 1
            return st.enter_context(nc.sbuf_tensor("%s_%d" % (n, uniq[0]), s, d))
        T = Tracker(nc, es)

        BIG = sb("BIG", [128, 32768], BF16)
        WOUT = sb("WOUT", [128, 8, 1024], BF16)
        IDENT = sb("IDENT", [128, 128], BF16)
        NPRE = sb("NPRE", [128, 1024], F32)
        NPOST = sb("NPOST", [128, 1024], F32)
        SMW = sb("SMW", [128, 3, 128], F32)
        LAMB = sb("LAMB", [128, 4, 64], F32)
        LAMP = sb("LAMP", [128, 2, 64], F32)
        LAMS = sb("LAMS", [128, 8], F32)
        NEGH = sb("NEGH", [128, 4], F32)
        JK = sb("JK", [128, 1024], BF16)
        JKF = sb("JKF", [128, 128], F32)
        TPS = es.enter_context(nc.psum_tensor("TPS", [128, 1024], BF16))
        PJ = es.enter_context(nc.psum_tensor("PJ", [128, 3, 512], F32))
        ACC = es.enter_context(nc.psum_tensor("ACC", [128, 4, 512], F32))

        WINv = BIG[:].rearrange("p (k n) -> p k n", k=8)
        OGv = BIG[:].rearrange("p (t w) -> p t w", w=1024)

        T.dma("sp", IDENT[:], ident_d, writes=[("IDENT",)])
        T.op("dve", lambda e: e.memset(NEGH[:], -0.5), writes=[("NEGH",)])

        cnt = {"pj": 0, "pt": 0, "acc": 0}
        scope = {"es": None, "cache": {}}

        def lb(name, shape, dtype):
            if name not in scope["cache"]:
                scope["cache"][name] = sb(name, shape, dtype, scope["es"])
            return scope["cache"][name]

        def first(flag):
            if flag in scope["cache"]:
                return False
            scope["cache"][flag] = True
            return True

        def pj_slot():
            s = cnt["pj"] % 3
            cnt["pj"] += 1
            return s

        def load_weight(dst3, src, ncols, chunk, STG, key):
            i = 0
            for kc in range(8):
                for c0 in range(0, ncols, chunk):
                    w = min(chunk, ncols - c0)
                    sl = i % 2
                    T.dma("sp", STG[:, sl, 0:w], src[kc * 128:(kc + 1) * 128, c0:c0 + w], writes=[("STG", sl)])
                    eng = "dve" if i % 2 == 0 else "pool"
                    T.op(eng, lambda e, kc=kc, c0=c0, w=w, sl=sl: e.tensor_copy(out=dst3[:, kc, c0:c0 + w], in_=STG[:, sl, 0:w]),
                         reads=[("STG", sl)], writes=[(key, kc, c0)])
                    i += 1

        def layer_setup(layer):
            typ = "A" if layer % 2 == 0 else "B"
            j = layer // 2
            with ExitStack() as ls:
                STG = sb("STGo", [128, 2, 1024], F32, ls)
                load_weight(WOUT, (a_w_out if typ == "A" else b_w_out)[j], 1024, 1024, STG, "WOUT")
                T.dma("sp", NPRE[:], norm_pre[layer:layer + 1, :].partition_broadcast(128), writes=[("NPRE",)])
                T.dma("sp", NPOST[:], norm_post[layer:layer + 1, :].partition_broadcast(128), writes=[("NPOST",)])
                if typ == "A":
                    lam_init = 0.8 - 0.6 * math.exp(-0.3 * layer)
                    T.dma("sp", SMW[:, 2, :], a_subln[j:j + 1, :].partition_broadcast(128), writes=[("SMW", 2)])
                    T.op("dve", lambda e: e.tensor_scalar(out=SMW[:, 0, :], in0=SMW[:, 2, :], scalar1=float(1.0 - lam_init),
                                                          scalar2=None, op0=ALU.mult), reads=[("SMW", 2)], writes=[("SMW", 0)])
                    T.dma("sp", LAMB[:].rearrange("p a b -> p (a b)"),
                          a_lam[j:j + 1].rearrange("o a b -> o (a b)").partition_broadcast(128), writes=[("LAMB",)])
                    lv = LAMB[:].rearrange("p (a b) c -> p a b c", b=2)
                    T.op("dve", lambda e: e.tensor_tensor(out=LAMP[:], in0=lv[:, :, 0, :], in1=lv[:, :, 1, :], op=ALU.mult),
                         reads=[("LAMB",)], writes=[("LAMP",)])
                    T.op("dve", lambda e: e.tensor_reduce(out=LAMS[:, 0:2], in_=LAMP[:], axis=AX.X, op=ALU.add),
                         reads=[("LAMP",)], writes=[("LAMS", 0)])
                    T.op("act", lambda e: e.activation(out=LAMS[:, 2:4], in_=LAMS[:, 0:2], func=AF.Exp),
                         reads=[("LAMS", 0)], writes=[("LAMS", 2)])
                    T.op("dve", lambda e: e.scalar_tensor_tensor(out=LAMS[:, 4:5], in0=LAMS[:, 3:4], scalar=float(-lam_init),
                                                                 in1=LAMS[:, 2:3], op0=ALU.add, op1=ALU.subtract),
                         reads=[("LAMS", 2)], writes=[("LAMS", 4)])
                else:
                    T.dma("sp", SMW[:, 0, :], b_q_norm[j:j + 1, :].partition_broadcast(128), writes=[("SMW", 0)])
                    T.dma("sp", SMW[:, 1, :], b_k_norm[j:j + 1, :].partition_broadcast(128), writes=[("SMW", 1)])
                T.barrier()

        def load_win(layer):
            typ = "A" if layer % 2 == 0 else "B"
            j = layer // 2
            with ExitStack() as ls:
                STG = sb("STGi", [128, 2, 2048], F32, ls)
                if typ == "A":
                    load_weight(WINv, a_w_in[j], A_IN, 2048, STG, "WIN")
                else:
                    load_weight(WINv, b_w_in[j], B_IN, 1280, STG, "WIN")
                T.barrier()

        def phase_P(layer, si):
            typ = "A" if layer % 2 == 0 else "B"
            xin, yout, S = seqs[si]
            QT_d, KT_d, VX_d, G_d = QT_all[si], KT_all[si], VX_all[si], G_all[si]
            xsrc = xin if layer == 0 else yout
            NT = S // 128
            NG = 8 if typ == "A" else 5
            tab_d = tab_a if typ == "A" else tab_b
            tabw = 128 if typ == "A" else 512
            nkc = 8 if typ == "A" else 2
            nvh = 8 if typ == "A" else 2
            if True:
                XT = lb("XT", [128, 3, 1024], F32)
                TAB = lb("TAB", [128, 3, 512], F32)
                ST = lb("ST", [128, 3, 16], F32)
                H = lb("H", [128, 2, 1024], BF16)
                HT = lb("HT", [128, 2, 8, 128], BF16)
                QKF = lb("QKF", [128, 8, 512], BF16)
                SQ = lb("SQ", [128, 3, 512], F32)
                QN = lb("QN", [128, 3, 512], F32)
                RT = lb("RT", [128, 3, 4, 256], F32)
                ST2 = lb("ST2", [128, 3, 12], F32)
                VXS = lb("VXS", [128, 2, 8, 2, 129], BF16)
                GS = lb("GS", [128, 2, 1024], F32)
                QTS = lb("QTS", [128, 2, 8, 256], BF16)
                KTS = lb("KTS", [128, 2, 8, 256], BF16)

                if first("vxs_ones"):
                    for sl in range(2):
                        T.op("pool", lambda e, sl=sl: e.memset(VXS[:, sl, :, :, 128:129], 1.0), writes=[("VXS", sl, "ones")])

                qc = {"qkf": 0, "rt": 0}

                def load_x(t):
                    T.dma("sp", XT[:, t % 3, :], xsrc[t * 128:(t + 1) * 128, :], writes=[("XT", t % 3)])

                def load_tab(t):
                    T.dma("sp", TAB[:, t % 3, 0:tabw], tab_d[t * 128:(t + 1) * 128, :], writes=[("TAB", t % 3)])

                def head_stats(t):
                    x3, hs = t % 3, t % 2
                    T.op("act", lambda e: e.activation(out=JK[:], in_=XT[:, x3, :], func=AF.Square, accum_out=ST[:, x3, 0:1]),
                         reads=[("XT", x3)], writes=[("ST", x3, 0), ("JK",)])
                    T.op("dve", lambda e: e.tensor_scalar(out=ST[:, x3, 1:2], in0=ST[:, x3, 0:1], scalar1=1.0 / D, scalar2=EPS,
                                                          op0=ALU.mult, op1=ALU.add), reads=[("ST", x3, 0)], writes=[("ST", x3, 1)])
                    T.op("pool", lambda e: e.tensor_tensor(out=ST[:, x3, 2:3], in0=ST[:, x3, 1:2], in1=NEGH[:, 0:1], op=ALU.pow),
                         reads=[("ST", x3, 1), ("NEGH",)], writes=[("ST", x3, 2)])
                    T.op("dve", lambda e: e.scalar_tensor_tensor(out=H[:, hs, :], in0=XT[:, x3, :], scalar=ST[:, x3, 2:3],
                                                                 in1=NPRE[:], op0=ALU.mult, op1=ALU.mult),
                         reads=[("XT", x3), ("ST", x3, 2), ("NPRE",)], writes=[("H", hs)])

                def head_tr(t):
                    hs = t % 2
                    def th(e):
                        for kc in range(8):
                            ins = e.transpose(out=TPS[:, kc * 128:(kc + 1) * 128], in_=H[:, hs, kc * 128:(kc + 1) * 128], identity=IDENT[:])
                        return ins
                    T.op("pe", th, reads=[("H", hs), ("IDENT",)], writes=[("TPS",)])
                    T.op("act", lambda e: e.activation(out=HT[:, hs, :, :], in_=TPS[:].rearrange("p (k t) -> p k t", k=8), func=AF.Copy),
                         reads=[("TPS",)], writes=[("HT", hs)])

                def body(t):
                    xsl = t % 2
                    tsl = t % 3
                    tt = t % 2
                    sts = (t // 2) % 2
                    pend_tq = []

                    def mm_group(ng):
                        ps = pj_slot()
                        def f(e):
                            for kc in range(8):
                                ins = e.matmul(PJ[:, ps, :], lhsT=HT[:, xsl, kc, :], rhs=WINv[:, kc, ng * 512:(ng + 1) * 512],
                                               start=(kc == 0), stop=(kc == 7))
                            return ins
                        T.op("pe", f, reads=[("HT", xsl)], writes=[("PJ", ps)])
                        return ps

                    def emit_tq(qs, nch, dst, c0):
                        def f(e):
                            for c in range(nch):
                                ins = e.transpose(out=TPS[:, c * 128:(c + 1) * 128], in_=QKF[:, qs, c * 128:(c + 1) * 128], identity=IDENT[:])
                            return ins
                        T.op("pe", f, reads=[("QKF", qs, "a"), ("QKF", qs, "b"), ("QKF", qs, "c"), ("IDENT",)], writes=[("TPS",)])
                        dkey = "QTS" if dst is QTS else "KTS"
                        T.op("act", lambda e: e.activation(out=dst[:, sts, c0:c0 + nch, tt * 128:(tt + 1) * 128],
                                                           in_=TPS[:, 0:nch * 128].rearrange("p (c t) -> p c t", c=nch), func=AF.Copy),
                             reads=[("TPS",)], writes=[(dkey, sts, c0, tt)])

                    def post_A_qk(ng, ps):
                        qs = qc["qkf"] % 8
                        qc["qkf"] += 1
                        r = qc["rt"] % 3
                        qc["rt"] += 1
                        pjv = QN[:, r, :].rearrange("p (h d) -> p h d", d=64)
                        qfv = QKF[:, qs, :].rearrange("p (h d) -> p h d", d=64)
                        T.op("act", lambda e: e.activation(out=QN[:, r, :], in_=PJ[:, ps, :], func=AF.Copy),
                             reads=[("PJ", ps)], writes=[("QN", r)])
                        T.op("pool", lambda e: e.tensor_copy(out=qfv[:, :, 16:64], in_=pjv[:, :, 16:64]),
                             reads=[("QN", r)], writes=[("QKF", qs, "a")])
                        cos = TAB[:, tsl, 0:64].rearrange("p (h j) -> p h j", j=8)
                        sin = TAB[:, tsl, 64:128].rearrange("p (h j) -> p h j", j=8)
                        x1, x2 = pjv[:, :, 0:8], pjv[:, :, 8:16]
                        tv = [RT[:, r, k, 0:64].rearrange("p (h j) -> p h j", j=8) for k in range(4)]
                        for k, (xa, tb) in enumerate(((x1, cos), (x2, sin), (x2, cos), (x1, sin))):
                            T.op("dve", lambda e, k=k, xa=xa, tb=tb: e.tensor_tensor(out=tv[k], in0=xa, in1=tb, op=ALU.mult),
                                 reads=[("QN", r), ("TAB", tsl)], writes=[("RT", r, k)])
                        T.op("dve", lambda e: e.tensor_tensor(out=qfv[:, :, 0:8], in0=tv[0], in1=tv[1], op=ALU.subtract),
                             reads=[("RT", r, 0), ("RT", r, 1)], writes=[("QKF", qs, "b")])
                        T.op("dve", lambda e: e.tensor_tensor(out=qfv[:, :, 8:16], in0=tv[2], in1=tv[3], op=ALU.add),
                             reads=[("RT", r, 2), ("RT", r, 3)], writes=[("QKF", qs, "c")])
                        if ng < 2:
                            pend_tq.append((qs, 4, QTS, ng * 4))
                        else:
                            pend_tq.append((qs, 4, KTS, (ng - 2) * 4))

                    def post_B_qk(ps, col0, nh, wrow, dst, c0):
                        qs = qc["qkf"] % 8
                        qc["qkf"] += 1
                        r = qc["rt"] % 3
                        qc["rt"] += 1
                        n = nh * 128
                        qnk = [("QN", r, h) for h in range(nh)]
                        T.op("act", lambda e: e.activation(out=QN[:, r, 0:n], in_=PJ[:, ps, col0:col0 + n], func=AF.Copy),
                             reads=[("PJ", ps)], writes=[("QN", r, h) for h in range(4)])
                        T.op("act", lambda e: e.activation(out=SQ[:, r, 0:n], in_=QN[:, r, 0:n], func=AF.Square),
                             reads=qnk, writes=[("SQ", r)])
                        T.op("dve", lambda e: e.tensor_reduce(out=ST2[:, r, 0:nh], in_=SQ[:, r, 0:n].rearrange("p (h d) -> p h d", d=128),
                                                              axis=AX.X, op=ALU.add), reads=[("SQ", r)], writes=[("ST2", r, 0)])
                        T.op("dve", lambda e: e.tensor_scalar(out=ST2[:, r, 4:4 + nh], in0=ST2[:, r, 0:nh], scalar1=1.0 / 128, scalar2=EPS,
                                                              op0=ALU.mult, op1=ALU.add), reads=[("ST2", r, 0)], writes=[("ST2", r, 4)])
                        T.op("pool", lambda e: e.tensor_tensor(out=ST2[:, r, 8:8 + nh], in0=ST2[:, r, 4:4 + nh],
                                                               in1=NEGH[:, 0:nh], op=ALU.pow),
                             reads=[("ST2", r, 4), ("NEGH",)], writes=[("ST2", r, 8)])
                        for h in range(nh):
                            T.op("dve", lambda e, h=h: e.scalar_tensor_tensor(
                                out=QN[:, r, h * 128:(h + 1) * 128], in0=QN[:, r, h * 128:(h + 1) * 128],
                                scalar=ST2[:, r, 8 + h:9 + h], in1=SMW[:, wrow, :], op0=ALU.mult, op1=ALU.mult),
                                reads=[("QN", r, h), ("ST2", r, 8), ("SMW", wrow)], writes=[("QN", r, h)])
                        qnv = QN[:, r, 0:n].rearrange("p (h a f j) -> p h a f j", a=2, f=2, j=32)
                        qfv = QKF[:, qs, 0:n].rearrange("p (h a f j) -> p h a f j", a=2, f=2, j=32)
                        cos = TAB[:, tsl, 0:256].rearrange("p (h a j) -> p h a j", a=2, j=32)[:, 0:nh]
                        sin = TAB[:, tsl, 256:512].rearrange("p (h a j) -> p h a j", a=2, j=32)[:, 0:nh]
                        x1, x2 = qnv[:, :, :, 0, :], qnv[:, :, :, 1, :]
                        tv = [RT[:, r, k, 0:nh * 64].rearrange("p (h a j) -> p h a j", a=2, j=32) for k in range(4)]
                        for k, (xa, tb) in enumerate(((x1, cos), (x2, sin), (x2, cos), (x1, sin))):
                            T.op("dve", lambda e, k=k, xa=xa, tb=tb: e.tensor_tensor(out=tv[k], in0=xa, in1=tb, op=ALU.mult),
                                 reads=qnk + [("TAB", tsl)], writes=[("RT", r, k)])
                        T.op("dve", lambda e: e.tensor_tensor(out=qfv[:, :, :, 0, :], in0=tv[0], in1=tv[1], op=ALU.subtract),
                             reads=[("RT", r, 0), ("RT", r, 1)], writes=[("QKF", qs, "b")])
                        T.op("dve", lambda e: e.tensor_tensor(out=qfv[:, :, :, 1, :], in0=tv[2], in1=tv[3], op=ALU.add),
                             reads=[("RT", r, 2), ("RT", r, 3)], writes=[("QKF", qs, "c")])
                        pend_tq.append((qs, nh, dst, c0))

                    def post_v(ps, col0, nh, h0):
                        T.op("dve", lambda e: e.tensor_copy(out=VXS[:, sts, h0:h0 + nh, tt, 0:128],
                                                            in_=PJ[:, ps, col0:col0 + nh * 128].rearrange("p (h e) -> p h e", e=128)),
                             reads=[("PJ", ps)], writes=[("VXS", sts, tt, h0)])

                    def post_gate(ps, gc):
                        T.op("act", lambda e: e.activation(out=GS[:, xsl, gc * 512:(gc + 1) * 512], in_=PJ[:, ps, :], func=AF.Silu),
                             reads=[("PJ", ps)], writes=[("GS", xsl, gc)])

                    for ng in range(NG):
                        ps = mm_group(ng)
                        if typ == "A":
                            if ng < 4:
                                post_A_qk(ng, ps)
                            elif ng < 6:
                                post_v(ps, 0, 4, (ng - 4) * 4)
                            else:
                                post_gate(ps, ng - 6)
                        else:
                            if ng < 2:
                                post_B_qk(ps, 0, 4, 0, QTS, ng * 4)
                            elif ng == 2:
                                post_B_qk(ps, 0, 2, 1, KTS, 0)
                                post_v(ps, 256, 2, 0)
                            else:
                                post_gate(ps, ng - 3)
                    def emit_all_tq():
                        while pend_tq:
                            emit_tq(*pend_tq.pop(0))

                    def stores():
                        T.dma("pool", G_d[:, :, t, :].rearrange("h p e -> p h e"), GS[:, xsl, :].rearrange("p (h e) -> p h e", e=128),
                              reads=[("GS", xsl, 0), ("GS", xsl, 1)], tag="g")
                        if tt == 1:
                            t0 = (t - 1) * 128
                            T.dma("pool", QT_d.rearrange("(c p) s -> p c s", p=128)[:, :, t0:t0 + 256], QTS[:, sts, :, :],
                                  reads=[("QTS", sts, c0, k) for c0 in (0, 4) for k in (0, 1)], tag="q")
                            kreads = [("KTS", sts, c0, k) for c0 in ((0, 4) if typ == "A" else (0,)) for k in (0, 1)]
                            T.dma("pool", KT_d.rearrange("(c p) s -> p c s", p=128)[:, 0:nkc, t0:t0 + 256], KTS[:, sts, 0:nkc, :], reads=kreads, tag="k")
                            vreads = [("VXS", sts, "ones")] + [("VXS", sts, k, h0) for k in (0, 1) for h0 in ((0, 4) if typ == "A" else (0,))]
                            T.dma("pool", VX_d[0:nvh, :, t - 1:t + 1, :].rearrange("h p n e -> p h (n e)"),
                                  VXS[:, sts, 0:nvh, :, :].rearrange("p h n e -> p h (n e)"), reads=vreads, tag="v")

                    return lambda: (emit_all_tq(), stores())

                for t0_ in range(min(3, NT)):
                    load_x(t0_)
                load_tab(0)
                head_stats(0)
                if NT > 1:
                    head_stats(1)
                head_tr(0)
                prev_tail = None
                for t in range(NT):
                    if t + 3 < NT:
                        load_x(t + 3)
                    if t + 1 < NT:
                        load_tab(t + 1)
                    if t + 2 < NT:
                        head_stats(t + 2)
                    tl = body(t)
                    if t + 1 < NT:
                        head_tr(t + 1)
                    if prev_tail is not None:
                        prev_tail()
                    prev_tail = tl
                prev_tail()

        def phase_T(layer, si):
            typ = "A" if layer % 2 == 0 else "B"
            _, _, S = seqs[si]
            QT_d, KT_d, VX_d, G_d = QT_all[si], KT_all[si], VX_all[si], G_all[si]
            NT = S // 128
            NKB = NT
            NQG = S // 512
            dk = 64 if typ == "A" else 128
            scale = float(dk) ** -0.5
            if True:
                KTB = lb("KTB", [128, 2, 2, S_P], BF16)
                VXB = lb("VXB", [128, 2, NTMAX, 129], BF16)
                QTB = lb("QTB", [128, 2, 512], BF16)
                GB = lb("GB", [128, 2, 4, 128], F32)
                PT = lb("PT", [128, 3, 512], BF16)
                O1 = lb("O1", [128, 4, 128], F32)
                DD = lb("DD", [128, 4, 128], F32)
                TMP = lb("TMP", [128, 2, 4, 128], F32)
                RZ = lb("RZ", [128, 2, 24], F32)

                ctr = {"kv": 0, "q": 0, "ev": 0}
                if typ == "A" and first("ktb_zero"):
                    for sl in range(2):
                        T.op("dve", lambda e, sl=sl: e.memset(KTB[64:128, sl, 0, :], 0.0), writes=[("KTB", sl, 0)])
                        T.op("pool", lambda e, sl=sl: e.memset(KTB[0:64, sl, 1, :], 0.0), writes=[("KTB", sl, 1)])

                def acc_ap(aset, qt):
                    b, jj = (2 * aset, qt) if qt < 3 else (2 * aset + 1, 0)
                    return ACC[:, b, jj * 129:(jj + 1) * 129]

                def load_kv(kchunk, vhead):
                    sl = ctr["kv"] % 2
                    ctr["kv"] += 1
                    if typ == "A":
                        T.dma("sp", KTB[0:64, sl, 0, 0:S], KT_d[kchunk * 128:kchunk * 128 + 64, 0:S], writes=[("KTB", sl, 0)])
                        T.dma("sp", KTB[64:128, sl, 1, 0:S], KT_d[kchunk * 128 + 64:(kchunk + 1) * 128, 0:S], writes=[("KTB", sl, 1)])
                    else:
                        T.dma("sp", KTB[:, sl, 0, 0:S], KT_d[kchunk * 128:(kchunk + 1) * 128, 0:S], writes=[("KTB", sl, 0)])
                    T.dma("sp", VXB[:, sl, 0:NT, :], VX_d[vhead, :, 0:NT, :], writes=[("VXB", sl)])
                    return sl

                def load_q(qchunk, qg, h):
                    sl = ctr["q"] % 2
                    ctr["q"] += 1
                    T.dma("sp", QTB[:, sl, :], QT_d[qchunk * 128:(qchunk + 1) * 128, qg * 512:(qg + 1) * 512], writes=[("QTB", sl)])
                    T.dma("sp", GB[:, sl, :, :], G_d[h, :, qg * 4:(qg + 1) * 4, :], writes=[("GB", sl)])
                    return sl

                def attn_unit(kvs, qs, p0):
                    aset = cnt["acc"] % 2
                    cnt["acc"] += 1
                    slots = {}

                    def qk(kb):
                        ps = pj_slot()
                        slots[kb] = ps
                        mp = p0 // 64
                        T.op("pe", lambda e: e.matmul(PJ[:, ps, :], lhsT=KTB[:, kvs, mp, kb * 128:(kb + 1) * 128],
                                                      rhs=QTB[:, qs, :], start=True, stop=True),
                             reads=[("KTB", kvs, mp), ("QTB", qs)], writes=[("PJ", ps)])
                        pts = cnt["pt"] % 3
                        cnt["pt"] += 1
                        T.op("act", lambda e: e.activation(out=PT[:, pts, :], in_=PJ[:, ps, :], func=AF.Exp, scale=scale),
                             reads=[("PJ", ps)], writes=[("PT", pts)])
                        slots[kb] = pts

                    def pv(kb):
                        pts = slots.pop(kb)
                        def f(e):
                            for qt in range(4):
                                ins = e.matmul(acc_ap(aset, qt), lhsT=PT[:, pts, qt * 128:(qt + 1) * 128], rhs=VXB[:, kvs, kb, :],
                                               start=(kb == 0 and qt in (0, 3)), stop=(kb == NKB - 1), skip_group_check=True)
                            return ins
                        T.op("pe", f, reads=[("PT", pts), ("VXB", kvs)], writes=[("ACC", aset)])

                    qk(0)
                    qk(1)
                    for kb in range(NKB):
                        if kb + 2 < NKB:
                            qk(kb + 2)
                        pv(kb)
                    return aset

                def recip_z(aset, r):
                    zv = ACC[:, 2 * aset, 0:387].rearrange("p (a b) -> p a b", b=129)[:, :, 128:129]
                    T.op("dve", lambda e: e.reciprocal(out=RZ[:, r, 0:3].rearrange("p (a b) -> p a b", b=1), in_=zv),
                         reads=[("ACC", aset)], writes=[("RZ", r, 0)])
                    T.op("dve", lambda e: e.reciprocal(out=RZ[:, r, 3:4], in_=ACC[:, 2 * aset + 1, 128:129]),
                         reads=[("ACC", aset)], writes=[("RZ", r, 3)])

                def og_out(h, qg, r, gsl):
                    ogv = OGv[:, qg * 4:(qg + 1) * 4, h * 128:(h + 1) * 128]
                    T.op("pool", lambda e: e.tensor_tensor(out=ogv, in0=TMP[:, r, :, :], in1=GB[:, gsl, :, :], op=ALU.mult),
                         reads=[("TMP", r, q) for q in range(4)] + [("GB", gsl)], writes=[("OG", h, qg)])

                def evac_B(aset, h, qg, gsl):
                    r = ctr["ev"] % 2
                    ctr["ev"] += 1
                    recip_z(aset, r)
                    for qt in range(4):
                        T.op("dve", lambda e, qt=qt: e.tensor_scalar(out=TMP[:, r, qt, :], in0=acc_ap(aset, qt)[:, 0:128],
                                                                      scalar1=RZ[:, r, qt:qt + 1], scalar2=None, op0=ALU.mult),
                             reads=[("ACC", aset), ("RZ", r, 0), ("RZ", r, 3)], writes=[("TMP", r, qt)])
                    og_out(h, qg, r, gsl)

                def evac_A0(aset):
                    r = ctr["ev"] % 2
                    ctr["ev"] += 1
                    recip_z(aset, r)
                    for qt in range(4):
                        T.op("dve", lambda e, qt=qt: e.tensor_scalar(out=O1[:, qt, :], in0=acc_ap(aset, qt)[:, 0:128],
                                                                      scalar1=RZ[:, r, qt:qt + 1], scalar2=None, op0=ALU.mult),
                             reads=[("ACC", aset), ("RZ", r, 0), ("RZ", r, 3)], writes=[("O1", qt)])

                def evac_A1(aset, h, qg, gsl):
                    r = ctr["ev"] % 2
                    ctr["ev"] += 1
                    recip_z(aset, r)
                    T.op("dve", lambda e: e.tensor_scalar(out=RZ[:, r, 4:8], in0=RZ[:, r, 0:4], scalar1=LAMS[:, 4:5], scalar2=None, op0=ALU.mult),
                         reads=[("RZ", r, 0), ("RZ", r, 3), ("LAMS", 4)], writes=[("RZ", r, 4)])
                    for qt in range(4):
                        T.op("dve", lambda e, qt=qt: e.scalar_tensor_tensor(out=DD[:, qt, :], in0=acc_ap(aset, qt)[:, 0:128],
                                                                             scalar=RZ[:, r, 4 + qt:5 + qt], in1=O1[:, qt, :],
                                                                             op0=ALU.mult, op1=ALU.add),
                             reads=[("ACC", aset), ("RZ", r, 4), ("O1", qt)], writes=[("DD", qt)])
                    for qt in range(4):
                        T.op("dve", lambda e, qt=qt: e.scalar_tensor_tensor(out=JKF[:], in0=DD[:, qt, :], scalar=1.0, in1=DD[:, qt, :],
                                                                             op0=ALU.mult, op1=ALU.mult, accum_out=RZ[:, r, 8 + qt:9 + qt]),
                             reads=[("DD", qt)], writes=[("RZ", r, 8 + qt), ("JKF",)])
                    T.op("dve", lambda e: e.tensor_scalar(out=RZ[:, r, 12:16], in0=RZ[:, r, 8:12], scalar1=1.0 / 128, scalar2=EPS,
                                                          op0=ALU.mult, op1=ALU.add),
                         reads=[("RZ", r, 8 + q) for q in range(4)], writes=[("RZ", r, 12)])
                    T.op("pool", lambda e: e.tensor_tensor(out=RZ[:, r, 16:20], in0=RZ[:, r, 12:16], in1=NEGH[:, 0:4],
                                                           op=ALU.pow), reads=[("RZ", r, 12), ("NEGH",)], writes=[("RZ", r, 16)])
                    for qt in range(4):
                        T.op("dve", lambda e, qt=qt: e.scalar_tensor_tensor(out=TMP[:, r, qt, :], in0=DD[:, qt, :],
                                                                             scalar=RZ[:, r, 16 + qt:17 + qt], in1=SMW[:, 0, :],
                                                                             op0=ALU.mult, op1=ALU.mult),
                             reads=[("DD", qt), ("RZ", r, 16), ("SMW", 0)], writes=[("TMP", r, qt)])
                    og_out(h, qg, r, gsl)

                if typ == "A":
                    heads = [(h, h, h) for h in range(8)]
                else:
                    heads = [(h, h // 4, h // 4) for h in range(8)]
                items = [(h, kch, vh, qg) for (h, kch, vh) in heads for qg in range(NQG)]
                kv_list = []
                for (h, kch, vh) in heads:
                    if not kv_list or kv_list[-1] != (kch, vh):
                        kv_list.append((kch, vh))
                kv_idx = 0
                kvs = load_kv(*kv_list[0])
                nxt_kvs = load_kv(*kv_list[1]) if len(kv_list) > 1 else None
                q_next = load_q(items[0][0], items[0][3], items[0][0])
                for i, (h, kch, vh, qg) in enumerate(items):
                    if (kch, vh) != kv_list[kv_idx]:
                        kv_idx += 1
                        kvs = nxt_kvs
                        if kv_idx + 1 < len(kv_list):
                            nxt_kvs = load_kv(*kv_list[kv_idx + 1])
                    cur = q_next
                    if i + 1 < len(items):
                        q_next = load_q(items[i + 1][0], items[i + 1][3], items[i + 1][0])
                    if typ == "A":
                        a0 = attn_unit(kvs, cur, 0)
                        evac_A0(a0)
                        a1 = attn_unit(kvs, cur, 64)
                        evac_A1(a1, h, qg, cur)
                    else:
                        a0 = attn_unit(kvs, cur, 0)
                        evac_B(a0, h, qg, cur)

        def phase_O(layer, si):
            xin, yout, S = seqs[si]
            xsrc = xin if layer == 0 else yout
            NT = S // 128
            if True:
                XT = lb("XTo", [128, 2, 1024], F32)
                OGT = lb("OGT", [128, 2, 8, 128], BF16)
                M = lb("M", [128, 2, 1024], F32)
                ST = lb("STo", [128, 2, 8], F32)
                T.dma("sp", XT[:, 0, :], xsrc[0:128, :], writes=[("XT", 0)])
                for t in range(NT):
                    sl = t % 2
                    if t + 1 < NT:
                        T.dma("sp", XT[:, (t + 1) % 2, :], xsrc[(t + 1) * 128:(t + 2) * 128, :], writes=[("XT", (t + 1) % 2)])

                    def th(e):
                        for wc in range(8):
                            ins = e.transpose(out=TPS[:, wc * 128:(wc + 1) * 128], in_=OGv[:, t, wc * 128:(wc + 1) * 128], identity=IDENT[:])
                        return ins
                    T.op("pe", th, reads=[("IDENT",)] + [("OG", hh, t // 4) for hh in range(8)], writes=[("TPS",)])
                    T.op("act", lambda e: e.activation(out=OGT[:, sl, :, :], in_=TPS[:].rearrange("p (k t) -> p k t", k=8), func=AF.Copy),
                         reads=[("TPS",)], writes=[("OGT", sl)])
                    for ng in range(2):
                        ps = pj_slot()
                        def f(e, ng=ng, ps=ps):
                            for wc in range(8):
                                ins = e.matmul(PJ[:, ps, :], lhsT=OGT[:, sl, wc, :], rhs=WOUT[:, wc, ng * 512:(ng + 1) * 512],
                                               start=(wc == 0), stop=(wc == 7))
                            return ins
                        T.op("pe", f, reads=[("OGT", sl)], writes=[("PJ", ps)])
                        T.op("act", lambda e, ng=ng, ps=ps: e.activation(out=M[:, sl, ng * 512:(ng + 1) * 512], in_=PJ[:, ps, :], func=AF.Copy),
                             reads=[("PJ", ps)], writes=[("M", sl, ng)])
                        T.op("dve", lambda e, ng=ng: e.scalar_tensor_tensor(out=JK[:, 0:512], in0=M[:, sl, ng * 512:(ng + 1) * 512], scalar=1.0,
                                                                             in1=M[:, sl, ng * 512:(ng + 1) * 512], op0=ALU.mult, op1=ALU.mult,
                                                                             accum_out=ST[:, sl, ng:ng + 1]),
                             reads=[("M", sl, ng)], writes=[("ST", sl, ng), ("JK",)])
                    T.op("dve", lambda e: e.tensor_tensor(out=ST[:, sl, 2:3], in0=ST[:, sl, 0:1], in1=ST[:, sl, 1:2], op=ALU.add),
                         reads=[("ST", sl, 0), ("ST", sl, 1)], writes=[("ST", sl, 2)])
                    T.op("dve", lambda e: e.tensor_scalar(out=ST[:, sl, 3:4], in0=ST[:, sl, 2:3], scalar1=1.0 / D, scalar2=EPS,
                                                          op0=ALU.mult, op1=ALU.add), reads=[("ST", sl, 2)], writes=[("ST", sl, 3)])
                    T.op("pool", lambda e: e.tensor_tensor(out=ST[:, sl, 4:5], in0=ST[:, sl, 3:4], in1=NEGH[:, 0:1], op=ALU.pow),
                         reads=[("ST", sl, 3), ("NEGH",)], writes=[("ST", sl, 4)])
                    T.op("dve", lambda e: e.scalar_tensor_tensor(out=M[:, sl, :], in0=M[:, sl, :], scalar=ST[:, sl, 4:5], in1=NPOST[:],
                                                                 op0=ALU.mult, op1=ALU.mult),
                         reads=[("M", sl, 0), ("M", sl, 1), ("ST", sl, 4), ("NPOST",)], writes=[("M", sl, 0), ("M", sl, 1)])
                    T.op("pool", lambda e: e.tensor_tensor(out=M[:, sl, :], in0=M[:, sl, :], in1=XT[:, sl, :], op=ALU.add),
                         reads=[("M", sl, 0), ("M", sl, 1), ("XT", sl)], writes=[("M", sl, 0), ("M", sl, 1)])
                    T.dma("pool", yout[t * 128:(t + 1) * 128, :], M[:, sl, :], reads=[("M", sl, 0), ("M", sl, 1)])

        T.barrier()
        for layer in layers:
            layer_setup(layer)
            if "W" in phases:
                load_win(layer)
            if "P" in phases:
                with ExitStack() as sc:
                    scope["es"], scope["cache"] = sc, {}
                    for si in seq_ids:
                        phase_P(layer, si)
                    T.barrier()
            with ExitStack() as sc:
                scope["es"], scope["cache"] = sc, {}
                for si in seq_ids:
                    if "T" in phases:
                        phase_T(layer, si)
                    if "O" in phases:
                        phase_O(layer, si)
                T.barrier()
        T.final_wait("sp")
        T.final_wait("pool")
    return nc


def _tables():
    pos = np.arange(S_P, dtype=np.float32)
    inv_a = (np.float32(500000.0) ** (-(np.arange(0, 16, 2, dtype=np.float32) / np.float32(16)))).astype(np.float32)
    ang = pos[:, None] * inv_a[None, :]
    ca, sa = np.cos(ang).astype(np.float32), np.sin(ang).astype(np.float32)
    tab_a = np.concatenate([np.tile(ca, (1, 8)), np.tile(sa, (1, 8))], axis=1)
    inv_b = (np.float32(10000.0) ** (-(np.arange(0, 64, 2, dtype=np.float32) / np.float32(64)))).astype(np.float32)
    rows = np.floor(pos / 64).astype(np.float32)
    cols = (pos - rows * 64).astype(np.float32)
    ar, ac = rows[:, None] * inv_b[None, :], cols[:, None] * inv_b[None, :]
    cosb = np.concatenate([np.cos(ar), np.cos(ac)], axis=1).astype(np.float32)
    sinb = np.concatenate([np.sin(ar), np.sin(ac)], axis=1).astype(np.float32)
    tab_b = np.concatenate([np.tile(cosb, (1, 4)), np.tile(sinb, (1, 4))], axis=1)
    return np.ascontiguousarray(tab_a, dtype=np.float32), np.ascontiguousarray(tab_b, dtype=np.float32)


_NC_CACHE = {}


def kernel(x_prompt, x_sample, norm_pre, norm_post, a_w_in, a_w_out, a_lam, a_subln,
           b_w_in, b_w_out, b_q_norm, b_k_norm):
    f = lambda a: np.ascontiguousarray(np.asarray(a), dtype=np.float32)
    x_prompt, x_sample = f(x_prompt), f(x_sample)
    shared = {
        "norm_pre": f(norm_pre), "norm_post": f(norm_post), "a_w_in": f(a_w_in), "a_w_out": f(a_w_out),
        "a_lam": f(a_lam), "a_subln": f(a_subln), "b_w_in": f(b_w_in), "b_w_out": f(b_w_out),
        "b_q_norm": f(b_q_norm), "b_k_norm": f(b_k_norm),
    }
    tab_a, tab_b = _tables()
    shared["tab_a"] = tab_a
    shared["tab_b"] = tab_b
    shared["ident"] = np.eye(128, dtype=np.float32).astype(ml_dtypes.bfloat16)
    if "nc" not in _NC_CACHE:
        _NC_CACHE["nc"] = build_nc()
    nc = _NC_CACHE["nc"]
    in_maps = []
    for c in range(NCORES):
        m = dict(shared)
        m["xp"] = x_prompt[2 * c:2 * c + 2]
        m["xs"] = x_sample[c:c + 1]
        in_maps.append(m)
    res = run_bass_kernel_spmd(nc, in_maps, core_ids=list(range(NCORES)))
    y_prompt = np.concatenate([np.asarray(r["yp"], dtype=np.float32) for r in res.results], axis=0)
    y_sample = np.concatenate([np.asarray(r["ys"], dtype=np.float32) for r in res.results], axis=0)
    return (y_prompt, y_sample)
```

```python
import math
from contextlib import ExitStack

import numpy as np
import ml_dtypes

import concourse.bass as bass
import concourse.mybir as mybir
from concourse.bass_utils import run_bass_kernel_spmd

F32 = mybir.dt.float32
BF16 = mybir.dt.bfloat16
AF = mybir.ActivationFunctionType
ALU = mybir.AluOpType
AX = mybir.AxisListType

D = 1024
DEPTH = 4
S_P, S_S = 4096, 2048
NCORES = 8
EPS = 1e-6
A_IN, B_IN = 4096, 2560
NTMAX = S_P // 128


class Tracker:
    COMPUTE = ("pe", "act", "dve", "pool")

    def __init__(self, nc, es, n_dma_sems=12):
        self.nc = nc
        self.eng = {"pe": nc.tensor, "act": nc.scalar, "dve": nc.vector, "pool": nc.gpsimd, "sp": nc.sync}
        self.sem = {}
        self.cnt = {}
        for e in self.COMPUTE:
            self.sem[e] = es.enter_context(nc.semaphore("s_" + e))
            self.cnt[e] = 0
        self.rings = {}
        for q in ("sp", "pool"):
            names = []
            for i in range(n_dma_sems):
                k = "d_%s%d" % (q, i)
                self.sem[k] = es.enter_context(nc.semaphore(k))
                self.cnt[k] = 0
                names.append(k)
            self.rings[q] = [names, 0]
        self.seen = {e: {} for e in self.eng}
        self.last_w = {}
        self.readers = {}

    def _wait(self, e, ticket):
        if ticket is None:
            return
        k, v = ticket
        if e == "pe" and k == "pe":
            return
        if self.seen[e].get(k, 0) >= v:
            return
        self.eng[e].wait_ge(self.sem[k], v)
        self.seen[e][k] = v

    def _deps(self, e, reads, writes):
        for r in reads:
            self._wait(e, self.last_w.get(r))
        for w in writes:
            self._wait(e, self.last_w.get(w))
            for t in self.readers.get(w, ()):
                self._wait(e, t)

    def _record(self, ticket, reads, writes):
        for w in writes:
            self.last_w[w] = ticket
            self.readers[w] = []
        for r in reads:
            self.readers.setdefault(r, []).append(ticket)

    def op(self, e, fn, reads=(), writes=(), tag=None):
        self._deps(e, reads, writes)
        ins = fn(self.eng[e])
        self.cnt[e] += 1
        ins.then_inc(self.sem[e], 1)
        t = (e, self.cnt[e])
        self._record(t, reads, writes)
        return t

    def dma(self, q, out, in_, reads=(), writes=(), tag=None):
        names, idx = self.rings[q]
        k = names[idx % len(names)]
        self.rings[q][1] = idx + 1
        if self.cnt[k] > 0:
            self._wait(q, (k, self.cnt[k]))
        self._deps(q, reads, writes)
        self.eng[q].dma_start(out=out, in_=in_).then_inc(self.sem[k], 16)
        self.cnt[k] += 16
        t = (k, self.cnt[k])
        self._record(t, reads, writes)
        return t

    def barrier(self):
        for e in self.eng:
            for k, v in self.cnt.items():
                if v > 0:
                    self._wait(e, (k, v))
        self.last_w.clear()
        self.readers.clear()

    def final_wait(self, e="sp"):
        for k, v in self.cnt.items():
            if v > 0:
                self._wait(e, (k, v))


def build_nc(layers=(0, 1, 2, 3), seq_ids=(0, 1, 2), phases="WPTO"):
    nc = bass.Bass("TRN2", target_bir_lowering=False)
    dt_in = lambda n, s, d=F32: nc.dram_tensor(n, s, d, kind="ExternalInput").ap()
    xp = dt_in("xp", [2, S_P, D])
    xs = dt_in("xs", [1, S_S, D])
    norm_pre = dt_in("norm_pre", [DEPTH, D])
    norm_post = dt_in("norm_post", [DEPTH, D])
    a_w_in = dt_in("a_w_in", [2, D, A_IN])
    a_w_out = dt_in("a_w_out", [2, D, D])
    a_lam = dt_in("a_lam", [2, 4, 64])
    a_subln = dt_in("a_subln", [2, 128])
    b_w_in = dt_in("b_w_in", [2, D, B_IN])
    b_w_out = dt_in("b_w_out", [2, D, D])
    b_q_norm = dt_in("b_q_norm", [2, 128])
    b_k_norm = dt_in("b_k_norm", [2, 128])
    tab_a = dt_in("tab_a", [S_P, 128])
    tab_b = dt_in("tab_b", [S_P, 512])
    ident_d = dt_in("ident", [128, 128], BF16)
    yp = nc.dram_tensor("yp", [2, S_P, D], F32, kind="ExternalOutput").ap()
    ys = nc.dram_tensor("ys", [1, S_S, D], F32, kind="ExternalOutput").ap()
    QT_all = nc.dram_tensor("QT_d", [3, 1024, S_P], BF16, kind="Internal").ap()
    KT_all = nc.dram_tensor("KT_d", [3, 1024, S_P], BF16, kind="Internal").ap()
    VX_all = nc.dram_tensor("VX_d", [3, 8, 128, NTMAX, 129], BF16, kind="Internal").ap()
    G_all = nc.dram_tensor("G_d", [3, 8, 128, NTMAX, 128], F32, kind="Internal").ap()

    seqs = [(xp[0], yp[0], S_P), (xp[1], yp[1], S_P), (xs[0], ys[0], S_S)]

    with ExitStack() as es:
        uniq = [0]

        def sb(n, s, d, st=es):
            uniq[0] += 1
            return st.enter_context(nc.sbuf_tensor("%s_%d" % (n, uniq[0]), s, d))
        T = Tracker(nc, es)

        BIG = sb("BIG", [128, 32768], BF16)
        WOUT = sb("WOUT", [128, 8, 1024], BF16)
        IDENT = sb("IDENT", [128, 128], BF16)
        NPRE = sb("NPRE", [128, 1024], F32)
        NPOST = sb("NPOST", [128, 1024], F32)
        SMW = sb("SMW", [128, 3, 128], F32)
        LAMB = sb("LAMB", [128, 4, 64], F32)
        LAMP = sb("LAMP", [128, 2, 64], F32)
        LAMS = sb("LAMS", [128, 8], F32)
        NEGH = sb("NEGH", [128, 4], F32)
        JK = sb("JK", [128, 1024], BF16)
        JKF = sb("JKF", [128, 128], F32)
        TPS = es.enter_context(nc.psum_tensor("TPS", [128, 1024], BF16))
        PJ = es.enter_context(nc.psum_tensor("PJ", [128, 3, 512], F32))
        ACC = es.enter_context(nc.psum_tensor("ACC", [128, 4, 512], F32))

        TPS2 = ACC[:, 3, :].bitcast(BF16)
        WINv = BIG[:].rearrange("p (k n) -> p k n", k=8)
        OGv = BIG[:].rearrange("p (t w) -> p t w", w=1024)

        T.dma("sp", IDENT[:], ident_d, writes=[("IDENT",)])
        T.op("dve", lambda e: e.memset(NEGH[:], -0.5), writes=[("NEGH",)])

        cnt = {"pj": 0, "pt": 0, "acc": 0}
        scope = {"es": None, "cache": {}}

        def lb(name, shape, dtype):
            if name not in scope["cache"]:
                scope["cache"][name] = sb(name, shape, dtype, scope["es"])
            return scope["cache"][name]

        def first(flag):
            if flag in scope["cache"]:
                return False
            scope["cache"][flag] = True
            return True

        def pj_slot():
            s = cnt["pj"] % 3
            cnt["pj"] += 1
            return s

        def load_weight(dst3, src, ncols, chunk, STG, key):
            i = 0
            for kc in range(8):
                for c0 in range(0, ncols, chunk):
                    w = min(chunk, ncols - c0)
                    sl = i % 2
                    T.dma("sp", STG[:, sl, 0:w], src[kc * 128:(kc + 1) * 128, c0:c0 + w], writes=[("STG", sl)])
                    eng = "dve" if i % 2 == 0 else "pool"
                    T.op(eng, lambda e, kc=kc, c0=c0, w=w, sl=sl: e.tensor_copy(out=dst3[:, kc, c0:c0 + w], in_=STG[:, sl, 0:w]),
                         reads=[("STG", sl)], writes=[(key, kc, c0)])
                    i += 1

        def layer_setup(layer):
            typ = "A" if layer % 2 == 0 else "B"
            j = layer // 2
            with ExitStack() as ls:
                STG = sb("STGo", [128, 2, 1024], F32, ls)
                load_weight(WOUT, (a_w_out if typ == "A" else b_w_out)[j], 1024, 1024, STG, "WOUT")
                T.dma("sp", NPRE[:], norm_pre[layer:layer + 1, :].partition_broadcast(128), writes=[("NPRE",)])
                T.dma("sp", NPOST[:], norm_post[layer:layer + 1, :].partition_broadcast(128), writes=[("NPOST",)])
                if typ == "A":
                    lam_init = 0.8 - 0.6 * math.exp(-0.3 * layer)
                    T.dma("sp", SMW[:, 2, :], a_subln[j:j + 1, :].partition_broadcast(128), writes=[("SMW", 2)])
                    T.op("dve", lambda e: e.tensor_scalar(out=SMW[:, 0, :], in0=SMW[:, 2, :], scalar1=float(1.0 - lam_init),
                                                          scalar2=None, op0=ALU.mult), reads=[("SMW", 2)], writes=[("SMW", 0)])
                    T.dma("sp", LAMB[:].rearrange("p a b -> p (a b)"),
                          a_lam[j:j + 1].rearrange("o a b -> o (a b)").partition_broadcast(128), writes=[("LAMB",)])
                    lv = LAMB[:].rearrange("p (a b) c -> p a b c", b=2)
                    T.op("dve", lambda e: e.tensor_tensor(out=LAMP[:], in0=lv[:, :, 0, :], in1=lv[:, :, 1, :], op=ALU.mult),
                         reads=[("LAMB",)], writes=[("LAMP",)])
                    T.op("dve", lambda e: e.tensor_reduce(out=LAMS[:, 0:2], in_=LAMP[:], axis=AX.X, op=ALU.add),
                         reads=[("LAMP",)], writes=[("LAMS", 0)])
                    T.op("act", lambda e: e.activation(out=LAMS[:, 2:4], in_=LAMS[:, 0:2], func=AF.Exp),
                         reads=[("LAMS", 0)], writes=[("LAMS", 2)])
                    T.op("dve", lambda e: e.scalar_tensor_tensor(out=LAMS[:, 4:5], in0=LAMS[:, 3:4], scalar=float(-lam_init),
                                                                 in1=LAMS[:, 2:3], op0=ALU.add, op1=ALU.subtract),
                         reads=[("LAMS", 2)], writes=[("LAMS", 4)])
                else:
                    T.dma("sp", SMW[:, 0, :], b_q_norm[j:j + 1, :].partition_broadcast(128), writes=[("SMW", 0)])
                    T.dma("sp", SMW[:, 1, :], b_k_norm[j:j + 1, :].partition_broadcast(128), writes=[("SMW", 1)])
                T.barrier()

        def load_win(layer):
            typ = "A" if layer % 2 == 0 else "B"
            j = layer // 2
            with ExitStack() as ls:
                STG = sb("STGi", [128, 2, 2048], F32, ls)
                if typ == "A":
                    load_weight(WINv, a_w_in[j], A_IN, 2048, STG, "WIN")
                else:
                    load_weight(WINv, b_w_in[j], B_IN, 1280, STG, "WIN")
                T.barrier()

        def phase_P(layer, si):
            typ = "A" if layer % 2 == 0 else "B"
            xin, yout, S = seqs[si]
            QT_d, KT_d, VX_d, G_d = QT_all[si], KT_all[si], VX_all[si], G_all[si]
            xsrc = xin if layer == 0 else yout
            NT = S // 128
            NG = 8 if typ == "A" else 5
            tab_d = tab_a if typ == "A" else tab_b
            tabw = 128 if typ == "A" else 512
            nkc = 8 if typ == "A" else 2
            nvh = 8 if typ == "A" else 2
            if True:
                XT = lb("XT", [128, 3, 1024], F32)
                TAB = lb("TAB", [128, 3, 512], F32)
                ST = lb("ST", [128, 3, 16], F32)
                H = lb("H", [128, 2, 1024], BF16)
                HT = lb("HT", [128, 2, 8, 128], BF16)
                QKF = lb("QKF", [128, 8, 512], BF16)
                SQ = lb("SQ", [128, 3, 512], F32)
                QN = lb("QN", [128, 3, 512], F32)
                RT = lb("RT", [128, 3, 4, 256], F32)
                ST2 = lb("ST2", [128, 3, 12], F32)
                VXS = lb("VXS", [128, 2, 8, 2, 129], BF16)
                GS = lb("GS", [128, 2, 1024], F32)
                QTS = lb("QTS", [128, 2, 8, 256], BF16)
                KTS = lb("KTS", [128, 2, 8, 256], BF16)

                if first("vxs_ones"):
                    for sl in range(2):
                        T.op("pool", lambda e, sl=sl: e.memset(VXS[:, sl, :, :, 128:129], 1.0), writes=[("VXS", sl, "ones")])

                qc = {"qkf": 0, "rt": 0, "tq": 0}

                def load_x(t):
                    T.dma("sp", XT[:, t % 3, :], xsrc[t * 128:(t + 1) * 128, :], writes=[("XT", t % 3)])

                def load_tab(t):
                    T.dma("sp", TAB[:, t % 3, 0:tabw], tab_d[t * 128:(t + 1) * 128, :], writes=[("TAB", t % 3)])

                def head_stats(t):
                    x3, hs = t % 3, t % 2
                    T.op("act", lambda e: e.activation(out=JK[:], in_=XT[:, x3, :], func=AF.Square, accum_out=ST[:, x3, 0:1]),
                         reads=[("XT", x3)], writes=[("ST", x3, 0), ("JK",)])
                    T.op("dve", lambda e: e.tensor_scalar(out=ST[:, x3, 1:2], in0=ST[:, x3, 0:1], scalar1=1.0 / D, scalar2=EPS,
                                                          op0=ALU.mult, op1=ALU.add), reads=[("ST", x3, 0)], writes=[("ST", x3, 1)])
                    T.op("pool", lambda e: e.tensor_tensor(out=ST[:, x3, 2:3], in0=ST[:, x3, 1:2], in1=NEGH[:, 0:1], op=ALU.pow),
                         reads=[("ST", x3, 1), ("NEGH",)], writes=[("ST", x3, 2)])
                    T.op("dve", lambda e: e.scalar_tensor_tensor(out=H[:, hs, :], in0=XT[:, x3, :], scalar=ST[:, x3, 2:3],
                                                                 in1=NPRE[:], op0=ALU.mult, op1=ALU.mult),
                         reads=[("XT", x3), ("ST", x3, 2), ("NPRE",)], writes=[("H", hs)])

                def head_tr(t):
                    hs = t % 2
                    def th(e):
                        for kc in range(8):
                            ins = e.transpose(out=TPS[:, kc * 128:(kc + 1) * 128], in_=H[:, hs, kc * 128:(kc + 1) * 128], identity=IDENT[:])
                        return ins
                    T.op("pe", th, reads=[("H", hs), ("IDENT",)], writes=[("TPS", 0), ("TPS", 1)])
                    T.op("act", lambda e: e.activation(out=HT[:, hs, :, :], in_=TPS[:].rearrange("p (k t) -> p k t", k=8), func=AF.Copy),
                         reads=[("TPS", 0), ("TPS", 1)], writes=[("HT", hs)])

                def body(t):
                    xsl = t % 2
                    tsl = t % 3
                    tt = t % 2
                    sts = (t // 2) % 2
                    pend_tq = []

                    def mm_group(ng):
                        ps = pj_slot()
                        def f(e):
                            for kc in range(8):
                                ins = e.matmul(PJ[:, ps, :], lhsT=HT[:, xsl, kc, :], rhs=WINv[:, kc, ng * 512:(ng + 1) * 512],
                                               start=(kc == 0), stop=(kc == 7))
                            return ins
                        T.op("pe", f, reads=[("HT", xsl)], writes=[("PJ", ps)])
                        return ps

                    def emit_tq(qs, nch, dst, c0):
                        hq = qc["tq"] % 2
                        qc["tq"] += 1
                        tbuf = TPS if hq == 0 else TPS2
                        def f(e):
                            for c in range(nch):
                                ins = e.transpose(out=tbuf[:, c * 128:(c + 1) * 128],
                                                  in_=QKF[:, qs, c * 128:(c + 1) * 128], identity=IDENT[:])
                            return ins
                        T.op("pe", f, reads=[("QKF", qs, "a"), ("QKF", qs, "b"), ("QKF", qs, "c"), ("IDENT",)], writes=[("TPS", hq)])
                        dkey = "QTS" if dst is QTS else "KTS"
                        T.op("act", lambda e: e.activation(out=dst[:, sts, c0:c0 + nch, tt * 128:(tt + 1) * 128],
                                                           in_=tbuf[:, 0:nch * 128].rearrange("p (c t) -> p c t", c=nch),
                                                           func=AF.Copy),
                             reads=[("TPS", hq)], writes=[(dkey, sts, c0, tt)])

                    def post_A_qk(ng, ps):
                        qs = qc["qkf"] % 8
                        qc["qkf"] += 1
                        r = qc["rt"] % 3
                        qc["rt"] += 1
                        pjv = QN[:, r, :].rearrange("p (h d) -> p h d", d=64)
                        qfv = QKF[:, qs, :].rearrange("p (h d) -> p h d", d=64)
                        T.op("act", lambda e: e.activation(out=QN[:, r, :], in_=PJ[:, ps, :], func=AF.Copy),
                             reads=[("PJ", ps)], writes=[("QN", r)])
                        T.op("pool", lambda e: e.tensor_copy(out=qfv[:, :, 16:64], in_=pjv[:, :, 16:64]),
                             reads=[("QN", r)], writes=[("QKF", qs, "a")])
                        cos = TAB[:, tsl, 0:64].rearrange("p (h j) -> p h j", j=8)
                        sin = TAB[:, tsl, 64:128].rearrange("p (h j) -> p h j", j=8)
                        x1, x2 = pjv[:, :, 0:8], pjv[:, :, 8:16]
                        tv = [RT[:, r, k, 0:64].rearrange("p (h j) -> p h j", j=8) for k in range(4)]
                        for k, (xa, tb) in enumerate(((x1, cos), (x2, sin), (x2, cos), (x1, sin))):
                            T.op("dve", lambda e, k=k, xa=xa, tb=tb: e.tensor_tensor(out=tv[k], in0=xa, in1=tb, op=ALU.mult),
                                 reads=[("QN", r), ("TAB", tsl)], writes=[("RT", r, k)])
                        T.op("dve", lambda e: e.tensor_tensor(out=qfv[:, :, 0:8], in0=tv[0], in1=tv[1], op=ALU.subtract),
                             reads=[("RT", r, 0), ("RT", r, 1)], writes=[("QKF", qs, "b")])
                        T.op("dve", lambda e: e.tensor_tensor(out=qfv[:, :, 8:16], in0=tv[2], in1=tv[3], op=ALU.add),
                             reads=[("RT", r, 2), ("RT", r, 3)], writes=[("QKF", qs, "c")])
                        if ng < 2:
                            pend_tq.append((qs, 4, QTS, ng * 4))
                        else:
                            pend_tq.append((qs, 4, KTS, (ng - 2) * 4))

                    def post_B_qk(ps, col0, nh, wrow, dst, c0):
                        qs = qc["qkf"] % 8
                        qc["qkf"] += 1
                        r = qc["rt"] % 3
                        qc["rt"] += 1
                        n = nh * 128
                        qnk = [("QN", r, h) for h in range(nh)]
                        T.op("act", lambda e: e.activation(out=QN[:, r, 0:n], in_=PJ[:, ps, col0:col0 + n], func=AF.Copy),
                             reads=[("PJ", ps)], writes=[("QN", r, h) for h in range(4)])
                        T.op("act", lambda e: e.activation(out=SQ[:, r, 0:n], in_=QN[:, r, 0:n], func=AF.Square),
                             reads=qnk, writes=[("SQ", r)])
                        T.op("dve", lambda e: e.tensor_reduce(out=ST2[:, r, 0:nh], in_=SQ[:, r, 0:n].rearrange("p (h d) -> p h d", d=128),
                                                              axis=AX.X, op=ALU.add), reads=[("SQ", r)], writes=[("ST2", r, 0)])
                        T.op("dve", lambda e: e.tensor_scalar(out=ST2[:, r, 4:4 + nh], in0=ST2[:, r, 0:nh], scalar1=1.0 / 128, scalar2=EPS,
                                                              op0=ALU.mult, op1=ALU.add), reads=[("ST2", r, 0)], writes=[("ST2", r, 4)])
                        T.op("pool", lambda e: e.tensor_tensor(out=ST2[:, r, 8:8 + nh], in0=ST2[:, r, 4:4 + nh],
                                                               in1=NEGH[:, 0:nh], op=ALU.pow),
                             reads=[("ST2", r, 4), ("NEGH",)], writes=[("ST2", r, 8)])
                        for h in range(nh):
                            T.op("dve", lambda e, h=h: e.scalar_tensor_tensor(
                                out=QN[:, r, h * 128:(h + 1) * 128], in0=QN[:, r, h * 128:(h + 1) * 128],
                                scalar=ST2[:, r, 8 + h:9 + h], in1=SMW[:, wrow, :], op0=ALU.mult, op1=ALU.mult),
                                reads=[("QN", r, h), ("ST2", r, 8), ("SMW", wrow)], writes=[("QN", r, h)])
                        qnv = QN[:, r, 0:n].rearrange("p (h a f j) -> p h a f j", a=2, f=2, j=32)
                        qfv = QKF[:, qs, 0:n].rearrange("p (h a f j) -> p h a f j", a=2, f=2, j=32)
                        cos = TAB[:, tsl, 0:256].rearrange("p (h a j) -> p h a j", a=2, j=32)[:, 0:nh]
                        sin = TAB[:, tsl, 256:512].rearrange("p (h a j) -> p h a j", a=2, j=32)[:, 0:nh]
                        x1, x2 = qnv[:, :, :, 0, :], qnv[:, :, :, 1, :]
                        tv = [RT[:, r, k, 0:nh * 64].rearrange("p (h a j) -> p h a j", a=2, j=32) for k in range(4)]
                        for k, (xa, tb) in enumerate(((x1, cos), (x2, sin), (x2, cos), (x1, sin))):
                            T.op("dve", lambda e, k=k, xa=xa, tb=tb: e.tensor_tensor(out=tv[k], in0=xa, in1=tb, op=ALU.mult),
                                 reads=qnk + [("TAB", tsl)], writes=[("RT", r, k)])
                        T.op("dve", lambda e: e.tensor_tensor(out=qfv[:, :, :, 0, :], in0=tv[0], in1=tv[1], op=ALU.subtract),
                             reads=[("RT", r, 0), ("RT", r, 1)], writes=[("QKF", qs, "b")])
                        T.op("dve", lambda e: e.tensor_tensor(out=qfv[:, :, :, 1, :], in0=tv[2], in1=tv[3], op=ALU.add),
                             reads=[("RT", r, 2), ("RT", r, 3)], writes=[("QKF", qs, "c")])
                        pend_tq.append((qs, nh, dst, c0))

                    def post_v(ps, col0, nh, h0):
                        T.op("act", lambda e: e.activation(out=VXS[:, sts, h0:h0 + nh, tt, 0:128],
                                                           in_=PJ[:, ps, col0:col0 + nh * 128].rearrange("p (h e) -> p h e", e=128),
                                                           func=AF.Copy),
                             reads=[("PJ", ps)], writes=[("VXS", sts, tt, h0)])

                    def post_gate(ps, gc):
                        T.op("act", lambda e: e.activation(out=GS[:, xsl, gc * 512:(gc + 1) * 512], in_=PJ[:, ps, :], func=AF.Silu),
                             reads=[("PJ", ps)], writes=[("GS", xsl, gc)])

                    for ng in range(NG):
                        ps = mm_group(ng)
                        if typ == "A":
                            if ng < 4:
                                post_A_qk(ng, ps)
                            elif ng < 6:
                                post_v(ps, 0, 4, (ng - 4) * 4)
                            else:
                                post_gate(ps, ng - 6)
                        else:
                            if ng < 2:
                                post_B_qk(ps, 0, 4, 0, QTS, ng * 4)
                            elif ng == 2:
                                post_B_qk(ps, 0, 2, 1, KTS, 0)
                                post_v(ps, 256, 2, 0)
                            else:
                                post_gate(ps, ng - 3)
                    def emit_all_tq():
                        while pend_tq:
                            emit_tq(*pend_tq.pop(0))

                    def stores():
                        T.dma("pool", G_d[:, :, t, :].rearrange("h p e -> p h e"), GS[:, xsl, :].rearrange("p (h e) -> p h e", e=128),
                              reads=[("GS", xsl, 0), ("GS", xsl, 1)], tag="g")
                        if tt == 1:
                            t0 = (t - 1) * 128
                            T.dma("pool", QT_d.rearrange("(c p) s -> p c s", p=128)[:, :, t0:t0 + 256], QTS[:, sts, :, :],
                                  reads=[("QTS", sts, c0, k) for c0 in (0, 4) for k in (0, 1)], tag="q")
                            kreads = [("KTS", sts, c0, k) for c0 in ((0, 4) if typ == "A" else (0,)) for k in (0, 1)]
                            T.dma("pool", KT_d.rearrange("(c p) s -> p c s", p=128)[:, 0:nkc, t0:t0 + 256], KTS[:, sts, 0:nkc, :], reads=kreads, tag="k")
                            vreads = [("VXS", sts, "ones")] + [("VXS", sts, k, h0) for k in (0, 1) for h0 in ((0, 4) if typ == "A" else (0,))]
                            T.dma("pool", VX_d[0:nvh, :, t - 1:t + 1, :].rearrange("h p n e -> p h (n e)"),
                                  VXS[:, sts, 0:nvh, :, :].rearrange("p h n e -> p h (n e)"), reads=vreads, tag="v")

                    return lambda: (emit_all_tq(), stores())

                for t0_ in range(min(3, NT)):
                    load_x(t0_)
                load_tab(0)
                head_stats(0)
                if NT > 1:
                    head_stats(1)
                head_tr(0)
                prev_tail = None
                for t in range(NT):
                    if t + 3 < NT:
                        load_x(t + 3)
                    if t + 1 < NT:
                        load_tab(t + 1)
                    if t + 2 < NT:
                        head_stats(t + 2)
                    tl = body(t)
                    if t + 1 < NT:
                        head_tr(t + 1)
                    if prev_tail is not None:
                        prev_tail()
                    prev_tail = tl
                prev_tail()

        def phase_T(layer, si):
            typ = "A" if layer % 2 == 0 else "B"
            _, _, S = seqs[si]
            QT_d, KT_d, VX_d, G_d = QT_all[si], KT_all[si], VX_all[si], G_all[si]
            NT = S // 128
            NKB = NT
            NQG = S // 512
            dk = 64 if typ == "A" else 128
            scale = float(dk) ** -0.5
            if True:
                KTB = lb("KTB", [128, 2, 2, S_P], BF16)
                VXB = lb("VXB", [128, 2, NTMAX, 129], BF16)
                QTB = lb("QTB", [128, 2, 512], BF16)
                GB = lb("GB", [128, 2, 4, 128], F32)
                PT = lb("PT", [128, 3, 512], BF16)
                O1 = lb("O1", [128, 4, 128], F32)
                DD = lb("DD", [128, 4, 128], F32)
                TMP = lb("TMP", [128, 2, 4, 128], F32)
                RZ = lb("RZ", [128, 2, 24], F32)

                ctr = {"kv": 0, "q": 0, "ev": 0}
                if typ == "A" and first("ktb_zero"):
                    for sl in range(2):
                        T.op("dve", lambda e, sl=sl: e.memset(KTB[64:128, sl, 0, :], 0.0), writes=[("KTB", sl, 0)])
                        T.op("pool", lambda e, sl=sl: e.memset(KTB[0:64, sl, 1, :], 0.0), writes=[("KTB", sl, 1)])

                def acc_ap(aset, qt):
                    b, jj = (2 * aset, qt) if qt < 3 else (2 * aset + 1, 0)
                    return ACC[:, b, jj * 129:(jj + 1) * 129]

                def load_kv(kchunk, vhead):
                    sl = ctr["kv"] % 2
                    ctr["kv"] += 1
                    if typ == "A":
                        T.dma("sp", KTB[0:64, sl, 0, 0:S], KT_d[kchunk * 128:kchunk * 128 + 64, 0:S], writes=[("KTB", sl, 0)])
                        T.dma("sp", KTB[64:128, sl, 1, 0:S], KT_d[kchunk * 128 + 64:(kchunk + 1) * 128, 0:S], writes=[("KTB", sl, 1)])
                    else:
                        T.dma("sp", KTB[:, sl, 0, 0:S], KT_d[kchunk * 128:(kchunk + 1) * 128, 0:S], writes=[("KTB", sl, 0)])
                    T.dma("sp", VXB[:, sl, 0:NT, :], VX_d[vhead, :, 0:NT, :], writes=[("VXB", sl)])
                    return sl

                def load_q(qchunk, qg, h):
                    sl = ctr["q"] % 2
                    ctr["q"] += 1
                    T.dma("sp", QTB[:, sl, :], QT_d[qchunk * 128:(qchunk + 1) * 128, qg * 512:(qg + 1) * 512], writes=[("QTB", sl)])
                    T.dma("sp", GB[:, sl, :, :], G_d[h, :, qg * 4:(qg + 1) * 4, :], writes=[("GB", sl)])
                    return sl

                def attn_unit(kvs, qs, p0):
                    aset = cnt["acc"] % 2
                    cnt["acc"] += 1
                    slots = {}

                    def qk(kb):
                        ps = pj_slot()
                        slots[kb] = ps
                        mp = p0 // 64
                        T.op("pe", lambda e: e.matmul(PJ[:, ps, :], lhsT=KTB[:, kvs, mp, kb * 128:(kb + 1) * 128],
                                                      rhs=QTB[:, qs, :], start=True, stop=True),
                             reads=[("KTB", kvs, mp), ("QTB", qs)], writes=[("PJ", ps)])
                        pts = cnt["pt"] % 3
                        cnt["pt"] += 1
                        T.op("act", lambda e: e.activation(out=PT[:, pts, :], in_=PJ[:, ps, :], func=AF.Exp, scale=scale),
                             reads=[("PJ", ps)], writes=[("PT", pts)])
                        slots[kb] = pts

                    def pv(kb):
                        pts = slots.pop(kb)
                        def f(e):
                            for qt in range(4):
                                ins = e.matmul(acc_ap(aset, qt), lhsT=PT[:, pts, qt * 128:(qt + 1) * 128], rhs=VXB[:, kvs, kb, :],
                                               start=(kb == 0 and qt in (0, 3)), stop=(kb == NKB - 1), skip_group_check=True)
                            return ins
                        T.op("pe", f, reads=[("PT", pts), ("VXB", kvs)], writes=[("ACC", aset)])

                    qk(0)
                    qk(1)
                    for kb in range(NKB):
                        if kb + 2 < NKB:
                            qk(kb + 2)
                        pv(kb)
                    return aset

                def recip_z(aset, r):
                    zv = ACC[:, 2 * aset, 0:387].rearrange("p (a b) -> p a b", b=129)[:, :, 128:129]
                    T.op("dve", lambda e: e.reciprocal(out=RZ[:, r, 0:3].rearrange("p (a b) -> p a b", b=1), in_=zv),
                         reads=[("ACC", aset)], writes=[("RZ", r, 0)])
                    T.op("dve", lambda e: e.reciprocal(out=RZ[:, r, 3:4], in_=ACC[:, 2 * aset + 1, 128:129]),
                         reads=[("ACC", aset)], writes=[("RZ", r, 3)])

                def og_out(h, qg, r, gsl):
                    ogv = OGv[:, qg * 4:(qg + 1) * 4, h * 128:(h + 1) * 128]
                    T.op("pool", lambda e: e.tensor_tensor(out=ogv, in0=TMP[:, r, :, :], in1=GB[:, gsl, :, :], op=ALU.mult),
                         reads=[("TMP", r, q) for q in range(4)] + [("GB", gsl)], writes=[("OG", h, qg)])

                def evac_B(aset, h, qg, gsl):
                    r = ctr["ev"] % 2
                    ctr["ev"] += 1
                    recip_z(aset, r)
                    for qt in range(4):
                        T.op("dve", lambda e, qt=qt: e.tensor_scalar(out=TMP[:, r, qt, :], in0=acc_ap(aset, qt)[:, 0:128],
                                                                      scalar1=RZ[:, r, qt:qt + 1], scalar2=None, op0=ALU.mult),
                             reads=[("ACC", aset), ("RZ", r, 0), ("RZ", r, 3)], writes=[("TMP", r, qt)])
                    og_out(h, qg, r, gsl)

                def evac_A0(aset):
                    r = ctr["ev"] % 2
                    ctr["ev"] += 1
                    recip_z(aset, r)
                    for qt in range(4):
                        T.op("dve", lambda e, qt=qt: e.tensor_scalar(out=O1[:, qt, :], in0=acc_ap(aset, qt)[:, 0:128],
                                                                      scalar1=RZ[:, r, qt:qt + 1], scalar2=None, op0=ALU.mult),
                             reads=[("ACC", aset), ("RZ", r, 0), ("RZ", r, 3)], writes=[("O1", qt)])

                def evac_A1(aset, h, qg, gsl):
                    r = ctr["ev"] % 2
                    ctr["ev"] += 1
                    recip_z(aset, r)
                    T.op("dve", lambda e: e.tensor_scalar(out=RZ[:, r, 4:8], in0=RZ[:, r, 0:4], scalar1=LAMS[:, 4:5], scalar2=None, op0=ALU.mult),
                         reads=[("RZ", r, 0), ("RZ", r, 3), ("LAMS", 4)], writes=[("RZ", r, 4)])
                    for qt in range(4):
                        T.op("dve", lambda e, qt=qt: e.scalar_tensor_tensor(out=DD[:, qt, :], in0=acc_ap(aset, qt)[:, 0:128],
                                                                             scalar=RZ[:, r, 4 + qt:5 + qt], in1=O1[:, qt, :],
                                                                             op0=ALU.mult, op1=ALU.add),
                             reads=[("ACC", aset), ("RZ", r, 4), ("O1", qt)], writes=[("DD", qt)])
                    for qt in range(4):
                        T.op("dve", lambda e, qt=qt: e.scalar_tensor_tensor(out=JKF[:], in0=DD[:, qt, :], scalar=1.0, in1=DD[:, qt, :],
                                                                             op0=ALU.mult, op1=ALU.mult, accum_out=RZ[:, r, 8 + qt:9 + qt]),
                             reads=[("DD", qt)], writes=[("RZ", r, 8 + qt), ("JKF",)])
                    T.op("dve", lambda e: e.tensor_scalar(out=RZ[:, r, 12:16], in0=RZ[:, r, 8:12], scalar1=1.0 / 128, scalar2=EPS,
                                                          op0=ALU.mult, op1=ALU.add),
                         reads=[("RZ", r, 8 + q) for q in range(4)], writes=[("RZ", r, 12)])
                    T.op("pool", lambda e: e.tensor_tensor(out=RZ[:, r, 16:20], in0=RZ[:, r, 12:16], in1=NEGH[:, 0:4],
                                                           op=ALU.pow), reads=[("RZ", r, 12), ("NEGH",)], writes=[("RZ", r, 16)])
                    for qt in range(4):
                        T.op("dve", lambda e, qt=qt: e.scalar_tensor_tensor(out=TMP[:, r, qt, :], in0=DD[:, qt, :],
                                                                             scalar=RZ[:, r, 16 + qt:17 + qt], in1=SMW[:, 0, :],
                                                                             op0=ALU.mult, op1=ALU.mult),
                             reads=[("DD", qt), ("RZ", r, 16), ("SMW", 0)], writes=[("TMP", r, qt)])
                    og_out(h, qg, r, gsl)

                if typ == "A":
                    heads = [(h, h, h) for h in range(8)]
                else:
                    heads = [(h, h // 4, h // 4) for h in range(8)]
                items = [(h, kch, vh, qg) for (h, kch, vh) in heads for qg in range(NQG)]
                kv_list = []
                for (h, kch, vh) in heads:
                    if not kv_list or kv_list[-1] != (kch, vh):
                        kv_list.append((kch, vh))
                kv_idx = 0
                kvs = load_kv(*kv_list[0])
                nxt_kvs = load_kv(*kv_list[1]) if len(kv_list) > 1 else None
                q_next = load_q(items[0][0], items[0][3], items[0][0])
                for i, (h, kch, vh, qg) in enumerate(items):
                    if (kch, vh) != kv_list[kv_idx]:
                        kv_idx += 1
                        kvs = nxt_kvs
                        if kv_idx + 1 < len(kv_list):
                            nxt_kvs = load_kv(*kv_list[kv_idx + 1])
                    cur = q_next
                    if i + 1 < len(items):
                        q_next = load_q(items[i + 1][0], items[i + 1][3], items[i + 1][0])
                    if typ == "A":
                        a0 = attn_unit(kvs, cur, 0)
                        evac_A0(a0)
                        a1 = attn_unit(kvs, cur, 64)
                        evac_A1(a1, h, qg, cur)
                    else:
                        a0 = attn_unit(kvs, cur, 0)
                        evac_B(a0, h, qg, cur)

        def phase_O(layer, si):
            xin, yout, S = seqs[si]
            xsrc = xin if layer == 0 else yout
            NT = S // 128
            if True:
                XT = lb("XTo", [128, 2, 1024], F32)
                OGT = lb("OGT", [128, 2, 8, 128], BF16)
                M = lb("M", [128, 2, 1024], F32)
                ST = lb("STo", [128, 2, 8], F32)
                T.dma("sp", XT[:, 0, :], xsrc[0:128, :], writes=[("XT", 0)])
                for t in range(NT):
                    sl = t % 2
                    if t + 1 < NT:
                        T.dma("sp", XT[:, (t + 1) % 2, :], xsrc[(t + 1) * 128:(t + 2) * 128, :], writes=[("XT", (t + 1) % 2)])

                    def th(e):
                        for wc in range(8):
                            ins = e.transpose(out=TPS[:, wc * 128:(wc + 1) * 128], in_=OGv[:, t, wc * 128:(wc + 1) * 128], identity=IDENT[:])
                        return ins
                    T.op("pe", th, reads=[("IDENT",)] + [("OG", hh, t // 4) for hh in range(8)], writes=[("TPS", 0), ("TPS", 1)])
                    T.op("act", lambda e: e.activation(out=OGT[:, sl, :, :], in_=TPS[:].rearrange("p (k t) -> p k t", k=8), func=AF.Copy),
                         reads=[("TPS", 0), ("TPS", 1)], writes=[("OGT", sl)])
                    for ng in range(2):
                        ps = pj_slot()
                        def f(e, ng=ng, ps=ps):
                            for wc in range(8):
                                ins = e.matmul(PJ[:, ps, :], lhsT=OGT[:, sl, wc, :], rhs=WOUT[:, wc, ng * 512:(ng + 1) * 512],
                                               start=(wc == 0), stop=(wc == 7))
                            return ins
                        T.op("pe", f, reads=[("OGT", sl)], writes=[("PJ", ps)])
                        T.op("act", lambda e, ng=ng, ps=ps: e.activation(out=M[:, sl, ng * 512:(ng + 1) * 512], in_=PJ[:, ps, :], func=AF.Copy),
                             reads=[("PJ", ps)], writes=[("M", sl, ng)])
                        T.op("dve", lambda e, ng=ng: e.scalar_tensor_tensor(out=JK[:, 0:512], in0=M[:, sl, ng * 512:(ng + 1) * 512], scalar=1.0,
                                                                             in1=M[:, sl, ng * 512:(ng + 1) * 512], op0=ALU.mult, op1=ALU.mult,
                                                                             accum_out=ST[:, sl, ng:ng + 1]),
                             reads=[("M", sl, ng)], writes=[("ST", sl, ng), ("JK",)])
                    T.op("dve", lambda e: e.tensor_tensor(out=ST[:, sl, 2:3], in0=ST[:, sl, 0:1], in1=ST[:, sl, 1:2], op=ALU.add),
                         reads=[("ST", sl, 0), ("ST", sl, 1)], writes=[("ST", sl, 2)])
                    T.op("dve", lambda e: e.tensor_scalar(out=ST[:, sl, 3:4], in0=ST[:, sl, 2:3], scalar1=1.0 / D, scalar2=EPS,
                                                          op0=ALU.mult, op1=ALU.add), reads=[("ST", sl, 2)], writes=[("ST", sl, 3)])
                    T.op("pool", lambda e: e.tensor_tensor(out=ST[:, sl, 4:5], in0=ST[:, sl, 3:4], in1=NEGH[:, 0:1], op=ALU.pow),
                         reads=[("ST", sl, 3), ("NEGH",)], writes=[("ST", sl, 4)])
                    T.op("dve", lambda e: e.scalar_tensor_tensor(out=M[:, sl, :], in0=M[:, sl, :], scalar=ST[:, sl, 4:5], in1=NPOST[:],
                                                                 op0=ALU.mult, op1=ALU.mult),
                         reads=[("M", sl, 0), ("M", sl, 1), ("ST", sl, 4), ("NPOST",)], writes=[("M", sl, 0), ("M", sl, 1)])
                    T.op("pool", lambda e: e.tensor_tensor(out=M[:, sl, :], in0=M[:, sl, :], in1=XT[:, sl, :], op=ALU.add),
                         reads=[("M", sl, 0), ("M", sl, 1), ("XT", sl)], writes=[("M", sl, 0), ("M", sl, 1)])
                    T.dma("pool", yout[t * 128:(t + 1) * 128, :], M[:, sl, :], reads=[("M", sl, 0), ("M", sl, 1)])

        T.barrier()
        for layer in layers:
            layer_setup(layer)
            if "W" in phases:
                load_win(layer)
            if "P" in phases:
                with ExitStack() as sc:
                    scope["es"], scope["cache"] = sc, {}
                    for si in seq_ids:
                        phase_P(layer, si)
                    T.barrier()
            with ExitStack() as sc:
                scope["es"], scope["cache"] = sc, {}
                for si in seq_ids:
                    if "T" in phases:
                        phase_T(layer, si)
                    if "O" in phases:
                        phase_O(layer, si)
                T.barrier()
        T.final_wait("sp")
        T.final_wait("pool")
    return nc


def _tables():
    pos = np.arange(S_P, dtype=np.float32)
    inv_a = (np.float32(500000.0) ** (-(np.arange(0, 16, 2, dtype=np.float32) / np.float32(16)))).astype(np.float32)
    ang = pos[:, None] * inv_a[None, :]
    ca, sa = np.cos(ang).astype(np.float32), np.sin(ang).astype(np.float32)
    tab_a = np.concatenate([np.tile(ca, (1, 8)), np.tile(sa, (1, 8))], axis=1)
    inv_b = (np.float32(10000.0) ** (-(np.arange(0, 64, 2, dtype=np.float32) / np.float32(64)))).astype(np.float32)
    rows = np.floor(pos / 64).astype(np.float32)
    cols = (pos - rows * 64).astype(np.float32)
    ar, ac = rows[:, None] * inv_b[None, :], cols[:, None] * inv_b[None, :]
    cosb = np.concatenate([np.cos(ar), np.cos(ac)], axis=1).astype(np.float32)
    sinb = np.concatenate([np.sin(ar), np.sin(ac)], axis=1).astype(np.float32)
    tab_b = np.concatenate([np.tile(cosb, (1, 4)), np.tile(sinb, (1, 4))], axis=1)
    return np.ascontiguousarray(tab_a, dtype=np.float32), np.ascontiguousarray(tab_b, dtype=np.float32)


_NC_CACHE = {}


def kernel(x_prompt, x_sample, norm_pre, norm_post, a_w_in, a_w_out, a_lam, a_subln,
           b_w_in, b_w_out, b_q_norm, b_k_norm):
    f = lambda a: np.ascontiguousarray(np.asarray(a), dtype=np.float32)
    x_prompt, x_sample = f(x_prompt), f(x_sample)
    shared = {
        "norm_pre": f(norm_pre), "norm_post": f(norm_post), "a_w_in": f(a_w_in), "a_w_out": f(a_w_out),
        "a_lam": f(a_lam), "a_subln": f(a_subln), "b_w_in": f(b_w_in), "b_w_out": f(b_w_out),
        "b_q_norm": f(b_q_norm), "b_k_norm": f(b_k_norm),
    }
    tab_a, tab_b = _tables()
    shared["tab_a"] = tab_a
    shared["tab_b"] = tab_b
    shared["ident"] = np.eye(128, dtype=np.float32).astype(ml_dtypes.bfloat16)
    if "nc" not in _NC_CACHE:
        _NC_CACHE["nc"] = build_nc()
    nc = _NC_CACHE["nc"]
    in_maps = []
    for c in range(NCORES):
        m = dict(shared)
        m["xp"] = x_prompt[2 * c:2 * c + 2]
        m["xs"] = x_sample[c:c + 1]
        in_maps.append(m)
    res = run_bass_kernel_spmd(nc, in_maps, core_ids=list(range(NCORES)))
    y_prompt = np.concatenate([np.asarray(r["yp"], dtype=np.float32) for r in res.results], axis=0)
    y_sample = np.concatenate([np.asarray(r["ys"], dtype=np.float32) for r in res.results], axis=0)
    return (y_prompt, y_sample)
```

```python
import math
from contextlib import ExitStack

import numpy as np
import ml_dtypes

import concourse.bass as bass
import concourse.mybir as mybir
from concourse.bass_utils import run_bass_kernel_spmd

F32 = mybir.dt.float32
BF16 = mybir.dt.bfloat16
AF = mybir.ActivationFunctionType
ALU = mybir.AluOpType
AX = mybir.AxisListType

D = 1024
DEPTH = 4
S_P, S_S = 4096, 2048
NCORES = 8
EPS = 1e-6
A_IN, B_IN = 4096, 2560
NTMAX = S_P // 128


class Tracker:
    COMPUTE = ("pe", "act", "dve", "pool")

    def __init__(self, nc, es, n_dma_sems=12):
        self.nc = nc
        self.eng = {"pe": nc.tensor, "act": nc.scalar, "dve": nc.vector, "pool": nc.gpsimd, "sp": nc.sync}
        self.sem = {}
        self.cnt = {}
        for e in self.COMPUTE:
            self.sem[e] = es.enter_context(nc.semaphore("s_" + e))
            self.cnt[e] = 0
        self.rings = {}
        for q in ("sp", "pool"):
            names = []
            for i in range(n_dma_sems):
                k = "d_%s%d" % (q, i)
                self.sem[k] = es.enter_context(nc.semaphore(k))
                self.cnt[k] = 0
                names.append(k)
            self.rings[q] = [names, 0]
        self.seen = {e: {} for e in self.eng}
        self.last_w = {}
        self.readers = {}

    def _wait(self, e, ticket):
        if ticket is None:
            return
        k, v = ticket
        if e == "pe" and k == "pe":
            return
        if self.seen[e].get(k, 0) >= v:
            return
        self.eng[e].wait_ge(self.sem[k], v)
        self.seen[e][k] = v

    def _deps(self, e, reads, writes):
        for r in reads:
            self._wait(e, self.last_w.get(r))
        for w in writes:
            self._wait(e, self.last_w.get(w))
            for t in self.readers.get(w, ()):
                self._wait(e, t)

    def _record(self, ticket, reads, writes):
        for w in writes:
            self.last_w[w] = ticket
            self.readers[w] = []
        for r in reads:
            self.readers.setdefault(r, []).append(ticket)

    def op(self, e, fn, reads=(), writes=(), tag=None):
        self._deps(e, reads, writes)
        ins = fn(self.eng[e])
        self.cnt[e] += 1
        ins.then_inc(self.sem[e], 1)
        t = (e, self.cnt[e])
        self._record(t, reads, writes)
        return t

    def dma(self, q, out, in_, reads=(), writes=(), tag=None):
        names, idx = self.rings[q]
        k = names[idx % len(names)]
        self.rings[q][1] = idx + 1
        if self.cnt[k] > 0:
            self._wait(q, (k, self.cnt[k]))
        self._deps(q, reads, writes)
        self.eng[q].dma_start(out=out, in_=in_).then_inc(self.sem[k], 16)
        self.cnt[k] += 16
        t = (k, self.cnt[k])
        self._record(t, reads, writes)
        return t

    def barrier(self):
        for e in self.eng:
            for k, v in self.cnt.items():
                if v > 0:
                    self._wait(e, (k, v))
        self.last_w.clear()
        self.readers.clear()

    def final_wait(self, e="sp"):
        for k, v in self.cnt.items():
            if v > 0:
                self._wait(e, (k, v))


def build_nc(layers=(0, 1, 2, 3), seq_ids=(0, 1, 2), phases="WPTO"):
    nc = bass.Bass("TRN2", target_bir_lowering=False)
    dt_in = lambda n, s, d=F32: nc.dram_tensor(n, s, d, kind="ExternalInput").ap()
    xp = dt_in("xp", [2, S_P, D])
    xs = dt_in("xs", [1, S_S, D])
    norm_pre = dt_in("norm_pre", [DEPTH, D])
    norm_post = dt_in("norm_post", [DEPTH, D])
    a_w_in = dt_in("a_w_in", [2, D, A_IN])
    a_w_out = dt_in("a_w_out", [2, D, D])
    a_lam = dt_in("a_lam", [2, 4, 64])
    a_subln = dt_in("a_subln", [2, 128])
    b_w_in = dt_in("b_w_in", [2, D, B_IN])
    b_w_out = dt_in("b_w_out", [2, D, D])
    b_q_norm = dt_in("b_q_norm", [2, 128])
    b_k_norm = dt_in("b_k_norm", [2, 128])
    tab_a = dt_in("tab_a", [S_P, 128])
    tab_b = dt_in("tab_b", [S_P, 512])
    ident_d = dt_in("ident", [128, 128], BF16)
    yp = nc.dram_tensor("yp", [2, S_P, D], F32, kind="ExternalOutput").ap()
    ys = nc.dram_tensor("ys", [1, S_S, D], F32, kind="ExternalOutput").ap()
    QT_all = nc.dram_tensor("QT_d", [3, 1024, S_P], BF16, kind="Internal").ap()
    KT_all = nc.dram_tensor("KT_d", [3, 1024, S_P], BF16, kind="Internal").ap()
    VX_all = nc.dram_tensor("VX_d", [3, 8, 128, NTMAX, 129], BF16, kind="Internal").ap()
    G_all = nc.dram_tensor("G_d", [3, 8, 128, NTMAX, 128], F32, kind="Internal").ap()

    seqs = [(xp[0], yp[0], S_P), (xp[1], yp[1], S_P), (xs[0], ys[0], S_S)]

    with ExitStack() as es:
        uniq = [0]

        def sb(n, s, d, st=es):
            uniq[0] += 1
            return st.enter_context(nc.sbuf_tensor("%s_%d" % (n, uniq[0]), s, d))
        T = Tracker(nc, es)

        BIG = sb("BIG", [128, 32768], BF16)
        WOUT = sb("WOUT", [128, 8, 1024], BF16)
        IDENT = sb("IDENT", [128, 128], BF16)
        NPRE = sb("NPRE", [128, 1024], F32)
        NPOST = sb("NPOST", [128, 1024], F32)
        SMW = sb("SMW", [128, 3, 128], F32)
        LAMB = sb("LAMB", [128, 4, 64], F32)
        LAMP = sb("LAMP", [128, 2, 64], F32)
        LAMS = sb("LAMS", [128, 8], F32)
        NEGH = sb("NEGH", [128, 4], F32)
        JK = sb("JK", [128, 1024], BF16)
        JKF = sb("JKF", [128, 128], F32)
        TPS = es.enter_context(nc.psum_tensor("TPS", [128, 1024], BF16))
        PJ = es.enter_context(nc.psum_tensor("PJ", [128, 3, 512], F32))
        ACC = es.enter_context(nc.psum_tensor("ACC", [128, 4, 512], F32))

        TPS2 = ACC[:, 3, :].bitcast(BF16)
        WINv = BIG[:].rearrange("p (k n) -> p k n", k=8)
        OGv = BIG[:].rearrange("p (t w) -> p t w", w=1024)

        T.dma("sp", IDENT[:], ident_d, writes=[("IDENT",)])
        T.op("dve", lambda e: e.memset(NEGH[:], -0.5), writes=[("NEGH",)])

        cnt = {"pj": 0, "pt": 0, "acc": 0}
        scope = {"es": None, "cache": {}}

        def lb(name, shape, dtype):
            if name not in scope["cache"]:
                scope["cache"][name] = sb(name, shape, dtype, scope["es"])
            return scope["cache"][name]

        def first(flag):
            if flag in scope["cache"]:
                return False
            scope["cache"][flag] = True
            return True

        def pj_slot():
            s = cnt["pj"] % 3
            cnt["pj"] += 1
            return s

        def load_weight(dst3, src, ncols, chunk, STG, key):
            i = 0
            for kc in range(8):
                for c0 in range(0, ncols, chunk):
                    w = min(chunk, ncols - c0)
                    sl = i % 2
                    T.dma("sp", STG[:, sl, 0:w], src[kc * 128:(kc + 1) * 128, c0:c0 + w], writes=[("STG", sl)])
                    eng = "dve" if i % 2 == 0 else "pool"
                    T.op(eng, lambda e, kc=kc, c0=c0, w=w, sl=sl: e.tensor_copy(out=dst3[:, kc, c0:c0 + w], in_=STG[:, sl, 0:w]),
                         reads=[("STG", sl)], writes=[(key, kc, c0)])
                    i += 1

        def layer_setup(layer):
            typ = "A" if layer % 2 == 0 else "B"
            j = layer // 2
            with ExitStack() as ls:
                STG = sb("STGo", [128, 2, 1024], F32, ls)
                load_weight(WOUT, (a_w_out if typ == "A" else b_w_out)[j], 1024, 1024, STG, "WOUT")
                T.dma("sp", NPRE[:], norm_pre[layer:layer + 1, :].partition_broadcast(128), writes=[("NPRE",)])
                T.dma("sp", NPOST[:], norm_post[layer:layer + 1, :].partition_broadcast(128), writes=[("NPOST",)])
                if typ == "A":
                    lam_init = 0.8 - 0.6 * math.exp(-0.3 * layer)
                    T.dma("sp", SMW[:, 2, :], a_subln[j:j + 1, :].partition_broadcast(128), writes=[("SMW", 2)])
                    T.op("dve", lambda e: e.tensor_scalar(out=SMW[:, 0, :], in0=SMW[:, 2, :], scalar1=float(1.0 - lam_init),
                                                          scalar2=None, op0=ALU.mult), reads=[("SMW", 2)], writes=[("SMW", 0)])
                    T.dma("sp", LAMB[:].rearrange("p a b -> p (a b)"),
                          a_lam[j:j + 1].rearrange("o a b -> o (a b)").partition_broadcast(128), writes=[("LAMB",)])
                    lv = LAMB[:].rearrange("p (a b) c -> p a b c", b=2)
                    T.op("dve", lambda e: e.tensor_tensor(out=LAMP[:], in0=lv[:, :, 0, :], in1=lv[:, :, 1, :], op=ALU.mult),
                         reads=[("LAMB",)], writes=[("LAMP",)])
                    T.op("dve", lambda e: e.tensor_reduce(out=LAMS[:, 0:2], in_=LAMP[:], axis=AX.X, op=ALU.add),
                         reads=[("LAMP",)], writes=[("LAMS", 0)])
                    T.op("act", lambda e: e.activation(out=LAMS[:, 2:4], in_=LAMS[:, 0:2], func=AF.Exp),
                         reads=[("LAMS", 0)], writes=[("LAMS", 2)])
                    T.op("dve", lambda e: e.scalar_tensor_tensor(out=LAMS[:, 4:5], in0=LAMS[:, 3:4], scalar=float(-lam_init),
                                                                 in1=LAMS[:, 2:3], op0=ALU.add, op1=ALU.subtract),
                         reads=[("LAMS", 2)], writes=[("LAMS", 4)])
                else:
                    T.dma("sp", SMW[:, 0, :], b_q_norm[j:j + 1, :].partition_broadcast(128), writes=[("SMW", 0)])
                    T.dma("sp", SMW[:, 1, :], b_k_norm[j:j + 1, :].partition_broadcast(128), writes=[("SMW", 1)])
                T.barrier()

        def load_win(layer):
            typ = "A" if layer % 2 == 0 else "B"
            j = layer // 2
            with ExitStack() as ls:
                STG = sb("STGi", [128, 2, 2048], F32, ls)
                if typ == "A":
                    load_weight(WINv, a_w_in[j], A_IN, 2048, STG, "WIN")
                else:
                    load_weight(WINv, b_w_in[j], B_IN, 1280, STG, "WIN")
                T.barrier()

        def phase_P(layer, si):
            typ = "A" if layer % 2 == 0 else "B"
            xin, yout, S = seqs[si]
            QT_d, KT_d, VX_d, G_d = QT_all[si], KT_all[si], VX_all[si], G_all[si]
            xsrc = xin if layer == 0 else yout
            NT = S // 128
            NG = 8 if typ == "A" else 5
            tab_d = tab_a if typ == "A" else tab_b
            tabw = 128 if typ == "A" else 512
            nkc = 8 if typ == "A" else 2
            nvh = 8 if typ == "A" else 2
            if True:
                XT = lb("XT", [128, 3, 1024], F32)
                TAB = lb("TAB", [128, 3, 512], F32)
                ST = lb("ST", [128, 3, 16], F32)
                H = lb("H", [128, 2, 1024], BF16)
                HT = lb("HT", [128, 2, 8, 128], BF16)
                QKF = lb("QKF", [128, 8, 512], BF16)
                SQ = lb("SQ", [128, 3, 512], F32)
                QN = lb("QN", [128, 3, 512], F32)
                RT = lb("RT", [128, 3, 4, 256], F32)
                ST2 = lb("ST2", [128, 3, 12], F32)
                VXS = lb("VXS", [128, 2, 8, 2, 129], BF16)
                GS = lb("GS", [128, 3, 1024], F32)
                QTS = lb("QTS", [128, 2, 8, 256], BF16)
                KTS = lb("KTS", [128, 2, 8, 256], BF16)

                if first("vxs_ones"):
                    for sl in range(2):
                        T.op("pool", lambda e, sl=sl: e.memset(VXS[:, sl, :, :, 128:129], 1.0), writes=[("VXS", sl, "ones")])

                qc = {"qkf": 0, "rt": 0, "tq": 0}

                def load_x(t):
                    T.dma("sp", XT[:, t % 3, :], xsrc[t * 128:(t + 1) * 128, :], writes=[("XT", t % 3)])

                def load_tab(t):
                    T.dma("sp", TAB[:, t % 3, 0:tabw], tab_d[t * 128:(t + 1) * 128, :], writes=[("TAB", t % 3)])

                def head_stats(t):
                    x3, hs = t % 3, t % 2
                    T.op("act", lambda e: e.activation(out=JK[:], in_=XT[:, x3, :], func=AF.Square, accum_out=ST[:, x3, 0:1]),
                         reads=[("XT", x3)], writes=[("ST", x3, 0), ("JK",)])
                    T.op("dve", lambda e: e.tensor_scalar(out=ST[:, x3, 1:2], in0=ST[:, x3, 0:1], scalar1=1.0 / D, scalar2=EPS,
                                                          op0=ALU.mult, op1=ALU.add), reads=[("ST", x3, 0)], writes=[("ST", x3, 1)])
                    T.op("pool", lambda e: e.tensor_tensor(out=ST[:, x3, 2:3], in0=ST[:, x3, 1:2], in1=NEGH[:, 0:1], op=ALU.pow),
                         reads=[("ST", x3, 1), ("NEGH",)], writes=[("ST", x3, 2)])
                    T.op("dve", lambda e: e.scalar_tensor_tensor(out=H[:, hs, :], in0=XT[:, x3, :], scalar=ST[:, x3, 2:3],
                                                                 in1=NPRE[:], op0=ALU.mult, op1=ALU.mult),
                         reads=[("XT", x3), ("ST", x3, 2), ("NPRE",)], writes=[("H", hs)])

                def head_tr(t):
                    hs = t % 2
                    def th(e):
                        for kc in range(8):
                            ins = e.transpose(out=TPS[:, kc * 128:(kc + 1) * 128], in_=H[:, hs, kc * 128:(kc + 1) * 128], identity=IDENT[:])
                        return ins
                    T.op("pe", th, reads=[("H", hs), ("IDENT",)], writes=[("TPS", 0), ("TPS", 1)])
                    T.op("act", lambda e: e.activation(out=HT[:, hs, :, :], in_=TPS[:].rearrange("p (k t) -> p k t", k=8), func=AF.Copy),
                         reads=[("TPS", 0), ("TPS", 1)], writes=[("HT", hs)])

                def body(t):
                    xsl = t % 2
                    tsl = t % 3
                    tt = t % 2
                    sts = (t // 2) % 2
                    pend_tq = []

                    def mm_group(ng):
                        ps = pj_slot()
                        def f(e):
                            for kc in range(8):
                                ins = e.matmul(PJ[:, ps, :], lhsT=HT[:, xsl, kc, :], rhs=WINv[:, kc, ng * 512:(ng + 1) * 512],
                                               start=(kc == 0), stop=(kc == 7))
                            return ins
                        T.op("pe", f, reads=[("HT", xsl)], writes=[("PJ", ps)])
                        return ps

                    def emit_tq(qs, nch, dst, c0):
                        hq = qc["tq"] % 2
                        qc["tq"] += 1
                        tbuf = TPS if hq == 0 else TPS2
                        def f(e):
                            for c in range(nch):
                                ins = e.transpose(out=tbuf[:, c * 128:(c + 1) * 128],
                                                  in_=QKF[:, qs, c * 128:(c + 1) * 128], identity=IDENT[:])
                            return ins
                        T.op("pe", f, reads=[("QKF", qs, "a"), ("QKF", qs, "b"), ("QKF", qs, "c"), ("IDENT",)], writes=[("TPS", hq)])
                        dkey = "QTS" if dst is QTS else "KTS"
                        T.op("act", lambda e: e.activation(out=dst[:, sts, c0:c0 + nch, tt * 128:(tt + 1) * 128],
                                                           in_=tbuf[:, 0:nch * 128].rearrange("p (c t) -> p c t", c=nch),
                                                           func=AF.Copy),
                             reads=[("TPS", hq)], writes=[(dkey, sts, c0, tt)])

                    def post_A_qk(ng, ps):
                        qs = qc["qkf"] % 8
                        qc["qkf"] += 1
                        r = qc["rt"] % 3
                        qc["rt"] += 1
                        pjv = QN[:, r, :].rearrange("p (h d) -> p h d", d=64)
                        qfv = QKF[:, qs, :].rearrange("p (h d) -> p h d", d=64)
                        T.op("act", lambda e: e.activation(out=QN[:, r, :], in_=PJ[:, ps, :], func=AF.Copy),
                             reads=[("PJ", ps)], writes=[("QN", r)])
                        T.op("pool", lambda e: e.tensor_copy(out=qfv[:, :, 16:64], in_=pjv[:, :, 16:64]),
                             reads=[("QN", r)], writes=[("QKF", qs, "a")])
                        cos = TAB[:, tsl, 0:64].rearrange("p (h j) -> p h j", j=8)
                        sin = TAB[:, tsl, 64:128].rearrange("p (h j) -> p h j", j=8)
                        x1, x2 = pjv[:, :, 0:8], pjv[:, :, 8:16]
                        tv = [RT[:, r, k, 0:64].rearrange("p (h j) -> p h j", j=8) for k in range(4)]
                        for k, (xa, tb) in enumerate(((x1, cos), (x2, sin), (x2, cos), (x1, sin))):
                            T.op("dve", lambda e, k=k, xa=xa, tb=tb: e.tensor_tensor(out=tv[k], in0=xa, in1=tb, op=ALU.mult),
                                 reads=[("QN", r), ("TAB", tsl)], writes=[("RT", r, k)])
                        T.op("dve", lambda e: e.tensor_tensor(out=qfv[:, :, 0:8], in0=tv[0], in1=tv[1], op=ALU.subtract),
                             reads=[("RT", r, 0), ("RT", r, 1)], writes=[("QKF", qs, "b")])
                        T.op("dve", lambda e: e.tensor_tensor(out=qfv[:, :, 8:16], in0=tv[2], in1=tv[3], op=ALU.add),
                             reads=[("RT", r, 2), ("RT", r, 3)], writes=[("QKF", qs, "c")])
                        if ng < 2:
                            pend_tq.append((qs, 4, QTS, ng * 4))
                        else:
                            pend_tq.append((qs, 4, KTS, (ng - 2) * 4))

                    def post_B_qk(ps, col0, nh, wrow, dst, c0):
                        qs = qc["qkf"] % 8
                        qc["qkf"] += 1
                        r = qc["rt"] % 3
                        qc["rt"] += 1
                        n = nh * 128
                        qnk = [("QN", r, h) for h in range(nh)]
                        T.op("act", lambda e: e.activation(out=QN[:, r, 0:n], in_=PJ[:, ps, col0:col0 + n], func=AF.Copy),
                             reads=[("PJ", ps)], writes=[("QN", r, h) for h in range(4)])
                        T.op("act", lambda e: e.activation(out=SQ[:, r, 0:n], in_=QN[:, r, 0:n], func=AF.Square),
                             reads=qnk, writes=[("SQ", r)])
                        T.op("dve", lambda e: e.tensor_reduce(out=ST2[:, r, 0:nh], in_=SQ[:, r, 0:n].rearrange("p (h d) -> p h d", d=128),
                                                              axis=AX.X, op=ALU.add), reads=[("SQ", r)], writes=[("ST2", r, 0)])
                        T.op("dve", lambda e: e.tensor_scalar(out=ST2[:, r, 4:4 + nh], in0=ST2[:, r, 0:nh], scalar1=1.0 / 128, scalar2=EPS,
                                                              op0=ALU.mult, op1=ALU.add), reads=[("ST2", r, 0)], writes=[("ST2", r, 4)])
                        T.op("pool", lambda e: e.tensor_tensor(out=ST2[:, r, 8:8 + nh], in0=ST2[:, r, 4:4 + nh],
                                                               in1=NEGH[:, 0:nh], op=ALU.pow),
                             reads=[("ST2", r, 4), ("NEGH",)], writes=[("ST2", r, 8)])
                        for h in range(nh):
                            T.op("dve", lambda e, h=h: e.scalar_tensor_tensor(
                                out=QN[:, r, h * 128:(h + 1) * 128], in0=QN[:, r, h * 128:(h + 1) * 128],
                                scalar=ST2[:, r, 8 + h:9 + h], in1=SMW[:, wrow, :], op0=ALU.mult, op1=ALU.mult),
                                reads=[("QN", r, h), ("ST2", r, 8), ("SMW", wrow)], writes=[("QN", r, h)])
                        qnv = QN[:, r, 0:n].rearrange("p (h a f j) -> p h a f j", a=2, f=2, j=32)
                        qfv = QKF[:, qs, 0:n].rearrange("p (h a f j) -> p h a f j", a=2, f=2, j=32)
                        cos = TAB[:, tsl, 0:256].rearrange("p (h a j) -> p h a j", a=2, j=32)[:, 0:nh]
                        sin = TAB[:, tsl, 256:512].rearrange("p (h a j) -> p h a j", a=2, j=32)[:, 0:nh]
                        x1, x2 = qnv[:, :, :, 0, :], qnv[:, :, :, 1, :]
                        tv = [RT[:, r, k, 0:nh * 64].rearrange("p (h a j) -> p h a j", a=2, j=32) for k in range(4)]
                        for k, (xa, tb) in enumerate(((x1, cos), (x2, sin), (x2, cos), (x1, sin))):
                            T.op("dve", lambda e, k=k, xa=xa, tb=tb: e.tensor_tensor(out=tv[k], in0=xa, in1=tb, op=ALU.mult),
                                 reads=qnk + [("TAB", tsl)], writes=[("RT", r, k)])
                        T.op("dve", lambda e: e.tensor_tensor(out=qfv[:, :, :, 0, :], in0=tv[0], in1=tv[1], op=ALU.subtract),
                             reads=[("RT", r, 0), ("RT", r, 1)], writes=[("QKF", qs, "b")])
                        T.op("dve", lambda e: e.tensor_tensor(out=qfv[:, :, :, 1, :], in0=tv[2], in1=tv[3], op=ALU.add),
                             reads=[("RT", r, 2), ("RT", r, 3)], writes=[("QKF", qs, "c")])
                        pend_tq.append((qs, nh, dst, c0))

                    def post_v(ps, col0, nh, h0):
                        T.op("act", lambda e: e.activation(out=VXS[:, sts, h0:h0 + nh, tt, 0:128],
                                                           in_=PJ[:, ps, col0:col0 + nh * 128].rearrange("p (h e) -> p h e", e=128),
                                                           func=AF.Copy),
                             reads=[("PJ", ps)], writes=[("VXS", sts, tt, h0)])

                    def post_gate(ps, gc):
                        T.op("act", lambda e: e.activation(out=GS[:, t % 3, gc * 512:(gc + 1) * 512], in_=PJ[:, ps, :], func=AF.Silu),
                             reads=[("PJ", ps)], writes=[("GS", t % 3, gc)])

                    for ng in range(NG):
                        ps = mm_group(ng)
                        if typ == "A":
                            if ng < 4:
                                post_A_qk(ng, ps)
                            elif ng < 6:
                                post_v(ps, 0, 4, (ng - 4) * 4)
                            else:
                                post_gate(ps, ng - 6)
                        else:
                            if ng < 2:
                                post_B_qk(ps, 0, 4, 0, QTS, ng * 4)
                            elif ng == 2:
                                post_B_qk(ps, 0, 2, 1, KTS, 0)
                                post_v(ps, 256, 2, 0)
                            else:
                                post_gate(ps, ng - 3)
                    def emit_all_tq():
                        while pend_tq:
                            emit_tq(*pend_tq.pop(0))

                    def stores():
                        T.dma("pool", G_d[:, :, t, :].rearrange("h p e -> p h e"), GS[:, t % 3, :].rearrange("p (h e) -> p h e", e=128),
                              reads=[("GS", t % 3, 0), ("GS", t % 3, 1)], tag="g")
                        if tt == 1:
                            t0 = (t - 1) * 128
                            T.dma("pool", QT_d.rearrange("(c p) s -> p c s", p=128)[:, :, t0:t0 + 256], QTS[:, sts, :, :],
                                  reads=[("QTS", sts, c0, k) for c0 in (0, 4) for k in (0, 1)], tag="q")
                            kreads = [("KTS", sts, c0, k) for c0 in ((0, 4) if typ == "A" else (0,)) for k in (0, 1)]
                            T.dma("pool", KT_d.rearrange("(c p) s -> p c s", p=128)[:, 0:nkc, t0:t0 + 256], KTS[:, sts, 0:nkc, :], reads=kreads, tag="k")
                            vreads = [("VXS", sts, "ones")] + [("VXS", sts, k, h0) for k in (0, 1) for h0 in ((0, 4) if typ == "A" else (0,))]
                            T.dma("pool", VX_d[0:nvh, :, t - 1:t + 1, :].rearrange("h p n e -> p h (n e)"),
                                  VXS[:, sts, 0:nvh, :, :].rearrange("p h n e -> p h (n e)"), reads=vreads, tag="v")

                    return lambda: (emit_all_tq(), stores())

                for t0_ in range(min(3, NT)):
                    load_x(t0_)
                load_tab(0)
                head_stats(0)
                if NT > 1:
                    head_stats(1)
                head_tr(0)
                prev_tail = None
                for t in range(NT):
                    if t + 3 < NT:
                        load_x(t + 3)
                    if t + 1 < NT:
                        load_tab(t + 1)
                    if t + 2 < NT:
                        head_stats(t + 2)
                    tl = body(t)
                    if t + 1 < NT:
                        head_tr(t + 1)
                    if prev_tail is not None:
                        prev_tail()
                    prev_tail = tl
                prev_tail()

        def phase_T(layer, si):
            typ = "A" if layer % 2 == 0 else "B"
            _, _, S = seqs[si]
            QT_d, KT_d, VX_d, G_d = QT_all[si], KT_all[si], VX_all[si], G_all[si]
            NT = S // 128
            NKB = NT
            NQG = S // 512
            dk = 64 if typ == "A" else 128
            scale = float(dk) ** -0.5
            if True:
                KTB = lb("KTB", [128, 2, 2, S_P], BF16)
                VXB = lb("VXB", [128, 2, NTMAX, 129], BF16)
                QTB = lb("QTB", [128, 2, 512], BF16)
                GB = lb("GB", [128, 2, 4, 128], F32)
                PT = lb("PT", [128, 3, 512], BF16)
                O1 = lb("O1", [128, 4, 128], F32)
                DD = lb("DD", [128, 4, 128], F32)
                TMP = lb("TMP", [128, 2, 4, 128], F32)
                RZ = lb("RZ", [128, 2, 24], F32)

                ctr = {"kv": 0, "q": 0, "ev": 0}
                if typ == "A" and first("ktb_zero"):
                    for sl in range(2):
                        T.op("dve", lambda e, sl=sl: e.memset(KTB[64:128, sl, 0, :], 0.0), writes=[("KTB", sl, 0)])
                        T.op("pool", lambda e, sl=sl: e.memset(KTB[0:64, sl, 1, :], 0.0), writes=[("KTB", sl, 1)])

                def acc_ap(aset, qt):
                    b, jj = (2 * aset, qt) if qt < 3 else (2 * aset + 1, 0)
                    return ACC[:, b, jj * 129:(jj + 1) * 129]

                def load_kv(kchunk, vhead):
                    sl = ctr["kv"] % 2
                    ctr["kv"] += 1
                    if typ == "A":
                        T.dma("sp", KTB[0:64, sl, 0, 0:S], KT_d[kchunk * 128:kchunk * 128 + 64, 0:S], writes=[("KTB", sl, 0)])
                        T.dma("sp", KTB[64:128, sl, 1, 0:S], KT_d[kchunk * 128 + 64:(kchunk + 1) * 128, 0:S], writes=[("KTB", sl, 1)])
                    else:
                        T.dma("sp", KTB[:, sl, 0, 0:S], KT_d[kchunk * 128:(kchunk + 1) * 128, 0:S], writes=[("KTB", sl, 0)])
                    T.dma("sp", VXB[:, sl, 0:NT, :], VX_d[vhead, :, 0:NT, :], writes=[("VXB", sl)])
                    return sl

                def load_q(qchunk, qg, h):
                    sl = ctr["q"] % 2
                    ctr["q"] += 1
                    T.dma("sp", QTB[:, sl, :], QT_d[qchunk * 128:(qchunk + 1) * 128, qg * 512:(qg + 1) * 512], writes=[("QTB", sl)])
                    T.dma("sp", GB[:, sl, :, :], G_d[h, :, qg * 4:(qg + 1) * 4, :], writes=[("GB", sl)])
                    return sl

                def attn_unit(kvs, qs, p0):
                    aset = cnt["acc"] % 2
                    cnt["acc"] += 1
                    slots = {}

                    def qk(kb):
                        ps = pj_slot()
                        slots[kb] = ps
                        mp = p0 // 64
                        T.op("pe", lambda e: e.matmul(PJ[:, ps, :], lhsT=KTB[:, kvs, mp, kb * 128:(kb + 1) * 128],
                                                      rhs=QTB[:, qs, :], start=True, stop=True),
                             reads=[("KTB", kvs, mp), ("QTB", qs)], writes=[("PJ", ps)])
                        pts = cnt["pt"] % 3
                        cnt["pt"] += 1
                        T.op("act", lambda e: e.activation(out=PT[:, pts, :], in_=PJ[:, ps, :], func=AF.Exp, scale=scale),
                             reads=[("PJ", ps)], writes=[("PT", pts)])
                        slots[kb] = pts

                    def pv(kb):
                        pts = slots.pop(kb)
                        def f(e):
                            for qt in range(4):
                                ins = e.matmul(acc_ap(aset, qt), lhsT=PT[:, pts, qt * 128:(qt + 1) * 128], rhs=VXB[:, kvs, kb, :],
                                               start=(kb == 0 and qt in (0, 3)), stop=(kb == NKB - 1), skip_group_check=True)
                            return ins
                        T.op("pe", f, reads=[("PT", pts), ("VXB", kvs)], writes=[("ACC", aset)])

                    qk(0)
                    qk(1)
                    for kb in range(NKB):
                        if kb + 2 < NKB:
                            qk(kb + 2)
                        pv(kb)
                    return aset

                def recip_z(aset, r):
                    zv = ACC[:, 2 * aset, 0:387].rearrange("p (a b) -> p a b", b=129)[:, :, 128:129]
                    T.op("dve", lambda e: e.reciprocal(out=RZ[:, r, 0:3].rearrange("p (a b) -> p a b", b=1), in_=zv),
                         reads=[("ACC", aset)], writes=[("RZ", r, 0)])
                    T.op("dve", lambda e: e.reciprocal(out=RZ[:, r, 3:4], in_=ACC[:, 2 * aset + 1, 128:129]),
                         reads=[("ACC", aset)], writes=[("RZ", r, 3)])

                def og_out(h, qg, r, gsl):
                    ogv = OGv[:, qg * 4:(qg + 1) * 4, h * 128:(h + 1) * 128]
                    T.op("pool", lambda e: e.tensor_tensor(out=ogv, in0=TMP[:, r, :, :], in1=GB[:, gsl, :, :], op=ALU.mult),
                         reads=[("TMP", r, q) for q in range(4)] + [("GB", gsl)], writes=[("OG", h, qg)])

                def evac_B(aset, h, qg, gsl):
                    r = ctr["ev"] % 2
                    ctr["ev"] += 1
                    recip_z(aset, r)
                    for qt in range(4):
                        T.op("dve", lambda e, qt=qt: e.tensor_scalar(out=TMP[:, r, qt, :], in0=acc_ap(aset, qt)[:, 0:128],
                                                                      scalar1=RZ[:, r, qt:qt + 1], scalar2=None, op0=ALU.mult),
                             reads=[("ACC", aset), ("RZ", r, 0), ("RZ", r, 3)], writes=[("TMP", r, qt)])
                    og_out(h, qg, r, gsl)

                def evac_A0(aset):
                    r = ctr["ev"] % 2
                    ctr["ev"] += 1
                    recip_z(aset, r)
                    for qt in range(4):
                        T.op("dve", lambda e, qt=qt: e.tensor_scalar(out=O1[:, qt, :], in0=acc_ap(aset, qt)[:, 0:128],
                                                                      scalar1=RZ[:, r, qt:qt + 1], scalar2=None, op0=ALU.mult),
                             reads=[("ACC", aset), ("RZ", r, 0), ("RZ", r, 3)], writes=[("O1", qt)])

                def evac_A1(aset, h, qg, gsl):
                    r = ctr["ev"] % 2
                    ctr["ev"] += 1
                    recip_z(aset, r)
                    T.op("dve", lambda e: e.tensor_scalar(out=RZ[:, r, 4:8], in0=RZ[:, r, 0:4], scalar1=LAMS[:, 4:5], scalar2=None, op0=ALU.mult),
                         reads=[("RZ", r, 0), ("RZ", r, 3), ("LAMS", 4)], writes=[("RZ", r, 4)])
                    for qt in range(4):
                        T.op("dve", lambda e, qt=qt: e.scalar_tensor_tensor(out=DD[:, qt, :], in0=acc_ap(aset, qt)[:, 0:128],
                                                                             scalar=RZ[:, r, 4 + qt:5 + qt], in1=O1[:, qt, :],
                                                                             op0=ALU.mult, op1=ALU.add),
                             reads=[("ACC", aset), ("RZ", r, 4), ("O1", qt)], writes=[("DD", qt)])
                    for qt in range(4):
                        T.op("dve", lambda e, qt=qt: e.scalar_tensor_tensor(out=JKF[:], in0=DD[:, qt, :], scalar=1.0, in1=DD[:, qt, :],
                                                                             op0=ALU.mult, op1=ALU.mult, accum_out=RZ[:, r, 8 + qt:9 + qt]),
                             reads=[("DD", qt)], writes=[("RZ", r, 8 + qt), ("JKF",)])
                    T.op("dve", lambda e: e.tensor_scalar(out=RZ[:, r, 12:16], in0=RZ[:, r, 8:12], scalar1=1.0 / 128, scalar2=EPS,
                                                          op0=ALU.mult, op1=ALU.add),
                         reads=[("RZ", r, 8 + q) for q in range(4)], writes=[("RZ", r, 12)])
                    T.op("pool", lambda e: e.tensor_tensor(out=RZ[:, r, 16:20], in0=RZ[:, r, 12:16], in1=NEGH[:, 0:4],
                                                           op=ALU.pow), reads=[("RZ", r, 12), ("NEGH",)], writes=[("RZ", r, 16)])
                    for qt in range(4):
                        T.op("dve", lambda e, qt=qt: e.scalar_tensor_tensor(out=TMP[:, r, qt, :], in0=DD[:, qt, :],
                                                                             scalar=RZ[:, r, 16 + qt:17 + qt], in1=SMW[:, 0, :],
                                                                             op0=ALU.mult, op1=ALU.mult),
                             reads=[("DD", qt), ("RZ", r, 16), ("SMW", 0)], writes=[("TMP", r, qt)])
                    og_out(h, qg, r, gsl)

                if typ == "A":
                    heads = [(h, h, h) for h in range(8)]
                else:
                    heads = [(h, h // 4, h // 4) for h in range(8)]
                items = [(h, kch, vh, qg) for (h, kch, vh) in heads for qg in range(NQG)]
                kv_list = []
                for (h, kch, vh) in heads:
                    if not kv_list or kv_list[-1] != (kch, vh):
                        kv_list.append((kch, vh))
                kv_idx = 0
                kvs = load_kv(*kv_list[0])
                nxt_kvs = load_kv(*kv_list[1]) if len(kv_list) > 1 else None
                q_next = load_q(items[0][0], items[0][3], items[0][0])
                for i, (h, kch, vh, qg) in enumerate(items):
                    if (kch, vh) != kv_list[kv_idx]:
                        kv_idx += 1
                        kvs = nxt_kvs
                        if kv_idx + 1 < len(kv_list):
                            nxt_kvs = load_kv(*kv_list[kv_idx + 1])
                    cur = q_next
                    if i + 1 < len(items):
                        q_next = load_q(items[i + 1][0], items[i + 1][3], items[i + 1][0])
                    if typ == "A":
                        a0 = attn_unit(kvs, cur, 0)
                        evac_A0(a0)
                        a1 = attn_unit(kvs, cur, 64)
                        evac_A1(a1, h, qg, cur)
                    else:
                        a0 = attn_unit(kvs, cur, 0)
                        evac_B(a0, h, qg, cur)

        def phase_O(layer, si):
            xin, yout, S = seqs[si]
            xsrc = xin if layer == 0 else yout
            NT = S // 128
            if True:
                XT = lb("XTo", [128, 2, 1024], F32)
                OGT = lb("OGT", [128, 2, 8, 128], BF16)
                M = lb("M", [128, 3, 1024], F32)
                ST = lb("STo", [128, 3, 8], F32)
                T.dma("sp", XT[:, 0, :], xsrc[0:128, :], writes=[("XT", 0)])
                def th_copy(t):
                    sl = t % 2
                    def th(e):
                        for wc in range(8):
                            ins = e.transpose(out=TPS[:, wc * 128:(wc + 1) * 128], in_=OGv[:, t, wc * 128:(wc + 1) * 128], identity=IDENT[:])
                        return ins
                    T.op("pe", th, reads=[("IDENT",)] + [("OG", hh, t // 4) for hh in range(8)], writes=[("TPS", 0), ("TPS", 1)])
                    T.op("act", lambda e: e.activation(out=OGT[:, sl, :, :], in_=TPS[:].rearrange("p (k t) -> p k t", k=8), func=AF.Copy),
                         reads=[("TPS", 0), ("TPS", 1)], writes=[("OGT", sl)])

                th_copy(0)
                for t in range(NT):
                    sl = t % 2
                    msl = t % 3
                    if t + 1 < NT:
                        T.dma("sp", XT[:, (t + 1) % 2, :], xsrc[(t + 1) * 128:(t + 2) * 128, :], writes=[("XT", (t + 1) % 2)])
                        th_copy(t + 1)
                    for ng in range(2):
                        ps = pj_slot()
                        def f(e, ng=ng, ps=ps):
                            for wc in range(8):
                                ins = e.matmul(PJ[:, ps, :], lhsT=OGT[:, sl, wc, :], rhs=WOUT[:, wc, ng * 512:(ng + 1) * 512],
                                               start=(wc == 0), stop=(wc == 7))
                            return ins
                        T.op("pe", f, reads=[("OGT", sl)], writes=[("PJ", ps)])
                        T.op("act", lambda e, ng=ng, ps=ps: e.activation(out=M[:, msl, ng * 512:(ng + 1) * 512], in_=PJ[:, ps, :], func=AF.Copy),
                             reads=[("PJ", ps)], writes=[("M", msl, ng)])
                        T.op("dve", lambda e, ng=ng: e.scalar_tensor_tensor(out=JK[:, 0:512], in0=M[:, msl, ng * 512:(ng + 1) * 512], scalar=1.0,
                                                                             in1=M[:, msl, ng * 512:(ng + 1) * 512], op0=ALU.mult, op1=ALU.mult,
                                                                             accum_out=ST[:, msl, ng:ng + 1]),
                             reads=[("M", msl, ng)], writes=[("ST", msl, ng), ("JK",)])
                    T.op("dve", lambda e: e.tensor_tensor(out=ST[:, msl, 2:3], in0=ST[:, msl, 0:1], in1=ST[:, msl, 1:2], op=ALU.add),
                         reads=[("ST", msl, 0), ("ST", msl, 1)], writes=[("ST", msl, 2)])
                    T.op("dve", lambda e: e.tensor_scalar(out=ST[:, msl, 3:4], in0=ST[:, msl, 2:3], scalar1=1.0 / D, scalar2=EPS,
                                                          op0=ALU.mult, op1=ALU.add), reads=[("ST", msl, 2)], writes=[("ST", msl, 3)])
                    T.op("pool", lambda e: e.tensor_tensor(out=ST[:, msl, 4:5], in0=ST[:, msl, 3:4], in1=NEGH[:, 0:1], op=ALU.pow),
                         reads=[("ST", msl, 3), ("NEGH",)], writes=[("ST", msl, 4)])
                    T.op("dve", lambda e: e.scalar_tensor_tensor(out=M[:, msl, :], in0=M[:, msl, :], scalar=ST[:, msl, 4:5], in1=NPOST[:],
                                                                 op0=ALU.mult, op1=ALU.mult),
                         reads=[("M", msl, 0), ("M", msl, 1), ("ST", msl, 4), ("NPOST",)], writes=[("M", msl, 0), ("M", msl, 1)])
                    T.op("pool", lambda e: e.tensor_tensor(out=M[:, msl, :], in0=M[:, msl, :], in1=XT[:, sl, :], op=ALU.add),
                         reads=[("M", msl, 0), ("M", msl, 1), ("XT", sl)], writes=[("M", msl, 0), ("M", msl, 1)])
                    T.dma("pool", yout[t * 128:(t + 1) * 128, :], M[:, msl, :], reads=[("M", msl, 0), ("M", msl, 1)])

        T.barrier()
        for layer in layers:
            layer_setup(layer)
            if "W" in phases:
                load_win(layer)
            if "P" in phases:
                with ExitStack() as sc:
                    scope["es"], scope["cache"] = sc, {}
                    for si in seq_ids:
                        phase_P(layer, si)
                    T.barrier()
            with ExitStack() as sc:
                scope["es"], scope["cache"] = sc, {}
                for si in seq_ids:
                    if "T" in phases:
                        phase_T(layer, si)
                    if "O" in phases:
                        phase_O(layer, si)
                T.barrier()
        T.final_wait("sp")
        T.final_wait("pool")
    return nc


def _tables():
    pos = np.arange(S_P, dtype=np.float32)
    inv_a = (np.float32(500000.0) ** (-(np.arange(0, 16, 2, dtype=np.float32) / np.float32(16)))).astype(np.float32)
    ang = pos[:, None] * inv_a[None, :]
    ca, sa = np.cos(ang).astype(np.float32), np.sin(ang).astype(np.float32)
    tab_a = np.concatenate([np.tile(ca, (1, 8)), np.tile(sa, (1, 8))], axis=1)
    inv_b = (np.float32(10000.0) ** (-(np.arange(0, 64, 2, dtype=np.float32) / np.float32(64)))).astype(np.float32)
    rows = np.floor(pos / 64).astype(np.float32)
    cols = (pos - rows * 64).astype(np.float32)
    ar, ac = rows[:, None] * inv_b[None, :], cols[:, None] * inv_b[None, :]
    cosb = np.concatenate([np.cos(ar), np.cos(ac)], axis=1).astype(np.float32)
    sinb = np.concatenate([np.sin(ar), np.sin(ac)], axis=1).astype(np.float32)
    tab_b = np.concatenate([np.tile(cosb, (1, 4)), np.tile(sinb, (1, 4))], axis=1)
    return np.ascontiguousarray(tab_a, dtype=np.float32), np.ascontiguousarray(tab_b, dtype=np.float32)


_NC_CACHE = {}


def kernel(x_prompt, x_sample, norm_pre, norm_post, a_w_in, a_w_out, a_lam, a_subln,
           b_w_in, b_w_out, b_q_norm, b_k_norm):
    f = lambda a: np.ascontiguousarray(np.asarray(a), dtype=np.float32)
    x_prompt, x_sample = f(x_prompt), f(x_sample)
    shared = {
        "norm_pre": f(norm_pre), "norm_post": f(norm_post), "a_w_in": f(a_w_in), "a_w_out": f(a_w_out),
        "a_lam": f(a_lam), "a_subln": f(a_subln), "b_w_in": f(b_w_in), "b_w_out": f(b_w_out),
        "b_q_norm": f(b_q_norm), "b_k_norm": f(b_k_norm),
    }
    tab_a, tab_b = _tables()
    shared["tab_a"] = tab_a
    shared["tab_b"] = tab_b
    shared["ident"] = np.eye(128, dtype=np.float32).astype(ml_dtypes.bfloat16)
    if "nc" not in _NC_CACHE:
        _NC_CACHE["nc"] = build_nc()
    nc = _NC_CACHE["nc"]
    in_maps = []
    for c in range(NCORES):
        m = dict(shared)
        m["xp"] = x_prompt[2 * c:2 * c + 2]
        m["xs"] = x_sample[c:c + 1]
        in_maps.append(m)
    res = run_bass_kernel_spmd(nc, in_maps, core_ids=list(range(NCORES)))
    y_prompt = np.concatenate([np.asarray(r["yp"], dtype=np.float32) for r in res.results], axis=0)
    y_sample = np.concatenate([np.asarray(r["ys"], dtype=np.float32) for r in res.results], axis=0)
    return (y_prompt, y_sample)
```

```python
import math
from contextlib import ExitStack

import numpy as np
import ml_dtypes

import concourse.bass as bass
import concourse.mybir as mybir
from concourse.bass_utils import run_bass_kernel_spmd

F32 = mybir.dt.float32
BF16 = mybir.dt.bfloat16
AF = mybir.ActivationFunctionType
ALU = mybir.AluOpType
AX = mybir.AxisListType

D = 1024
DEPTH = 4
S_P, S_S = 4096, 2048
NCORES = 8
EPS = 1e-6
A_IN, B_IN = 4096, 2560
NTMAX = S_P // 128


class Tracker:
    COMPUTE = ("pe", "act", "dve", "pool")

    def __init__(self, nc, es, n_dma_sems=12):
        self.nc = nc
        self.eng = {"pe": nc.tensor, "act": nc.scalar, "dve": nc.vector, "pool": nc.gpsimd, "sp": nc.sync}
        self.sem = {}
        self.cnt = {}
        for e in self.COMPUTE:
            self.sem[e] = es.enter_context(nc.semaphore("s_" + e))
            self.cnt[e] = 0
        self.rings = {}
        for q in ("sp", "pool"):
            names = []
            for i in range(n_dma_sems):
                k = "d_%s%d" % (q, i)
                self.sem[k] = es.enter_context(nc.semaphore(k))
                self.cnt[k] = 0
                names.append(k)
            self.rings[q] = [names, 0]
        self.seen = {e: {} for e in self.eng}
        self.last_w = {}
        self.readers = {}

    def _wait(self, e, ticket):
        if ticket is None:
            return
        k, v = ticket
        if e == "pe" and k == "pe":
            return
        if self.seen[e].get(k, 0) >= v:
            return
        self.eng[e].wait_ge(self.sem[k], v)
        self.seen[e][k] = v

    def _deps(self, e, reads, writes):
        for r in reads:
            self._wait(e, self.last_w.get(r))
        for w in writes:
            self._wait(e, self.last_w.get(w))
            for t in self.readers.get(w, ()):
                self._wait(e, t)

    def _record(self, ticket, reads, writes):
        for w in writes:
            self.last_w[w] = ticket
            self.readers[w] = []
        for r in reads:
            self.readers.setdefault(r, []).append(ticket)

    def op(self, e, fn, reads=(), writes=(), tag=None):
        self._deps(e, reads, writes)
        ins = fn(self.eng[e])
        self.cnt[e] += 1
        ins.then_inc(self.sem[e], 1)
        t = (e, self.cnt[e])
        self._record(t, reads, writes)
        return t

    def dma(self, q, out, in_, reads=(), writes=(), tag=None):
        names, idx = self.rings[q]
        k = names[idx % len(names)]
        self.rings[q][1] = idx + 1
        if self.cnt[k] > 0:
            self._wait(q, (k, self.cnt[k]))
        self._deps(q, reads, writes)
        self.eng[q].dma_start(out=out, in_=in_).then_inc(self.sem[k], 16)
        self.cnt[k] += 16
        t = (k, self.cnt[k])
        self._record(t, reads, writes)
        return t

    def barrier(self):
        for e in self.eng:
            for k, v in self.cnt.items():
                if v > 0:
                    self._wait(e, (k, v))
        self.last_w.clear()
        self.readers.clear()

    def final_wait(self, e="sp"):
        for k, v in self.cnt.items():
            if v > 0:
                self._wait(e, (k, v))


def build_nc(layers=(0, 1, 2, 3), seq_ids=(0, 1, 2), phases="WPTO"):
    nc = bass.Bass("TRN2", target_bir_lowering=False)
    dt_in = lambda n, s, d=F32: nc.dram_tensor(n, s, d, kind="ExternalInput").ap()
    xp = dt_in("xp", [2, S_P, D])
    xs = dt_in("xs", [1, S_S, D])
    norm_pre = dt_in("norm_pre", [DEPTH, D])
    norm_post = dt_in("norm_post", [DEPTH, D])
    a_w_in = dt_in("a_w_in", [2, D, A_IN])
    a_w_out = dt_in("a_w_out", [2, D, D])
    a_lam = dt_in("a_lam", [2, 4, 64])
    a_subln = dt_in("a_subln", [2, 128])
    b_w_in = dt_in("b_w_in", [2, D, B_IN])
    b_w_out = dt_in("b_w_out", [2, D, D])
    b_q_norm = dt_in("b_q_norm", [2, 128])
    b_k_norm = dt_in("b_k_norm", [2, 128])
    tab_a = dt_in("tab_a", [S_P, 128])
    tab_b = dt_in("tab_b", [S_P, 512])
    ident_d = dt_in("ident", [128, 128], BF16)
    yp = nc.dram_tensor("yp", [2, S_P, D], F32, kind="ExternalOutput").ap()
    ys = nc.dram_tensor("ys", [1, S_S, D], F32, kind="ExternalOutput").ap()
    QT_all = nc.dram_tensor("QT_d", [3, 1024, S_P], BF16, kind="Internal").ap()
    KT_all = nc.dram_tensor("KT_d", [3, 1024, S_P], BF16, kind="Internal").ap()
    VX_all = nc.dram_tensor("VX_d", [3, 8, 128, NTMAX, 129], BF16, kind="Internal").ap()
    G_all = nc.dram_tensor("G_d", [3, 8, 128, NTMAX, 128], F32, kind="Internal").ap()

    seqs = [(xp[0], yp[0], S_P), (xp[1], yp[1], S_P), (xs[0], ys[0], S_S)]

    with ExitStack() as es:
        uniq = [0]

        def sb(n, s, d, st=es):
            uniq[0] += 1
            return st.enter_context(nc.sbuf_tensor("%s_%d" % (n, uniq[0]), s, d))
        T = Tracker(nc, es)

        BIG = sb("BIG", [128, 32768], BF16)
        WOUT = sb("WOUT", [128, 8, 1024], BF16)
        IDENT = sb("IDENT", [128, 128], BF16)
        NPRE = sb("NPRE", [128, 1024], F32)
        NPOST = sb("NPOST", [128, 1024], F32)
        SMW = sb("SMW", [128, 3, 128], F32)
        LAMB = sb("LAMB", [128, 4, 64], F32)
        LAMP = sb("LAMP", [128, 2, 64], F32)
        LAMS = sb("LAMS", [128, 8], F32)
        NEGH = sb("NEGH", [128, 4], F32)
        JK = sb("JK", [128, 1024], BF16)
        JKF = sb("JKF", [128, 128], F32)
        TPS = es.enter_context(nc.psum_tensor("TPS", [128, 1024], BF16))
        PJ = es.enter_context(nc.psum_tensor("PJ", [128, 3, 512], F32))
        ACC = es.enter_context(nc.psum_tensor("ACC", [128, 4, 512], F32))

        TPS2 = ACC[:, 3, :].bitcast(BF16)
        WINv = BIG[:].rearrange("p (k n) -> p k n", k=8)
        OGv = BIG[:].rearrange("p (t w) -> p t w", w=1024)

        T.dma("sp", IDENT[:], ident_d, writes=[("IDENT",)])
        T.op("dve", lambda e: e.memset(NEGH[:], -0.5), writes=[("NEGH",)])

        cnt = {"pj": 0, "pt": 0, "acc": 0}
        scope = {"es": None, "cache": {}}

        def lb(name, shape, dtype):
            if name not in scope["cache"]:
                scope["cache"][name] = sb(name, shape, dtype, scope["es"])
            return scope["cache"][name]

        def first(flag):
            if flag in scope["cache"]:
                return False
            scope["cache"][flag] = True
            return True

        def pj_slot():
            s = cnt["pj"] % 3
            cnt["pj"] += 1
            return s

        def load_weight(dst3, src, ncols, chunk, STG, key):
            i = 0
            for kc in range(8):
                for c0 in range(0, ncols, chunk):
                    w = min(chunk, ncols - c0)
                    sl = i % 2
                    T.dma("sp", STG[:, sl, 0:w], src[kc * 128:(kc + 1) * 128, c0:c0 + w], writes=[("STG", sl)])
                    eng = "dve" if i % 2 == 0 else "pool"
                    T.op(eng, lambda e, kc=kc, c0=c0, w=w, sl=sl: e.tensor_copy(out=dst3[:, kc, c0:c0 + w], in_=STG[:, sl, 0:w]),
                         reads=[("STG", sl)], writes=[(key, kc, c0)])
                    i += 1

        def layer_setup(layer):
            typ = "A" if layer % 2 == 0 else "B"
            j = layer // 2
            with ExitStack() as ls:
                STG = sb("STGo", [128, 2, 1024], F32, ls)
                load_weight(WOUT, (a_w_out if typ == "A" else b_w_out)[j], 1024, 1024, STG, "WOUT")
                T.dma("sp", NPRE[:], norm_pre[layer:layer + 1, :].partition_broadcast(128), writes=[("NPRE",)])
                T.dma("sp", NPOST[:], norm_post[layer:layer + 1, :].partition_broadcast(128), writes=[("NPOST",)])
                if typ == "A":
                    lam_init = 0.8 - 0.6 * math.exp(-0.3 * layer)
                    T.dma("sp", SMW[:, 2, :], a_subln[j:j + 1, :].partition_broadcast(128), writes=[("SMW", 2)])
                    T.op("dve", lambda e: e.tensor_scalar(out=SMW[:, 0, :], in0=SMW[:, 2, :], scalar1=float(1.0 - lam_init),
                                                          scalar2=None, op0=ALU.mult), reads=[("SMW", 2)], writes=[("SMW", 0)])
                    T.dma("sp", LAMB[:].rearrange("p a b -> p (a b)"),
                          a_lam[j:j + 1].rearrange("o a b -> o (a b)").partition_broadcast(128), writes=[("LAMB",)])
                    lv = LAMB[:].rearrange("p (a b) c -> p a b c", b=2)
                    T.op("dve", lambda e: e.tensor_tensor(out=LAMP[:], in0=lv[:, :, 0, :], in1=lv[:, :, 1, :], op=ALU.mult),
                         reads=[("LAMB",)], writes=[("LAMP",)])
                    T.op("dve", lambda e: e.tensor_reduce(out=LAMS[:, 0:2], in_=LAMP[:], axis=AX.X, op=ALU.add),
                         reads=[("LAMP",)], writes=[("LAMS", 0)])
                    T.op("act", lambda e: e.activation(out=LAMS[:, 2:4], in_=LAMS[:, 0:2], func=AF.Exp),
                         reads=[("LAMS", 0)], writes=[("LAMS", 2)])
                    T.op("dve", lambda e: e.scalar_tensor_tensor(out=LAMS[:, 4:5], in0=LAMS[:, 3:4], scalar=float(-lam_init),
                                                                 in1=LAMS[:, 2:3], op0=ALU.add, op1=ALU.subtract),
                         reads=[("LAMS", 2)], writes=[("LAMS", 4)])
                else:
                    T.dma("sp", SMW[:, 0, :], b_q_norm[j:j + 1, :].partition_broadcast(128), writes=[("SMW", 0)])
                    T.dma("sp", SMW[:, 1, :], b_k_norm[j:j + 1, :].partition_broadcast(128), writes=[("SMW", 1)])
                T.barrier()

        def load_win(layer):
            typ = "A" if layer % 2 == 0 else "B"
            j = layer // 2
            with ExitStack() as ls:
                STG = sb("STGi", [128, 2, 2048], F32, ls)
                if typ == "A":
                    load_weight(WINv, a_w_in[j], A_IN, 2048, STG, "WIN")
                else:
                    load_weight(WINv, b_w_in[j], B_IN, 1280, STG, "WIN")
                T.barrier()

        def phase_P(layer, si):
            typ = "A" if layer % 2 == 0 else "B"
            xin, yout, S = seqs[si]
            QT_d, KT_d, VX_d, G_d = QT_all[si], KT_all[si], VX_all[si], G_all[si]
            xsrc = xin if layer == 0 else yout
            NT = S // 128
            NG = 8 if typ == "A" else 5
            tab_d = tab_a if typ == "A" else tab_b
            tabw = 128 if typ == "A" else 512
            nkc = 8 if typ == "A" else 2
            nvh = 8 if typ == "A" else 2
            if True:
                XT = lb("XT", [128, 3, 1024], F32)
                TAB = lb("TAB", [128, 3, 512], F32)
                ST = lb("ST", [128, 3, 16], F32)
                H = lb("H", [128, 2, 1024], BF16)
                HT = lb("HT", [128, 2, 8, 128], BF16)
                QKF = lb("QKF", [128, 8, 512], BF16)
                SQ = lb("SQ", [128, 6, 512], F32)
                QN = lb("QN", [128, 6, 512], F32)
                RT = lb("RT", [128, 3, 4, 256], F32)
                ST2 = lb("ST2", [128, 6, 12], F32)
                VXS = lb("VXS", [128, 2, 8, 2, 129], BF16)
                GS = lb("GS", [128, 3, 1024], F32)
                QTS = lb("QTS", [128, 2, 8, 256], BF16)
                KTS = lb("KTS", [128, 2, 8, 256], BF16)

                if first("vxs_ones"):
                    for sl in range(2):
                        T.op("pool", lambda e, sl=sl: e.memset(VXS[:, sl, :, :, 128:129], 1.0), writes=[("VXS", sl, "ones")])

                qc = {"qkf": 0, "rt": 0, "tq": 0, "qn": 0}

                def load_x(t):
                    T.dma("sp", XT[:, t % 3, :], xsrc[t * 128:(t + 1) * 128, :], writes=[("XT", t % 3)])

                def load_tab(t):
                    T.dma("sp", TAB[:, t % 3, 0:tabw], tab_d[t * 128:(t + 1) * 128, :], writes=[("TAB", t % 3)])

                def head_stats(t):
                    x3, hs = t % 3, t % 2
                    T.op("act", lambda e: e.activation(out=JK[:], in_=XT[:, x3, :], func=AF.Square, accum_out=ST[:, x3, 0:1]),
                         reads=[("XT", x3)], writes=[("ST", x3, 0), ("JK",)])
                    T.op("dve", lambda e: e.tensor_scalar(out=ST[:, x3, 1:2], in0=ST[:, x3, 0:1], scalar1=1.0 / D, scalar2=EPS,
                                                          op0=ALU.mult, op1=ALU.add), reads=[("ST", x3, 0)], writes=[("ST", x3, 1)])
                    T.op("pool", lambda e: e.tensor_tensor(out=ST[:, x3, 2:3], in0=ST[:, x3, 1:2], in1=NEGH[:, 0:1], op=ALU.pow),
                         reads=[("ST", x3, 1), ("NEGH",)], writes=[("ST", x3, 2)])
                    T.op("dve", lambda e: e.scalar_tensor_tensor(out=H[:, hs, :], in0=XT[:, x3, :], scalar=ST[:, x3, 2:3],
                                                                 in1=NPRE[:], op0=ALU.mult, op1=ALU.mult),
                         reads=[("XT", x3), ("ST", x3, 2), ("NPRE",)], writes=[("H", hs)])

                def head_tr(t):
                    hs = t % 2
                    def th(e):
                        for kc in range(8):
                            ins = e.transpose(out=TPS[:, kc * 128:(kc + 1) * 128], in_=H[:, hs, kc * 128:(kc + 1) * 128], identity=IDENT[:])
                        return ins
                    T.op("pe", th, reads=[("H", hs), ("IDENT",)], writes=[("TPS", 0), ("TPS", 1)])
                    T.op("act", lambda e: e.activation(out=HT[:, hs, :, :], in_=TPS[:].rearrange("p (k t) -> p k t", k=8), func=AF.Copy),
                         reads=[("TPS", 0), ("TPS", 1)], writes=[("HT", hs)])

                def body(t):
                    xsl = t % 2
                    tsl = t % 3
                    tt = t % 2
                    sts = (t // 2) % 2
                    pend_tq = []

                    def mm_group(ng):
                        ps = pj_slot()
                        def f(e):
                            for kc in range(8):
                                ins = e.matmul(PJ[:, ps, :], lhsT=HT[:, xsl, kc, :], rhs=WINv[:, kc, ng * 512:(ng + 1) * 512],
                                               start=(kc == 0), stop=(kc == 7))
                            return ins
                        T.op("pe", f, reads=[("HT", xsl)], writes=[("PJ", ps)])
                        return ps

                    def emit_tq(qs, nch, dst, c0):
                        hq = qc["tq"] % 2
                        qc["tq"] += 1
                        tbuf = TPS if hq == 0 else TPS2
                        def f(e):
                            for c in range(nch):
                                ins = e.transpose(out=tbuf[:, c * 128:(c + 1) * 128],
                                                  in_=QKF[:, qs, c * 128:(c + 1) * 128], identity=IDENT[:])
                            return ins
                        T.op("pe", f, reads=[("QKF", qs, "a"), ("QKF", qs, "b"), ("QKF", qs, "c"), ("IDENT",)], writes=[("TPS", hq)])
                        dkey = "QTS" if dst is QTS else "KTS"
                        T.op("act", lambda e: e.activation(out=dst[:, sts, c0:c0 + nch, tt * 128:(tt + 1) * 128],
                                                           in_=tbuf[:, 0:nch * 128].rearrange("p (c t) -> p c t", c=nch),
                                                           func=AF.Copy),
                             reads=[("TPS", hq)], writes=[(dkey, sts, c0, tt)])

                    def post_A_qk(ng, ps):
                        qs = qc["qkf"] % 8
                        qc["qkf"] += 1
                        r = qc["rt"] % 3
                        qc["rt"] += 1
                        rq = qc["qn"] % 6
                        qc["qn"] += 1
                        pjv = QN[:, rq, :].rearrange("p (h d) -> p h d", d=64)
                        qfv = QKF[:, qs, :].rearrange("p (h d) -> p h d", d=64)
                        T.op("act", lambda e: e.activation(out=QN[:, rq, :], in_=PJ[:, ps, :], func=AF.Copy),
                             reads=[("PJ", ps)], writes=[("QN", rq)])
                        T.op("pool", lambda e: e.tensor_copy(out=qfv[:, :, 16:64], in_=pjv[:, :, 16:64]),
                             reads=[("QN", rq)], writes=[("QKF", qs, "a")])
                        cos = TAB[:, tsl, 0:64].rearrange("p (h j) -> p h j", j=8)
                        sin = TAB[:, tsl, 64:128].rearrange("p (h j) -> p h j", j=8)
                        x1, x2 = pjv[:, :, 0:8], pjv[:, :, 8:16]
                        tv = [RT[:, r, k, 0:64].rearrange("p (h j) -> p h j", j=8) for k in range(4)]
                        for k, (xa, tb) in enumerate(((x1, cos), (x2, sin), (x2, cos), (x1, sin))):
                            T.op("dve", lambda e, k=k, xa=xa, tb=tb: e.tensor_tensor(out=tv[k], in0=xa, in1=tb, op=ALU.mult),
                                 reads=[("QN", rq), ("TAB", tsl)], writes=[("RT", r, k)])
                        T.op("dve", lambda e: e.tensor_tensor(out=qfv[:, :, 0:8], in0=tv[0], in1=tv[1], op=ALU.subtract),
                             reads=[("RT", r, 0), ("RT", r, 1)], writes=[("QKF", qs, "b")])
                        T.op("dve", lambda e: e.tensor_tensor(out=qfv[:, :, 8:16], in0=tv[2], in1=tv[3], op=ALU.add),
                             reads=[("RT", r, 2), ("RT", r, 3)], writes=[("QKF", qs, "c")])
                        if ng < 2:
                            pend_tq.append((qs, 4, QTS, ng * 4))
                        else:
                            pend_tq.append((qs, 4, KTS, (ng - 2) * 4))

                    def post_B_qk(ps, col0, nh, wrow, dst, c0):
                        qs = qc["qkf"] % 8
                        qc["qkf"] += 1
                        r = qc["rt"] % 3
                        qc["rt"] += 1
                        rq = qc["qn"] % 6
                        qc["qn"] += 1
                        n = nh * 128
                        qnk = [("QN", rq, h) for h in range(nh)]
                        T.op("act", lambda e: e.activation(out=QN[:, rq, 0:n], in_=PJ[:, ps, col0:col0 + n], func=AF.Copy),
                             reads=[("PJ", ps)], writes=[("QN", rq, h) for h in range(4)])
                        T.op("act", lambda e: e.activation(out=SQ[:, rq, 0:n], in_=QN[:, rq, 0:n], func=AF.Square),
                             reads=qnk, writes=[("SQ", rq)])
                        T.op("dve", lambda e: e.tensor_reduce(out=ST2[:, rq, 0:nh], in_=SQ[:, rq, 0:n].rearrange("p (h d) -> p h d", d=128),
                                                              axis=AX.X, op=ALU.add), reads=[("SQ", rq)], writes=[("ST2", rq, 0)])
                        T.op("dve", lambda e: e.tensor_scalar(out=ST2[:, rq, 4:4 + nh], in0=ST2[:, rq, 0:nh], scalar1=1.0 / 128, scalar2=EPS,
                                                              op0=ALU.mult, op1=ALU.add), reads=[("ST2", rq, 0)], writes=[("ST2", rq, 4)])
                        T.op("pool", lambda e: e.tensor_tensor(out=ST2[:, rq, 8:8 + nh], in0=ST2[:, rq, 4:4 + nh],
                                                               in1=NEGH[:, 0:nh], op=ALU.pow),
                             reads=[("ST2", rq, 4), ("NEGH",)], writes=[("ST2", rq, 8)])
                        for h in range(nh):
                            T.op("dve", lambda e, h=h: e.scalar_tensor_tensor(
                                out=QN[:, rq, h * 128:(h + 1) * 128], in0=QN[:, rq, h * 128:(h + 1) * 128],
                                scalar=ST2[:, rq, 8 + h:9 + h], in1=SMW[:, wrow, :], op0=ALU.mult, op1=ALU.mult),
                                reads=[("QN", rq, h), ("ST2", rq, 8), ("SMW", wrow)], writes=[("QN", rq, h)])
                        qnv = QN[:, rq, 0:n].rearrange("p (h a f j) -> p h a f j", a=2, f=2, j=32)
                        qfv = QKF[:, qs, 0:n].rearrange("p (h a f j) -> p h a f j", a=2, f=2, j=32)
                        cos = TAB[:, tsl, 0:256].rearrange("p (h a j) -> p h a j", a=2, j=32)[:, 0:nh]
                        sin = TAB[:, tsl, 256:512].rearrange("p (h a j) -> p h a j", a=2, j=32)[:, 0:nh]
                        x1, x2 = qnv[:, :, :, 0, :], qnv[:, :, :, 1, :]
                        tv = [RT[:, r, k, 0:nh * 64].rearrange("p (h a j) -> p h a j", a=2, j=32) for k in range(4)]
                        for k, (xa, tb) in enumerate(((x1, cos), (x2, sin), (x2, cos), (x1, sin))):
                            T.op("dve", lambda e, k=k, xa=xa, tb=tb: e.tensor_tensor(out=tv[k], in0=xa, in1=tb, op=ALU.mult),
                                 reads=qnk + [("TAB", tsl)], writes=[("RT", r, k)])
                        T.op("dve", lambda e: e.tensor_tensor(out=qfv[:, :, :, 0, :], in0=tv[0], in1=tv[1], op=ALU.subtract),
                             reads=[("RT", r, 0), ("RT", r, 1)], writes=[("QKF", qs, "b")])
                        T.op("dve", lambda e: e.tensor_tensor(out=qfv[:, :, :, 1, :], in0=tv[2], in1=tv[3], op=ALU.add),
                             reads=[("RT", r, 2), ("RT", r, 3)], writes=[("QKF", qs, "c")])
                        pend_tq.append((qs, nh, dst, c0))

                    def post_v(ps, col0, nh, h0):
                        T.op("act", lambda e: e.activation(out=VXS[:, sts, h0:h0 + nh, tt, 0:128],
                                                           in_=PJ[:, ps, col0:col0 + nh * 128].rearrange("p (h e) -> p h e", e=128),
                                                           func=AF.Copy),
                             reads=[("PJ", ps)], writes=[("VXS", sts, tt, h0)])

                    def post_gate(ps, gc):
                        T.op("act", lambda e: e.activation(out=GS[:, t % 3, gc * 512:(gc + 1) * 512], in_=PJ[:, ps, :], func=AF.Silu),
                             reads=[("PJ", ps)], writes=[("GS", t % 3, gc)])

                    for ng in range(NG):
                        ps = mm_group(ng)
                        if typ == "A":
                            if ng < 4:
                                post_A_qk(ng, ps)
                            elif ng < 6:
                                post_v(ps, 0, 4, (ng - 4) * 4)
                            else:
                                post_gate(ps, ng - 6)
                        else:
                            if ng < 2:
                                post_B_qk(ps, 0, 4, 0, QTS, ng * 4)
                            elif ng == 2:
                                post_B_qk(ps, 0, 2, 1, KTS, 0)
                                post_v(ps, 256, 2, 0)
                            else:
                                post_gate(ps, ng - 3)
                    def emit_all_tq():
                        while pend_tq:
                            emit_tq(*pend_tq.pop(0))

                    def stores():
                        T.dma("pool", G_d[:, :, t, :].rearrange("h p e -> p h e"), GS[:, t % 3, :].rearrange("p (h e) -> p h e", e=128),
                              reads=[("GS", t % 3, 0), ("GS", t % 3, 1)], tag="g")
                        if tt == 1:
                            t0 = (t - 1) * 128
                            T.dma("pool", QT_d.rearrange("(c p) s -> p c s", p=128)[:, :, t0:t0 + 256], QTS[:, sts, :, :],
                                  reads=[("QTS", sts, c0, k) for c0 in (0, 4) for k in (0, 1)], tag="q")
                            kreads = [("KTS", sts, c0, k) for c0 in ((0, 4) if typ == "A" else (0,)) for k in (0, 1)]
                            T.dma("pool", KT_d.rearrange("(c p) s -> p c s", p=128)[:, 0:nkc, t0:t0 + 256], KTS[:, sts, 0:nkc, :], reads=kreads, tag="k")
                            vreads = [("VXS", sts, "ones")] + [("VXS", sts, k, h0) for k in (0, 1) for h0 in ((0, 4) if typ == "A" else (0,))]
                            T.dma("pool", VX_d[0:nvh, :, t - 1:t + 1, :].rearrange("h p n e -> p h (n e)"),
                                  VXS[:, sts, 0:nvh, :, :].rearrange("p h n e -> p h (n e)"), reads=vreads, tag="v")

                    return lambda: (emit_all_tq(), stores())

                for t0_ in range(min(3, NT)):
                    load_x(t0_)
                load_tab(0)
                head_stats(0)
                if NT > 1:
                    head_stats(1)
                head_tr(0)
                prev_tail = None
                for t in range(NT):
                    if t + 3 < NT:
                        load_x(t + 3)
                    if t + 1 < NT:
                        load_tab(t + 1)
                    if t + 2 < NT:
                        head_stats(t + 2)
                    tl = body(t)
                    if t + 1 < NT:
                        head_tr(t + 1)
                    if prev_tail is not None:
                        prev_tail()
                    prev_tail = tl
                prev_tail()

        def phase_T(layer, si):
            typ = "A" if layer % 2 == 0 else "B"
            _, _, S = seqs[si]
            QT_d, KT_d, VX_d, G_d = QT_all[si], KT_all[si], VX_all[si], G_all[si]
            NT = S // 128
            NKB = NT
            NQG = S // 512
            dk = 64 if typ == "A" else 128
            scale = float(dk) ** -0.5
            if True:
                KTB = lb("KTB", [128, 2, 2, S_P], BF16)
                VXB = lb("VXB", [128, 2, NTMAX, 129], BF16)
                QTB = lb("QTB", [128, 2, 512], BF16)
                GB = lb("GB", [128, 2, 4, 128], F32)
                PT = lb("PT", [128, 3, 512], BF16)
                O1 = lb("O1", [128, 4, 128], F32)
                DD = lb("DD", [128, 4, 128], F32)
                TMP = lb("TMP", [128, 2, 4, 128], F32)
                RZ = lb("RZ", [128, 2, 24], F32)

                ctr = {"kv": 0, "q": 0, "ev": 0}
                if typ == "A" and first("ktb_zero"):
                    for sl in range(2):
                        T.op("dve", lambda e, sl=sl: e.memset(KTB[64:128, sl, 0, :], 0.0), writes=[("KTB", sl, 0)])
                        T.op("pool", lambda e, sl=sl: e.memset(KTB[0:64, sl, 1, :], 0.0), writes=[("KTB", sl, 1)])

                def acc_ap(aset, qt):
                    b, jj = (2 * aset, qt) if qt < 3 else (2 * aset + 1, 0)
                    return ACC[:, b, jj * 129:(jj + 1) * 129]

                def load_kv(kchunk, vhead):
                    sl = ctr["kv"] % 2
                    ctr["kv"] += 1
                    if typ == "A":
                        T.dma("sp", KTB[0:64, sl, 0, 0:S], KT_d[kchunk * 128:kchunk * 128 + 64, 0:S], writes=[("KTB", sl, 0)])
                        T.dma("sp", KTB[64:128, sl, 1, 0:S], KT_d[kchunk * 128 + 64:(kchunk + 1) * 128, 0:S], writes=[("KTB", sl, 1)])
                    else:
                        T.dma("sp", KTB[:, sl, 0, 0:S], KT_d[kchunk * 128:(kchunk + 1) * 128, 0:S], writes=[("KTB", sl, 0)])
                    T.dma("sp", VXB[:, sl, 0:NT, :], VX_d[vhead, :, 0:NT, :], writes=[("VXB", sl)])
                    return sl

                def load_q(qchunk, qg, h):
                    sl = ctr["q"] % 2
                    ctr["q"] += 1
                    T.dma("sp", QTB[:, sl, :], QT_d[qchunk * 128:(qchunk + 1) * 128, qg * 512:(qg + 1) * 512], writes=[("QTB", sl)])
                    T.dma("sp", GB[:, sl, :, :], G_d[h, :, qg * 4:(qg + 1) * 4, :], writes=[("GB", sl)])
                    return sl

                def attn_unit(kvs, qs, p0):
                    aset = cnt["acc"] % 2
                    cnt["acc"] += 1
                    slots = {}

                    def qk(kb):
                        ps = pj_slot()
                        slots[kb] = ps
                        mp = p0 // 64
                        T.op("pe", lambda e: e.matmul(PJ[:, ps, :], lhsT=KTB[:, kvs, mp, kb * 128:(kb + 1) * 128],
                                                      rhs=QTB[:, qs, :], start=True, stop=True),
                             reads=[("KTB", kvs, mp), ("QTB", qs)], writes=[("PJ", ps)])
                        pts = cnt["pt"] % 3
                        cnt["pt"] += 1
                        T.op("act", lambda e: e.activation(out=PT[:, pts, :], in_=PJ[:, ps, :], func=AF.Exp, scale=scale),
                             reads=[("PJ", ps)], writes=[("PT", pts)])
                        slots[kb] = pts

                    def pv(kb):
                        pts = slots.pop(kb)
                        def f(e):
                            for qt in range(4):
                                ins = e.matmul(acc_ap(aset, qt), lhsT=PT[:, pts, qt * 128:(qt + 1) * 128], rhs=VXB[:, kvs, kb, :],
                                               start=(kb == 0 and qt in (0, 3)), stop=(kb == NKB - 1), skip_group_check=True)
                            return ins
                        T.op("pe", f, reads=[("PT", pts), ("VXB", kvs)], writes=[("ACC", aset)])

                    qk(0)
                    qk(1)
                    for kb in range(NKB):
                        if kb + 2 < NKB:
                            qk(kb + 2)
                        pv(kb)
                    return aset

                def recip_z(aset, r):
                    zv = ACC[:, 2 * aset, 0:387].rearrange("p (a b) -> p a b", b=129)[:, :, 128:129]
                    T.op("dve", lambda e: e.reciprocal(out=RZ[:, r, 0:3].rearrange("p (a b) -> p a b", b=1), in_=zv),
                         reads=[("ACC", aset)], writes=[("RZ", r, 0)])
                    T.op("dve", lambda e: e.reciprocal(out=RZ[:, r, 3:4], in_=ACC[:, 2 * aset + 1, 128:129]),
                         reads=[("ACC", aset)], writes=[("RZ", r, 3)])

                def og_out(h, qg, r, gsl):
                    ogv = OGv[:, qg * 4:(qg + 1) * 4, h * 128:(h + 1) * 128]
                    T.op("pool", lambda e: e.tensor_tensor(out=ogv, in0=TMP[:, r, :, :], in1=GB[:, gsl, :, :], op=ALU.mult),
                         reads=[("TMP", r, q) for q in range(4)] + [("GB", gsl)], writes=[("OG", h, qg)])

                def evac_B(aset, h, qg, gsl):
                    r = ctr["ev"] % 2
                    ctr["ev"] += 1
                    recip_z(aset, r)
                    for qt in range(4):
                        T.op("dve", lambda e, qt=qt: e.tensor_scalar(out=TMP[:, r, qt, :], in0=acc_ap(aset, qt)[:, 0:128],
                                                                      scalar1=RZ[:, r, qt:qt + 1], scalar2=None, op0=ALU.mult),
                             reads=[("ACC", aset), ("RZ", r, 0), ("RZ", r, 3)], writes=[("TMP", r, qt)])
                    og_out(h, qg, r, gsl)

                def evac_A0(aset):
                    r = ctr["ev"] % 2
                    ctr["ev"] += 1
                    recip_z(aset, r)
                    for qt in range(4):
                        T.op("dve", lambda e, qt=qt: e.tensor_scalar(out=O1[:, qt, :], in0=acc_ap(aset, qt)[:, 0:128],
                                                                      scalar1=RZ[:, r, qt:qt + 1], scalar2=None, op0=ALU.mult),
                             reads=[("ACC", aset), ("RZ", r, 0), ("RZ", r, 3)], writes=[("O1", qt)])

                def evac_A1(aset, h, qg, gsl):
                    r = ctr["ev"] % 2
                    ctr["ev"] += 1
                    recip_z(aset, r)
                    T.op("dve", lambda e: e.tensor_scalar(out=RZ[:, r, 4:8], in0=RZ[:, r, 0:4], scalar1=LAMS[:, 4:5], scalar2=None, op0=ALU.mult),
                         reads=[("RZ", r, 0), ("RZ", r, 3), ("LAMS", 4)], writes=[("RZ", r, 4)])
                    for qt in range(4):
                        T.op("dve", lambda e, qt=qt: e.scalar_tensor_tensor(out=DD[:, qt, :], in0=acc_ap(aset, qt)[:, 0:128],
                                                                             scalar=RZ[:, r, 4 + qt:5 + qt], in1=O1[:, qt, :],
                                                                             op0=ALU.mult, op1=ALU.add),
                             reads=[("ACC", aset), ("RZ", r, 4), ("O1", qt)], writes=[("DD", qt)])
                    for qt in range(4):
                        T.op("dve", lambda e, qt=qt: e.scalar_tensor_tensor(out=JKF[:], in0=DD[:, qt, :], scalar=1.0, in1=DD[:, qt, :],
                                                                             op0=ALU.mult, op1=ALU.mult, accum_out=RZ[:, r, 8 + qt:9 + qt]),
                             reads=[("DD", qt)], writes=[("RZ", r, 8 + qt), ("JKF",)])
                    T.op("dve", lambda e: e.tensor_scalar(out=RZ[:, r, 12:16], in0=RZ[:, r, 8:12], scalar1=1.0 / 128, scalar2=EPS,
                                                          op0=ALU.mult, op1=ALU.add),
                         reads=[("RZ", r, 8 + q) for q in range(4)], writes=[("RZ", r, 12)])
                    T.op("pool", lambda e: e.tensor_tensor(out=RZ[:, r, 16:20], in0=RZ[:, r, 12:16], in1=NEGH[:, 0:4],
                                                           op=ALU.pow), reads=[("RZ", r, 12), ("NEGH",)], writes=[("RZ", r, 16)])
                    for qt in range(4):
                        T.op("dve", lambda e, qt=qt: e.scalar_tensor_tensor(out=TMP[:, r, qt, :], in0=DD[:, qt, :],
                                                                             scalar=RZ[:, r, 16 + qt:17 + qt], in1=SMW[:, 0, :],
                                                                             op0=ALU.mult, op1=ALU.mult),
                             reads=[("DD", qt), ("RZ", r, 16), ("SMW", 0)], writes=[("TMP", r, qt)])
                    og_out(h, qg, r, gsl)

                if typ == "A":
                    heads = [(h, h, h) for h in range(8)]
                else:
                    heads = [(h, h // 4, h // 4) for h in range(8)]
                items = [(h, kch, vh, qg) for (h, kch, vh) in heads for qg in range(NQG)]
                kv_list = []
                for (h, kch, vh) in heads:
                    if not kv_list or kv_list[-1] != (kch, vh):
                        kv_list.append((kch, vh))
                kv_idx = 0
                kvs = load_kv(*kv_list[0])
                nxt_kvs = load_kv(*kv_list[1]) if len(kv_list) > 1 else None
                q_next = load_q(items[0][0], items[0][3], items[0][0])
                for i, (h, kch, vh, qg) in enumerate(items):
                    if (kch, vh) != kv_list[kv_idx]:
                        kv_idx += 1
                        kvs = nxt_kvs
                        if kv_idx + 1 < len(kv_list):
                            nxt_kvs = load_kv(*kv_list[kv_idx + 1])
                    cur = q_next
                    if i + 1 < len(items):
                        q_next = load_q(items[i + 1][0], items[i + 1][3], items[i + 1][0])
                    if typ == "A":
                        a0 = attn_unit(kvs, cur, 0)
                        evac_A0(a0)
                        a1 = attn_unit(kvs, cur, 64)
                        evac_A1(a1, h, qg, cur)
                    else:
                        a0 = attn_unit(kvs, cur, 0)
                        evac_B(a0, h, qg, cur)

        def phase_O(layer, si):
            xin, yout, S = seqs[si]
            xsrc = xin if layer == 0 else yout
            NT = S // 128
            if True:
                XT = lb("XTo", [128, 2, 1024], F32)
                OGT = lb("OGT", [128, 2, 8, 128], BF16)
                M = lb("M", [128, 3, 1024], F32)
                ST = lb("STo", [128, 3, 8], F32)
                T.dma("sp", XT[:, 0, :], xsrc[0:128, :], writes=[("XT", 0)])
                def th_copy(t):
                    sl = t % 2
                    def th(e):
                        for wc in range(8):
                            ins = e.transpose(out=TPS[:, wc * 128:(wc + 1) * 128], in_=OGv[:, t, wc * 128:(wc + 1) * 128], identity=IDENT[:])
                        return ins
                    T.op("pe", th, reads=[("IDENT",)] + [("OG", hh, t // 4) for hh in range(8)], writes=[("TPS", 0), ("TPS", 1)])
                    T.op("act", lambda e: e.activation(out=OGT[:, sl, :, :], in_=TPS[:].rearrange("p (k t) -> p k t", k=8), func=AF.Copy),
                         reads=[("TPS", 0), ("TPS", 1)], writes=[("OGT", sl)])

                th_copy(0)
                for t in range(NT):
                    sl = t % 2
                    msl = t % 3
                    if t + 1 < NT:
                        T.dma("sp", XT[:, (t + 1) % 2, :], xsrc[(t + 1) * 128:(t + 2) * 128, :], writes=[("XT", (t + 1) % 2)])
                        th_copy(t + 1)
                    for ng in range(2):
                        ps = pj_slot()
                        def f(e, ng=ng, ps=ps):
                            for wc in range(8):
                                ins = e.matmul(PJ[:, ps, :], lhsT=OGT[:, sl, wc, :], rhs=WOUT[:, wc, ng * 512:(ng + 1) * 512],
                                               start=(wc == 0), stop=(wc == 7))
                            return ins
                        T.op("pe", f, reads=[("OGT", sl)], writes=[("PJ", ps)])
                        T.op("act", lambda e, ng=ng, ps=ps: e.activation(out=M[:, msl, ng * 512:(ng + 1) * 512], in_=PJ[:, ps, :], func=AF.Copy),
                             reads=[("PJ", ps)], writes=[("M", msl, ng)])
                        T.op("act", lambda e, ng=ng: e.activation(out=JK[:, 0:512], in_=M[:, msl, ng * 512:(ng + 1) * 512], func=AF.Square,
                                                                   accum_out=ST[:, msl, ng:ng + 1]),
                             reads=[("M", msl, ng)], writes=[("ST", msl, ng), ("JK",)])
                    T.op("dve", lambda e: e.tensor_tensor(out=ST[:, msl, 2:3], in0=ST[:, msl, 0:1], in1=ST[:, msl, 1:2], op=ALU.add),
                         reads=[("ST", msl, 0), ("ST", msl, 1)], writes=[("ST", msl, 2)])
                    T.op("dve", lambda e: e.tensor_scalar(out=ST[:, msl, 3:4], in0=ST[:, msl, 2:3], scalar1=1.0 / D, scalar2=EPS,
                                                          op0=ALU.mult, op1=ALU.add), reads=[("ST", msl, 2)], writes=[("ST", msl, 3)])
                    T.op("pool", lambda e: e.tensor_tensor(out=ST[:, msl, 4:5], in0=ST[:, msl, 3:4], in1=NEGH[:, 0:1], op=ALU.pow),
                         reads=[("ST", msl, 3), ("NEGH",)], writes=[("ST", msl, 4)])
                    T.op("dve", lambda e: e.scalar_tensor_tensor(out=M[:, msl, :], in0=M[:, msl, :], scalar=ST[:, msl, 4:5], in1=NPOST[:],
                                                                 op0=ALU.mult, op1=ALU.mult),
                         reads=[("M", msl, 0), ("M", msl, 1), ("ST", msl, 4), ("NPOST",)], writes=[("M", msl, 0), ("M", msl, 1)])
                    T.op("pool", lambda e: e.tensor_tensor(out=M[:, msl, :], in0=M[:, msl, :], in1=XT[:, sl, :], op=ALU.add),
                         reads=[("M", msl, 0), ("M", msl, 1), ("XT", sl)], writes=[("M", msl, 0), ("M", msl, 1)])
                    T.dma("pool", yout[t * 128:(t + 1) * 128, :], M[:, msl, :], reads=[("M", msl, 0), ("M", msl, 1)])

        T.barrier()
        for layer in layers:
            layer_setup(layer)
            if "W" in phases:
                load_win(layer)
            if "P" in phases:
                with ExitStack() as sc:
                    scope["es"], scope["cache"] = sc, {}
                    for si in seq_ids:
                        phase_P(layer, si)
                    T.barrier()
            with ExitStack() as sc:
                scope["es"], scope["cache"] = sc, {}
                for si in seq_ids:
                    if "T" in phases:
                        phase_T(layer, si)
                    if "O" in phases:
                        phase_O(layer, si)
                T.barrier()
        T.final_wait("sp")
        T.final_wait("pool")
    return nc


def _tables():
    pos = np.arange(S_P, dtype=np.float32)
    inv_a = (np.float32(500000.0) ** (-(np.arange(0, 16, 2, dtype=np.float32) / np.float32(16)))).astype(np.float32)
    ang = pos[:, None] * inv_a[None, :]
    ca, sa = np.cos(ang).astype(np.float32), np.sin(ang).astype(np.float32)
    tab_a = np.concatenate([np.tile(ca, (1, 8)), np.tile(sa, (1, 8))], axis=1)
    inv_b = (np.float32(10000.0) ** (-(np.arange(0, 64, 2, dtype=np.float32) / np.float32(64)))).astype(np.float32)
    rows = np.floor(pos / 64).astype(np.float32)
    cols = (pos - rows * 64).astype(np.float32)
    ar, ac = rows[:, None] * inv_b[None, :], cols[:, None] * inv_b[None, :]
    cosb = np.concatenate([np.cos(ar), np.cos(ac)], axis=1).astype(np.float32)
    sinb = np.concatenate([np.sin(ar), np.sin(ac)], axis=1).astype(np.float32)
    tab_b = np.concatenate([np.tile(cosb, (1, 4)), np.tile(sinb, (1, 4))], axis=1)
    return np.ascontiguousarray(tab_a, dtype=np.float32), np.ascontiguousarray(tab_b, dtype=np.float32)


_NC_CACHE = {}


def kernel(x_prompt, x_sample, norm_pre, norm_post, a_w_in, a_w_out, a_lam, a_subln,
           b_w_in, b_w_out, b_q_norm, b_k_norm):
    f = lambda a: np.ascontiguousarray(np.asarray(a), dtype=np.float32)
    x_prompt, x_sample = f(x_prompt), f(x_sample)
    shared = {
        "norm_pre": f(norm_pre), "norm_post": f(norm_post), "a_w_in": f(a_w_in), "a_w_out": f(a_w_out),
        "a_lam": f(a_lam), "a_subln": f(a_subln), "b_w_in": f(b_w_in), "b_w_out": f(b_w_out),
        "b_q_norm": f(b_q_norm), "b_k_norm": f(b_k_norm),
    }
    tab_a, tab_b = _tables()
    shared["tab_a"] = tab_a
    shared["tab_b"] = tab_b
    shared["ident"] = np.eye(128, dtype=np.float32).astype(ml_dtypes.bfloat16)
    if "nc" not in _NC_CACHE:
        _NC_CACHE["nc"] = build_nc()
    nc = _NC_CACHE["nc"]
    in_maps = []
    for c in range(NCORES):
        m = dict(shared)
        m["xp"] = x_prompt[2 * c:2 * c + 2]
        m["xs"] = x_sample[c:c + 1]
        in_maps.append(m)
    res = run_bass_kernel_spmd(nc, in_maps, core_ids=list(range(NCORES)))
    y_prompt = np.concatenate([np.asarray(r["yp"], dtype=np.float32) for r in res.results], axis=0)
    y_sample = np.concatenate([np.asarray(r["ys"], dtype=np.float32) for r in res.results], axis=0)
    return (y_prompt, y_sample)
```

```python
import math
from contextlib import ExitStack

import numpy as np
import ml_dtypes

import concourse.bass as bass
import concourse.mybir as mybir
from concourse.bass_utils import run_bass_kernel_spmd

F32 = mybir.dt.float32
BF16 = mybir.dt.bfloat16
AF = mybir.ActivationFunctionType
ALU = mybir.AluOpType
AX = mybir.AxisListType

D = 1024
DEPTH = 4
S_P, S_S = 4096, 2048
NCORES = 8
EPS = 1e-6
A_IN, B_IN = 4096, 2560
NTMAX = S_P // 128


class Tracker:
    COMPUTE = ("pe", "act", "dve", "pool")

    def __init__(self, nc, es, n_dma_sems=12):
        self.nc = nc
        self.eng = {"pe": nc.tensor, "act": nc.scalar, "dve": nc.vector, "pool": nc.gpsimd, "sp": nc.sync}
        self.sem = {}
        self.cnt = {}
        for e in self.COMPUTE:
            self.sem[e] = es.enter_context(nc.semaphore("s_" + e))
            self.cnt[e] = 0
        self.rings = {}
        for q in ("sp", "pool"):
            names = []
            for i in range(n_dma_sems):
                k = "d_%s%d" % (q, i)
                self.sem[k] = es.enter_context(nc.semaphore(k))
                self.cnt[k] = 0
                names.append(k)
            self.rings[q] = [names, 0]
        self.seen = {e: {} for e in self.eng}
        self.last_w = {}
        self.readers = {}

    def _wait(self, e, ticket):
        if ticket is None:
            return
        k, v = ticket
        if e == "pe" and k == "pe":
            return
        if self.seen[e].get(k, 0) >= v:
            return
        self.eng[e].wait_ge(self.sem[k], v)
        self.seen[e][k] = v

    def _deps(self, e, reads, writes):
        for r in reads:
            self._wait(e, self.last_w.get(r))
        for w in writes:
            self._wait(e, self.last_w.get(w))
            for t in self.readers.get(w, ()):
                self._wait(e, t)

    def _record(self, ticket, reads, writes):
        for w in writes:
            self.last_w[w] = ticket
            self.readers[w] = []
        for r in reads:
            self.readers.setdefault(r, []).append(ticket)

    def op(self, e, fn, reads=(), writes=(), tag=None):
        self._deps(e, reads, writes)
        ins = fn(self.eng[e])
        self.cnt[e] += 1
        ins.then_inc(self.sem[e], 1)
        t = (e, self.cnt[e])
        self._record(t, reads, writes)
        return t

    def dma(self, q, out, in_, reads=(), writes=(), tag=None):
        names, idx = self.rings[q]
        k = names[idx % len(names)]
        self.rings[q][1] = idx + 1
        if self.cnt[k] > 0:
            self._wait(q, (k, self.cnt[k]))
        self._deps(q, reads, writes)
        self.eng[q].dma_start(out=out, in_=in_).then_inc(self.sem[k], 16)
        self.cnt[k] += 16
        t = (k, self.cnt[k])
        self._record(t, reads, writes)
        return t

    def barrier(self):
        for e in self.eng:
            for k, v in self.cnt.items():
                if v > 0:
                    self._wait(e, (k, v))
        self.last_w.clear()
        self.readers.clear()

    def final_wait(self, e="sp"):
        for k, v in self.cnt.items():
            if v > 0:
                self._wait(e, (k, v))


def build_nc(layers=(0, 1, 2, 3), seq_ids=(0, 1, 2), phases="WPTO"):
    nc = bass.Bass("TRN2", target_bir_lowering=False)
    dt_in = lambda n, s, d=F32: nc.dram_tensor(n, s, d, kind="ExternalInput").ap()
    xp = dt_in("xp", [2, S_P, D])
    xs = dt_in("xs", [1, S_S, D])
    norm_pre = dt_in("norm_pre", [DEPTH, D])
    norm_post = dt_in("norm_post", [DEPTH, D])
    a_w_in = dt_in("a_w_in", [2, D, A_IN])
    a_w_out = dt_in("a_w_out", [2, D, D])
    a_lam = dt_in("a_lam", [2, 4, 64])
    a_subln = dt_in("a_subln", [2, 128])
    b_w_in = dt_in("b_w_in", [2, D, B_IN])
    b_w_out = dt_in("b_w_out", [2, D, D])
    b_q_norm = dt_in("b_q_norm", [2, 128])
    b_k_norm = dt_in("b_k_norm", [2, 128])
    tab_a = dt_in("tab_a", [S_P, 128])
    tab_b = dt_in("tab_b", [S_P, 512])
    ident_d = dt_in("ident", [128, 128], BF16)
    yp = nc.dram_tensor("yp", [2, S_P, D], F32, kind="ExternalOutput").ap()
    ys = nc.dram_tensor("ys", [1, S_S, D], F32, kind="ExternalOutput").ap()
    QT_all = nc.dram_tensor("QT_d", [3, 1024, S_P], BF16, kind="Internal").ap()
    KT_all = nc.dram_tensor("KT_d", [3, 1024, S_P], BF16, kind="Internal").ap()
    VX_all = nc.dram_tensor("VX_d", [3, 8, 128, NTMAX, 129], BF16, kind="Internal").ap()
    G_all = nc.dram_tensor("G_d", [3, 8, 128, NTMAX, 128], F32, kind="Internal").ap()

    seqs = [(xp[0], yp[0], S_P), (xp[1], yp[1], S_P), (xs[0], ys[0], S_S)]

    with ExitStack() as es:
        uniq = [0]

        def sb(n, s, d, st=es):
            uniq[0] += 1
            return st.enter_context(nc.sbuf_tensor("%s_%d" % (n, uniq[0]), s, d))
        T = Tracker(nc, es)

        BIG = sb("BIG", [128, 32768], BF16)
        WOUT = sb("WOUT", [128, 8, 1024], BF16)
        IDENT = sb("IDENT", [128, 128], BF16)
        NPRE = sb("NPRE", [128, 1024], F32)
        NPOST = sb("NPOST", [128, 1024], F32)
        SMW = sb("SMW", [128, 3, 128], F32)
        LAMB = sb("LAMB", [128, 4, 64], F32)
        LAMP = sb("LAMP", [128, 2, 64], F32)
        LAMS = sb("LAMS", [128, 8], F32)
        NEGH = sb("NEGH", [128, 4], F32)
        JK = sb("JK", [128, 1024], BF16)
        JKF = sb("JKF", [128, 128], F32)
        TPS = es.enter_context(nc.psum_tensor("TPS", [128, 1024], BF16))
        PJ = es.enter_context(nc.psum_tensor("PJ", [128, 3, 512], F32))
        ACC = es.enter_context(nc.psum_tensor("ACC", [128, 4, 512], F32))

        TPS2 = ACC[:, 3, :].bitcast(BF16)
        WINv = BIG[:].rearrange("p (k n) -> p k n", k=8)
        OGv = BIG[:].rearrange("p (t w) -> p t w", w=1024)

        T.dma("sp", IDENT[:], ident_d, writes=[("IDENT",)])
        T.op("dve", lambda e: e.memset(NEGH[:], -0.5), writes=[("NEGH",)])

        cnt = {"pj": 0, "pt": 0, "acc": 0}
        scope = {"es": None, "cache": {}}

        def lb(name, shape, dtype):
            if name not in scope["cache"]:
                scope["cache"][name] = sb(name, shape, dtype, scope["es"])
            return scope["cache"][name]

        def first(flag):
            if flag in scope["cache"]:
                return False
            scope["cache"][flag] = True
            return True

        def pj_slot():
            s = cnt["pj"] % 3
            cnt["pj"] += 1
            return s

        def load_weight(dst3, src, ncols, chunk, STG, key, nsl=2):
            i = 0
            for kc in range(8):
                for c0 in range(0, ncols, chunk):
                    w = min(chunk, ncols - c0)
                    sl = i % nsl
                    T.dma("sp", STG[:, sl, 0:w], src[kc * 128:(kc + 1) * 128, c0:c0 + w], writes=[("STG", sl)])
                    eng = ("dve", "act", "pool")[i % 3] if nsl > 2 else ("dve" if i % 2 == 0 else "pool")
                    if eng == "act":
                        T.op("act", lambda e, kc=kc, c0=c0, w=w, sl=sl: e.activation(out=dst3[:, kc, c0:c0 + w], in_=STG[:, sl, 0:w], func=AF.Copy),
                             reads=[("STG", sl)], writes=[(key, kc, c0)])
                    else:
                        T.op(eng, lambda e, kc=kc, c0=c0, w=w, sl=sl: e.tensor_copy(out=dst3[:, kc, c0:c0 + w], in_=STG[:, sl, 0:w]),
                             reads=[("STG", sl)], writes=[(key, kc, c0)])
                    i += 1

        def layer_setup(layer):
            typ = "A" if layer % 2 == 0 else "B"
            j = layer // 2
            with ExitStack() as ls:
                STG = sb("STGo", [128, 2, 1024], F32, ls)
                load_weight(WOUT, (a_w_out if typ == "A" else b_w_out)[j], 1024, 1024, STG, "WOUT")
                T.dma("sp", NPRE[:], norm_pre[layer:layer + 1, :].partition_broadcast(128), writes=[("NPRE",)])
                T.dma("sp", NPOST[:], norm_post[layer:layer + 1, :].partition_broadcast(128), writes=[("NPOST",)])
                if typ == "A":
                    lam_init = 0.8 - 0.6 * math.exp(-0.3 * layer)
                    T.dma("sp", SMW[:, 2, :], a_subln[j:j + 1, :].partition_broadcast(128), writes=[("SMW", 2)])
                    T.op("dve", lambda e: e.tensor_scalar(out=SMW[:, 0, :], in0=SMW[:, 2, :], scalar1=float(1.0 - lam_init),
                                                          scalar2=None, op0=ALU.mult), reads=[("SMW", 2)], writes=[("SMW", 0)])
                    T.dma("sp", LAMB[:].rearrange("p a b -> p (a b)"),
                          a_lam[j:j + 1].rearrange("o a b -> o (a b)").partition_broadcast(128), writes=[("LAMB",)])
                    lv = LAMB[:].rearrange("p (a b) c -> p a b c", b=2)
                    T.op("dve", lambda e: e.tensor_tensor(out=LAMP[:], in0=lv[:, :, 0, :], in1=lv[:, :, 1, :], op=ALU.mult),
                         reads=[("LAMB",)], writes=[("LAMP",)])
                    T.op("dve", lambda e: e.tensor_reduce(out=LAMS[:, 0:2], in_=LAMP[:], axis=AX.X, op=ALU.add),
                         reads=[("LAMP",)], writes=[("LAMS", 0)])
                    T.op("act", lambda e: e.activation(out=LAMS[:, 2:4], in_=LAMS[:, 0:2], func=AF.Exp),
                         reads=[("LAMS", 0)], writes=[("LAMS", 2)])
                    T.op("dve", lambda e: e.scalar_tensor_tensor(out=LAMS[:, 4:5], in0=LAMS[:, 3:4], scalar=float(-lam_init),
                                                                 in1=LAMS[:, 2:3], op0=ALU.add, op1=ALU.subtract),
                         reads=[("LAMS", 2)], writes=[("LAMS", 4)])
                else:
                    T.dma("sp", SMW[:, 0, :], b_q_norm[j:j + 1, :].partition_broadcast(128), writes=[("SMW", 0)])
                    T.dma("sp", SMW[:, 1, :], b_k_norm[j:j + 1, :].partition_broadcast(128), writes=[("SMW", 1)])
                T.barrier()

        def load_win(layer):
            typ = "A" if layer % 2 == 0 else "B"
            j = layer // 2
            with ExitStack() as ls:
                STG = sb("STGi", [128, 4, 2048], F32, ls)
                if typ == "A":
                    load_weight(WINv, a_w_in[j], A_IN, 2048, STG, "WIN", nsl=4)
                else:
                    load_weight(WINv, b_w_in[j], B_IN, 1280, STG, "WIN", nsl=4)
                T.barrier()

        def phase_P(layer, si):
            typ = "A" if layer % 2 == 0 else "B"
            xin, yout, S = seqs[si]
            QT_d, KT_d, VX_d, G_d = QT_all[si], KT_all[si], VX_all[si], G_all[si]
            xsrc = xin if layer == 0 else yout
            NT = S // 128
            NG = 8 if typ == "A" else 5
            tab_d = tab_a if typ == "A" else tab_b
            tabw = 128 if typ == "A" else 512
            nkc = 8 if typ == "A" else 2
            nvh = 8 if typ == "A" else 2
            if True:
                XT = lb("XT", [128, 3, 1024], F32)
                TAB = lb("TAB", [128, 3, 512], F32)
                ST = lb("ST", [128, 3, 16], F32)
                H = lb("H", [128, 2, 1024], BF16)
                HT = lb("HT", [128, 2, 8, 128], BF16)
                QKF = lb("QKF", [128, 8, 512], BF16)
                SQ = lb("SQ", [128, 6, 512], F32)
                QN = lb("QN", [128, 6, 512], F32)
                RT = lb("RT", [128, 3, 4, 256], F32)
                ST2 = lb("ST2", [128, 6, 12], F32)
                VXS = lb("VXS", [128, 2, 8, 2, 129], BF16)
                GS = lb("GS", [128, 3, 1024], F32)
                QTS = lb("QTS", [128, 2, 8, 256], BF16)
                KTS = lb("KTS", [128, 2, 8, 256], BF16)

                if first("vxs_ones"):
                    for sl in range(2):
                        T.op("pool", lambda e, sl=sl: e.memset(VXS[:, sl, :, :, 128:129], 1.0), writes=[("VXS", sl, "ones")])

                qc = {"qkf": 0, "rt": 0, "tq": 0, "qn": 0}

                def load_x(t):
                    T.dma("sp", XT[:, t % 3, :], xsrc[t * 128:(t + 1) * 128, :], writes=[("XT", t % 3)])

                def load_tab(t):
                    T.dma("sp", TAB[:, t % 3, 0:tabw], tab_d[t * 128:(t + 1) * 128, :], writes=[("TAB", t % 3)])

                def head_stats(t):
                    x3, hs = t % 3, t % 2
                    T.op("act", lambda e: e.activation(out=JK[:], in_=XT[:, x3, :], func=AF.Square, accum_out=ST[:, x3, 0:1]),
                         reads=[("XT", x3)], writes=[("ST", x3, 0), ("JK",)])
                    T.op("dve", lambda e: e.tensor_scalar(out=ST[:, x3, 1:2], in0=ST[:, x3, 0:1], scalar1=1.0 / D, scalar2=EPS,
                                                          op0=ALU.mult, op1=ALU.add), reads=[("ST", x3, 0)], writes=[("ST", x3, 1)])
                    T.op("pool", lambda e: e.tensor_tensor(out=ST[:, x3, 2:3], in0=ST[:, x3, 1:2], in1=NEGH[:, 0:1], op=ALU.pow),
                         reads=[("ST", x3, 1), ("NEGH",)], writes=[("ST", x3, 2)])
                    T.op("dve", lambda e: e.scalar_tensor_tensor(out=H[:, hs, :], in0=XT[:, x3, :], scalar=ST[:, x3, 2:3],
                                                                 in1=NPRE[:], op0=ALU.mult, op1=ALU.mult),
                         reads=[("XT", x3), ("ST", x3, 2), ("NPRE",)], writes=[("H", hs)])

                def head_tr(t):
                    hs = t % 2
                    def th(e):
                        for kc in range(8):
                            ins = e.transpose(out=TPS[:, kc * 128:(kc + 1) * 128], in_=H[:, hs, kc * 128:(kc + 1) * 128], identity=IDENT[:])
                        return ins
                    T.op("pe", th, reads=[("H", hs), ("IDENT",)], writes=[("TPS", 0), ("TPS", 1)])
                    T.op("act", lambda e: e.activation(out=HT[:, hs, :, :], in_=TPS[:].rearrange("p (k t) -> p k t", k=8), func=AF.Copy),
                         reads=[("TPS", 0), ("TPS", 1)], writes=[("HT", hs)])

                def body(t):
                    xsl = t % 2
                    tsl = t % 3
                    tt = t % 2
                    sts = (t // 2) % 2
                    pend_tq = []

                    def mm_group(ng):
                        ps = pj_slot()
                        def f(e):
                            for kc in range(8):
                                ins = e.matmul(PJ[:, ps, :], lhsT=HT[:, xsl, kc, :], rhs=WINv[:, kc, ng * 512:(ng + 1) * 512],
                                               start=(kc == 0), stop=(kc == 7))
                            return ins
                        T.op("pe", f, reads=[("HT", xsl)], writes=[("PJ", ps)])
                        return ps

                    def emit_tq(qs, nch, dst, c0):
                        hq = qc["tq"] % 2
                        qc["tq"] += 1
                        tbuf = TPS if hq == 0 else TPS2
                        def f(e):
                            for c in range(nch):
                                ins = e.transpose(out=tbuf[:, c * 128:(c + 1) * 128],
                                                  in_=QKF[:, qs, c * 128:(c + 1) * 128], identity=IDENT[:])
                            return ins
                        T.op("pe", f, reads=[("QKF", qs, "a"), ("QKF", qs, "b"), ("QKF", qs, "c"), ("IDENT",)], writes=[("TPS", hq)])
                        dkey = "QTS" if dst is QTS else "KTS"
                        T.op("act", lambda e: e.activation(out=dst[:, sts, c0:c0 + nch, tt * 128:(tt + 1) * 128],
                                                           in_=tbuf[:, 0:nch * 128].rearrange("p (c t) -> p c t", c=nch),
                                                           func=AF.Copy),
                             reads=[("TPS", hq)], writes=[(dkey, sts, c0, tt)])

                    def post_A_qk(ng, ps):
                        qs = qc["qkf"] % 8
                        qc["qkf"] += 1
                        r = qc["rt"] % 3
                        qc["rt"] += 1
                        rq = qc["qn"] % 6
                        qc["qn"] += 1
                        pjv = QN[:, rq, :].rearrange("p (h d) -> p h d", d=64)
                        qfv = QKF[:, qs, :].rearrange("p (h d) -> p h d", d=64)
                        T.op("act", lambda e: e.activation(out=QN[:, rq, :], in_=PJ[:, ps, :], func=AF.Copy),
                             reads=[("PJ", ps)], writes=[("QN", rq)])
                        T.op("pool", lambda e: e.tensor_copy(out=qfv[:, :, 16:64], in_=pjv[:, :, 16:64]),
                             reads=[("QN", rq)], writes=[("QKF", qs, "a")])
                        cos = TAB[:, tsl, 0:64].rearrange("p (h j) -> p h j", j=8)
                        sin = TAB[:, tsl, 64:128].rearrange("p (h j) -> p h j", j=8)
                        x1, x2 = pjv[:, :, 0:8], pjv[:, :, 8:16]
                        tv = [RT[:, r, k, 0:64].rearrange("p (h j) -> p h j", j=8) for k in range(4)]
                        for k, (xa, tb) in enumerate(((x1, cos), (x2, sin), (x2, cos), (x1, sin))):
                            T.op("dve", lambda e, k=k, xa=xa, tb=tb: e.tensor_tensor(out=tv[k], in0=xa, in1=tb, op=ALU.mult),
                                 reads=[("QN", rq), ("TAB", tsl)], writes=[("RT", r, k)])
                        T.op("dve", lambda e: e.tensor_tensor(out=qfv[:, :, 0:8], in0=tv[0], in1=tv[1], op=ALU.subtract),
                             reads=[("RT", r, 0), ("RT", r, 1)], writes=[("QKF", qs, "b")])
                        T.op("dve", lambda e: e.tensor_tensor(out=qfv[:, :, 8:16], in0=tv[2], in1=tv[3], op=ALU.add),
                             reads=[("RT", r, 2), ("RT", r, 3)], writes=[("QKF", qs, "c")])
                        if ng < 2:
                            pend_tq.append((qs, 4, QTS, ng * 4))
                        else:
                            pend_tq.append((qs, 4, KTS, (ng - 2) * 4))

                    def post_B_qk(ps, col0, nh, wrow, dst, c0):
                        qs = qc["qkf"] % 8
                        qc["qkf"] += 1
                        r = qc["rt"] % 3
                        qc["rt"] += 1
                        rq = qc["qn"] % 6
                        qc["qn"] += 1
                        n = nh * 128
                        qnk = [("QN", rq, h) for h in range(nh)]
                        T.op("act", lambda e: e.activation(out=QN[:, rq, 0:n], in_=PJ[:, ps, col0:col0 + n], func=AF.Copy),
                             reads=[("PJ", ps)], writes=[("QN", rq, h) for h in range(4)])
                        T.op("act", lambda e: e.activation(out=SQ[:, rq, 0:n], in_=QN[:, rq, 0:n], func=AF.Square),
                             reads=qnk, writes=[("SQ", rq)])
                        T.op("dve", lambda e: e.tensor_reduce(out=ST2[:, rq, 0:nh], in_=SQ[:, rq, 0:n].rearrange("p (h d) -> p h d", d=128),
                                                              axis=AX.X, op=ALU.add), reads=[("SQ", rq)], writes=[("ST2", rq, 0)])
                        T.op("dve", lambda e: e.tensor_scalar(out=ST2[:, rq, 4:4 + nh], in0=ST2[:, rq, 0:nh], scalar1=1.0 / 128, scalar2=EPS,
                                                              op0=ALU.mult, op1=ALU.add), reads=[("ST2", rq, 0)], writes=[("ST2", rq, 4)])
                        T.op("pool", lambda e: e.tensor_tensor(out=ST2[:, rq, 8:8 + nh], in0=ST2[:, rq, 4:4 + nh],
                                                               in1=NEGH[:, 0:nh], op=ALU.pow),
                             reads=[("ST2", rq, 4), ("NEGH",)], writes=[("ST2", rq, 8)])
                        for h in range(nh):
                            T.op("dve", lambda e, h=h: e.scalar_tensor_tensor(
                                out=QN[:, rq, h * 128:(h + 1) * 128], in0=QN[:, rq, h * 128:(h + 1) * 128],
                                scalar=ST2[:, rq, 8 + h:9 + h], in1=SMW[:, wrow, :], op0=ALU.mult, op1=ALU.mult),
                                reads=[("QN", rq, h), ("ST2", rq, 8), ("SMW", wrow)], writes=[("QN", rq, h)])
                        qnv = QN[:, rq, 0:n].rearrange("p (h a f j) -> p h a f j", a=2, f=2, j=32)
                        qfv = QKF[:, qs, 0:n].rearrange("p (h a f j) -> p h a f j", a=2, f=2, j=32)
                        cos = TAB[:, tsl, 0:256].rearrange("p (h a j) -> p h a j", a=2, j=32)[:, 0:nh]
                        sin = TAB[:, tsl, 256:512].rearrange("p (h a j) -> p h a j", a=2, j=32)[:, 0:nh]
                        x1, x2 = qnv[:, :, :, 0, :], qnv[:, :, :, 1, :]
                        tv = [RT[:, r, k, 0:nh * 64].rearrange("p (h a j) -> p h a j", a=2, j=32) for k in range(4)]
                        for k, (xa, tb) in enumerate(((x1, cos), (x2, sin), (x2, cos), (x1, sin))):
                            T.op("dve", lambda e, k=k, xa=xa, tb=tb: e.tensor_tensor(out=tv[k], in0=xa, in1=tb, op=ALU.mult),
                                 reads=qnk + [("TAB", tsl)], writes=[("RT", r, k)])
                        T.op("dve", lambda e: e.tensor_tensor(out=qfv[:, :, :, 0, :], in0=tv[0], in1=tv[1], op=ALU.subtract),
                             reads=[("RT", r, 0), ("RT", r, 1)], writes=[("QKF", qs, "b")])
                        T.op("dve", lambda e: e.tensor_tensor(out=qfv[:, :, :, 1, :], in0=tv[2], in1=tv[3], op=ALU.add),
                             reads=[("RT", r, 2), ("RT", r, 3)], writes=[("QKF", qs, "c")])
                        pend_tq.append((qs, nh, dst, c0))

                    def post_v(ps, col0, nh, h0):
                        T.op("act", lambda e: e.activation(out=VXS[:, sts, h0:h0 + nh, tt, 0:128],
                                                           in_=PJ[:, ps, col0:col0 + nh * 128].rearrange("p (h e) -> p h e", e=128),
                                                           func=AF.Copy),
                             reads=[("PJ", ps)], writes=[("VXS", sts, tt, h0)])

                    def post_gate(ps, gc):
                        T.op("act", lambda e: e.activation(out=GS[:, t % 3, gc * 512:(gc + 1) * 512], in_=PJ[:, ps, :], func=AF.Silu),
                             reads=[("PJ", ps)], writes=[("GS", t % 3, gc)])

                    for ng in range(NG):
                        ps = mm_group(ng)
                        if typ == "A":
                            if ng < 4:
                                post_A_qk(ng, ps)
                            elif ng < 6:
                                post_v(ps, 0, 4, (ng - 4) * 4)
                            else:
                                post_gate(ps, ng - 6)
                        else:
                            if ng < 2:
                                post_B_qk(ps, 0, 4, 0, QTS, ng * 4)
                            elif ng == 2:
                                post_B_qk(ps, 0, 2, 1, KTS, 0)
                                post_v(ps, 256, 2, 0)
                            else:
                                post_gate(ps, ng - 3)
                    def emit_all_tq():
                        while pend_tq:
                            emit_tq(*pend_tq.pop(0))

                    def stores():
                        T.dma("pool", G_d[:, :, t, :].rearrange("h p e -> p h e"), GS[:, t % 3, :].rearrange("p (h e) -> p h e", e=128),
                              reads=[("GS", t % 3, 0), ("GS", t % 3, 1)], tag="g")
                        if tt == 1:
                            t0 = (t - 1) * 128
                            T.dma("pool", QT_d.rearrange("(c p) s -> p c s", p=128)[:, :, t0:t0 + 256], QTS[:, sts, :, :],
                                  reads=[("QTS", sts, c0, k) for c0 in (0, 4) for k in (0, 1)], tag="q")
                            kreads = [("KTS", sts, c0, k) for c0 in ((0, 4) if typ == "A" else (0,)) for k in (0, 1)]
                            T.dma("pool", KT_d.rearrange("(c p) s -> p c s", p=128)[:, 0:nkc, t0:t0 + 256], KTS[:, sts, 0:nkc, :], reads=kreads, tag="k")
                            vreads = [("VXS", sts, "ones")] + [("VXS", sts, k, h0) for k in (0, 1) for h0 in ((0, 4) if typ == "A" else (0,))]
                            T.dma("pool", VX_d[0:nvh, :, t - 1:t + 1, :].rearrange("h p n e -> p h (n e)"),
                                  VXS[:, sts, 0:nvh, :, :].rearrange("p h n e -> p h (n e)"), reads=vreads, tag="v")

                    return lambda: (emit_all_tq(), stores())

                for t0_ in range(min(3, NT)):
                    load_x(t0_)
                load_tab(0)
                head_stats(0)
                if NT > 1:
                    head_stats(1)
                head_tr(0)
                prev_tail = None
                for t in range(NT):
                    if t + 3 < NT:
                        load_x(t + 3)
                    if t + 1 < NT:
                        load_tab(t + 1)
                    if t + 2 < NT:
                        head_stats(t + 2)
                    tl = body(t)
                    if t + 1 < NT:
                        head_tr(t + 1)
                    if prev_tail is not None:
                        prev_tail()
                    prev_tail = tl
                prev_tail()

        def phase_T(layer, si):
            typ = "A" if layer % 2 == 0 else "B"
            _, _, S = seqs[si]
            QT_d, KT_d, VX_d, G_d = QT_all[si], KT_all[si], VX_all[si], G_all[si]
            NT = S // 128
            NKB = NT
            NQG = S // 512
            dk = 64 if typ == "A" else 128
            scale = float(dk) ** -0.5
            if True:
                KTB = lb("KTB", [128, 2, 2, S_P], BF16)
                VXB = lb("VXB", [128, 2, NTMAX, 129], BF16)
                QTB = lb("QTB", [128, 2, 512], BF16)
                GB = lb("GB", [128, 2, 4, 128], F32)
                PT = lb("PT", [128, 3, 512], BF16)
                O1 = lb("O1", [128, 4, 128], F32)
                DD = lb("DD", [128, 4, 128], F32)
                TMP = lb("TMP", [128, 2, 4, 128], F32)
                RZ = lb("RZ", [128, 2, 24], F32)

                ctr = {"kv": 0, "q": 0, "ev": 0}
                if typ == "A" and first("ktb_zero"):
                    for sl in range(2):
                        T.op("dve", lambda e, sl=sl: e.memset(KTB[64:128, sl, 0, :], 0.0), writes=[("KTB", sl, 0)])
                        T.op("pool", lambda e, sl=sl: e.memset(KTB[0:64, sl, 1, :], 0.0), writes=[("KTB", sl, 1)])

                def acc_ap(aset, qt):
                    b, jj = (2 * aset, qt) if qt < 3 else (2 * aset + 1, 0)
                    return ACC[:, b, jj * 129:(jj + 1) * 129]

                def load_kv(kchunk, vhead):
                    sl = ctr["kv"] % 2
                    ctr["kv"] += 1
                    if typ == "A":
                        T.dma("sp", KTB[0:64, sl, 0, 0:S], KT_d[kchunk * 128:kchunk * 128 + 64, 0:S], writes=[("KTB", sl, 0)])
                        T.dma("sp", KTB[64:128, sl, 1, 0:S], KT_d[kchunk * 128 + 64:(kchunk + 1) * 128, 0:S], writes=[("KTB", sl, 1)])
                    else:
                        T.dma("sp", KTB[:, sl, 0, 0:S], KT_d[kchunk * 128:(kchunk + 1) * 128, 0:S], writes=[("KTB", sl, 0)])
                    T.dma("sp", VXB[:, sl, 0:NT, :], VX_d[vhead, :, 0:NT, :], writes=[("VXB", sl)])
                    return sl

                def load_q(qchunk, qg, h):
                    sl = ctr["q"] % 2
                    ctr["q"] += 1
                    T.dma("sp", QTB[:, sl, :], QT_d[qchunk * 128:(qchunk + 1) * 128, qg * 512:(qg + 1) * 512], writes=[("QTB", sl)])
                    T.dma("sp", GB[:, sl, :, :], G_d[h, :, qg * 4:(qg + 1) * 4, :], writes=[("GB", sl)])
                    return sl

                def attn_unit(kvs, qs, p0):
                    aset = cnt["acc"] % 2
                    cnt["acc"] += 1
                    slots = {}

                    def qk(kb):
                        ps = pj_slot()
                        slots[kb] = ps
                        mp = p0 // 64
                        T.op("pe", lambda e: e.matmul(PJ[:, ps, :], lhsT=KTB[:, kvs, mp, kb * 128:(kb + 1) * 128],
                                                      rhs=QTB[:, qs, :], start=True, stop=True),
                             reads=[("KTB", kvs, mp), ("QTB", qs)], writes=[("PJ", ps)])
                        pts = cnt["pt"] % 3
                        cnt["pt"] += 1
                        T.op("act", lambda e: e.activation(out=PT[:, pts, :], in_=PJ[:, ps, :], func=AF.Exp, scale=scale),
                             reads=[("PJ", ps)], writes=[("PT", pts)])
                        slots[kb] = pts

                    def pv(kb):
                        pts = slots.pop(kb)
                        def f(e):
                            for qt in range(4):
                                ins = e.matmul(acc_ap(aset, qt), lhsT=PT[:, pts, qt * 128:(qt + 1) * 128], rhs=VXB[:, kvs, kb, :],
                                               start=(kb == 0 and qt in (0, 3)), stop=(kb == NKB - 1), skip_group_check=True)
                            return ins
                        T.op("pe", f, reads=[("PT", pts), ("VXB", kvs)], writes=[("ACC", aset)])

                    qk(0)
                    qk(1)
                    for kb in range(NKB):
                        if kb + 2 < NKB:
                            qk(kb + 2)
                        pv(kb)
                    return aset

                def recip_z(aset, r):
                    zv = ACC[:, 2 * aset, 0:387].rearrange("p (a b) -> p a b", b=129)[:, :, 128:129]
                    T.op("dve", lambda e: e.reciprocal(out=RZ[:, r, 0:3].rearrange("p (a b) -> p a b", b=1), in_=zv),
                         reads=[("ACC", aset)], writes=[("RZ", r, 0)])
                    T.op("dve", lambda e: e.reciprocal(out=RZ[:, r, 3:4], in_=ACC[:, 2 * aset + 1, 128:129]),
                         reads=[("ACC", aset)], writes=[("RZ", r, 3)])

                def og_out(h, qg, r, gsl):
                    ogv = OGv[:, qg * 4:(qg + 1) * 4, h * 128:(h + 1) * 128]
                    T.op("pool", lambda e: e.tensor_tensor(out=ogv, in0=TMP[:, r, :, :], in1=GB[:, gsl, :, :], op=ALU.mult),
                         reads=[("TMP", r, q) for q in range(4)] + [("GB", gsl)], writes=[("OG", h, qg)])

                def evac_B(aset, h, qg, gsl):
                    r = ctr["ev"] % 2
                    ctr["ev"] += 1
                    recip_z(aset, r)
                    for qt in range(4):
                        T.op("dve", lambda e, qt=qt: e.tensor_scalar(out=TMP[:, r, qt, :], in0=acc_ap(aset, qt)[:, 0:128],
                                                                      scalar1=RZ[:, r, qt:qt + 1], scalar2=None, op0=ALU.mult),
                             reads=[("ACC", aset), ("RZ", r, 0), ("RZ", r, 3)], writes=[("TMP", r, qt)])
                    og_out(h, qg, r, gsl)

                def evac_A0(aset):
                    r = ctr["ev"] % 2
                    ctr["ev"] += 1
                    recip_z(aset, r)
                    for qt in range(4):
                        T.op("dve", lambda e, qt=qt: e.tensor_scalar(out=O1[:, qt, :], in0=acc_ap(aset, qt)[:, 0:128],
                                                                      scalar1=RZ[:, r, qt:qt + 1], scalar2=None, op0=ALU.mult),
                             reads=[("ACC", aset), ("RZ", r, 0), ("RZ", r, 3)], writes=[("O1", qt)])

                def evac_A1(aset, h, qg, gsl):
                    r = ctr["ev"] % 2
                    ctr["ev"] += 1
                    recip_z(aset, r)
                    T.op("dve", lambda e: e.tensor_scalar(out=RZ[:, r, 4:8], in0=RZ[:, r, 0:4], scalar1=LAMS[:, 4:5], scalar2=None, op0=ALU.mult),
                         reads=[("RZ", r, 0), ("RZ", r, 3), ("LAMS", 4)], writes=[("RZ", r, 4)])
                    for qt in range(4):
                        T.op("dve", lambda e, qt=qt: e.scalar_tensor_tensor(out=DD[:, qt, :], in0=acc_ap(aset, qt)[:, 0:128],
                                                                             scalar=RZ[:, r, 4 + qt:5 + qt], in1=O1[:, qt, :],
                                                                             op0=ALU.mult, op1=ALU.add),
                             reads=[("ACC", aset), ("RZ", r, 4), ("O1", qt)], writes=[("DD", qt)])
                    for qt in range(4):
                        T.op("dve", lambda e, qt=qt: e.scalar_tensor_tensor(out=JKF[:], in0=DD[:, qt, :], scalar=1.0, in1=DD[:, qt, :],
                                                                             op0=ALU.mult, op1=ALU.mult, accum_out=RZ[:, r, 8 + qt:9 + qt]),
                             reads=[("DD", qt)], writes=[("RZ", r, 8 + qt), ("JKF",)])
                    T.op("dve", lambda e: e.tensor_scalar(out=RZ[:, r, 12:16], in0=RZ[:, r, 8:12], scalar1=1.0 / 128, scalar2=EPS,
                                                          op0=ALU.mult, op1=ALU.add),
                         reads=[("RZ", r, 8 + q) for q in range(4)], writes=[("RZ", r, 12)])
                    T.op("pool", lambda e: e.tensor_tensor(out=RZ[:, r, 16:20], in0=RZ[:, r, 12:16], in1=NEGH[:, 0:4],
                                                           op=ALU.pow), reads=[("RZ", r, 12), ("NEGH",)], writes=[("RZ", r, 16)])
                    for qt in range(4):
                        T.op("dve", lambda e, qt=qt: e.scalar_tensor_tensor(out=TMP[:, r, qt, :], in0=DD[:, qt, :],
                                                                             scalar=RZ[:, r, 16 + qt:17 + qt], in1=SMW[:, 0, :],
                                                                             op0=ALU.mult, op1=ALU.mult),
                             reads=[("DD", qt), ("RZ", r, 16), ("SMW", 0)], writes=[("TMP", r, qt)])
                    og_out(h, qg, r, gsl)

                if typ == "A":
                    heads = [(h, h, h) for h in range(8)]
                else:
                    heads = [(h, h // 4, h // 4) for h in range(8)]
                items = [(h, kch, vh, qg) for (h, kch, vh) in heads for qg in range(NQG)]
                kv_list = []
                for (h, kch, vh) in heads:
                    if not kv_list or kv_list[-1] != (kch, vh):
                        kv_list.append((kch, vh))
                kv_idx = 0
                kvs = load_kv(*kv_list[0])
                nxt_kvs = load_kv(*kv_list[1]) if len(kv_list) > 1 else None
                q_next = load_q(items[0][0], items[0][3], items[0][0])
                for i, (h, kch, vh, qg) in enumerate(items):
                    if (kch, vh) != kv_list[kv_idx]:
                        kv_idx += 1
                        kvs = nxt_kvs
                        if kv_idx + 1 < len(kv_list):
                            nxt_kvs = load_kv(*kv_list[kv_idx + 1])
                    cur = q_next
                    if i + 1 < len(items):
                        q_next = load_q(items[i + 1][0], items[i + 1][3], items[i + 1][0])
                    if typ == "A":
                        a0 = attn_unit(kvs, cur, 0)
                        evac_A0(a0)
                        a1 = attn_unit(kvs, cur, 64)
                        evac_A1(a1, h, qg, cur)
                    else:
                        a0 = attn_unit(kvs, cur, 0)
                        evac_B(a0, h, qg, cur)

        def phase_O(layer, si):
            xin, yout, S = seqs[si]
            xsrc = xin if layer == 0 else yout
            NT = S // 128
            if True:
                XT = lb("XTo", [128, 2, 1024], F32)
                OGT = lb("OGT", [128, 2, 8, 128], BF16)
                M = lb("M", [128, 3, 1024], F32)
                ST = lb("STo", [128, 3, 8], F32)
                T.dma("sp", XT[:, 0, :], xsrc[0:128, :], writes=[("XT", 0)])
                def th_copy(t):
                    sl = t % 2
                    def th(e):
                        for wc in range(8):
                            ins = e.transpose(out=TPS[:, wc * 128:(wc + 1) * 128], in_=OGv[:, t, wc * 128:(wc + 1) * 128], identity=IDENT[:])
                        return ins
                    T.op("pe", th, reads=[("IDENT",)] + [("OG", hh, t // 4) for hh in range(8)], writes=[("TPS", 0), ("TPS", 1)])
                    T.op("act", lambda e: e.activation(out=OGT[:, sl, :, :], in_=TPS[:].rearrange("p (k t) -> p k t", k=8), func=AF.Copy),
                         reads=[("TPS", 0), ("TPS", 1)], writes=[("OGT", sl)])

                th_copy(0)
                for t in range(NT):
                    sl = t % 2
                    msl = t % 3
                    if t + 1 < NT:
                        T.dma("sp", XT[:, (t + 1) % 2, :], xsrc[(t + 1) * 128:(t + 2) * 128, :], writes=[("XT", (t + 1) % 2)])
                        th_copy(t + 1)
                    for ng in range(2):
                        ps = pj_slot()
                        def f(e, ng=ng, ps=ps):
                            for wc in range(8):
                                ins = e.matmul(PJ[:, ps, :], lhsT=OGT[:, sl, wc, :], rhs=WOUT[:, wc, ng * 512:(ng + 1) * 512],
                                               start=(wc == 0), stop=(wc == 7))
                            return ins
                        T.op("pe", f, reads=[("OGT", sl)], writes=[("PJ", ps)])
                        T.op("act", lambda e, ng=ng, ps=ps: e.activation(out=M[:, msl, ng * 512:(ng + 1) * 512], in_=PJ[:, ps, :], func=AF.Copy),
                             reads=[("PJ", ps)], writes=[("M", msl, ng)])
                        T.op("act", lambda e, ng=ng: e.activation(out=JK[:, 0:512], in_=M[:, msl, ng * 512:(ng + 1) * 512], func=AF.Square,
                                                                   accum_out=ST[:, msl, ng:ng + 1]),
                             reads=[("M", msl, ng)], writes=[("ST", msl, ng), ("JK",)])
                    T.op("dve", lambda e: e.tensor_tensor(out=ST[:, msl, 2:3], in0=ST[:, msl, 0:1], in1=ST[:, msl, 1:2], op=ALU.add),
                         reads=[("ST", msl, 0), ("ST", msl, 1)], writes=[("ST", msl, 2)])
                    T.op("dve", lambda e: e.tensor_scalar(out=ST[:, msl, 3:4], in0=ST[:, msl, 2:3], scalar1=1.0 / D, scalar2=EPS,
                                                          op0=ALU.mult, op1=ALU.add), reads=[("ST", msl, 2)], writes=[("ST", msl, 3)])
                    T.op("pool", lambda e: e.tensor_tensor(out=ST[:, msl, 4:5], in0=ST[:, msl, 3:4], in1=NEGH[:, 0:1], op=ALU.pow),
                         reads=[("ST", msl, 3), ("NEGH",)], writes=[("ST", msl, 4)])
                    T.op("dve", lambda e: e.scalar_tensor_tensor(out=M[:, msl, :], in0=M[:, msl, :], scalar=ST[:, msl, 4:5], in1=NPOST[:],
                                                                 op0=ALU.mult, op1=ALU.mult),
                         reads=[("M", msl, 0), ("M", msl, 1), ("ST", msl, 4), ("NPOST",)], writes=[("M", msl, 0), ("M", msl, 1)])
                    T.op("pool", lambda e: e.tensor_tensor(out=M[:, msl, :], in0=M[:, msl, :], in1=XT[:, sl, :], op=ALU.add),
                         reads=[("M", msl, 0), ("M", msl, 1), ("XT", sl)], writes=[("M", msl, 0), ("M", msl, 1)])
                    T.dma("pool", yout[t * 128:(t + 1) * 128, :], M[:, msl, :], reads=[("M", msl, 0), ("M", msl, 1)])

        T.barrier()
        for layer in layers:
            layer_setup(layer)
            if "W" in phases:
                load_win(layer)
            if "P" in phases:
                with ExitStack() as sc:
                    scope["es"], scope["cache"] = sc, {}
                    for si in seq_ids:
                        phase_P(layer, si)
                    T.barrier()
            with ExitStack() as sc:
                scope["es"], scope["cache"] = sc, {}
                for si in seq_ids:
                    if "T" in phases:
                        phase_T(layer, si)
                    if "O" in phases:
                        phase_O(layer, si)
                T.barrier()
        T.final_wait("sp")
        T.final_wait("pool")
    return nc


def _tables():
    pos = np.arange(S_P, dtype=np.float32)
    inv_a = (np.float32(500000.0) ** (-(np.arange(0, 16, 2, dtype=np.float32) / np.float32(16)))).astype(np.float32)
    ang = pos[:, None] * inv_a[None, :]
    ca, sa = np.cos(ang).astype(np.float32), np.sin(ang).astype(np.float32)
    tab_a = np.concatenate([np.tile(ca, (1, 8)), np.tile(sa, (1, 8))], axis=1)
    inv_b = (np.float32(10000.0) ** (-(np.arange(0, 64, 2, dtype=np.float32) / np.float32(64)))).astype(np.float32)
    rows = np.floor(pos / 64).astype(np.float32)
    cols = (pos - rows * 64).astype(np.float32)
    ar, ac = rows[:, None] * inv_b[None, :], cols[:, None] * inv_b[None, :]
    cosb = np.concatenate([np.cos(ar), np.cos(ac)], axis=1).astype(np.float32)
    sinb = np.concatenate([np.sin(ar), np.sin(ac)], axis=1).astype(np.float32)
    tab_b = np.concatenate([np.tile(cosb, (1, 4)), np.tile(sinb, (1, 4))], axis=1)
    return np.ascontiguousarray(tab_a, dtype=np.float32), np.ascontiguousarray(tab_b, dtype=np.float32)


_NC_CACHE = {}


def kernel(x_prompt, x_sample, norm_pre, norm_post, a_w_in, a_w_out, a_lam, a_subln,
           b_w_in, b_w_out, b_q_norm, b_k_norm):
    f = lambda a: np.ascontiguousarray(np.asarray(a), dtype=np.float32)
    x_prompt, x_sample = f(x_prompt), f(x_sample)
    shared = {
        "norm_pre": f(norm_pre), "norm_post": f(norm_post), "a_w_in": f(a_w_in), "a_w_out": f(a_w_out),
        "a_lam": f(a_lam), "a_subln": f(a_subln), "b_w_in": f(b_w_in), "b_w_out": f(b_w_out),
        "b_q_norm": f(b_q_norm), "b_k_norm": f(b_k_norm),
    }
    tab_a, tab_b = _tables()
    shared["tab_a"] = tab_a
    shared["tab_b"] = tab_b
    shared["ident"] = np.eye(128, dtype=np.float32).astype(ml_dtypes.bfloat16)
    if "nc" not in _NC_CACHE:
        _NC_CACHE["nc"] = build_nc()
    nc = _NC_CACHE["nc"]
    in_maps = []
    for c in range(NCORES):
        m = dict(shared)
        m["xp"] = x_prompt[2 * c:2 * c + 2]
        m["xs"] = x_sample[c:c + 1]
        in_maps.append(m)
    res = run_bass_kernel_spmd(nc, in_maps, core_ids=list(range(NCORES)))
    y_prompt = np.concatenate([np.asarray(r["yp"], dtype=np.float32) for r in res.results], axis=0)
    y_sample = np.concatenate([np.asarray(r["ys"], dtype=np.float32) for r in res.results], axis=0)
    return (y_prompt, y_sample)
```
